# Optimizing a Trainium2 kernel written in Bass

```python
import math
import jax, jax.numpy as jnp
from jax import lax
import numpy as np

D_MODEL = 2048
BATCH = 4
SEQ = 2048
DEPTH = 1
DEC_BATCH = 128
DEC_SEQ = 8
PAST_LEN = 16384
PAGE_SIZE = 128

N_META = 16
D_FF = 5632
S5_WIDTH = D_MODEL // 2
S5_GROUP = 16
S5_GROUPS = S5_WIDTH // S5_GROUP
S5_STATE = 64
MLSTM_HEADS = 4
MLSTM_DQK = D_MODEL // 16
MLSTM_DV = D_MODEL // 8
QK_WIDTH = 2 * MLSTM_HEADS * MLSTM_DQK
V_WIDTH = MLSTM_HEADS * MLSTM_DV
CONV_W = 4
CHUNK = 64
N_BRANCH = 2
IN_WIDTH = S5_WIDTH + QK_WIDTH + 2 * V_WIDTH + 2 * MLSTM_HEADS + N_BRANCH * D_MODEL
EPS = 1e-6
DT_MIN = 1e-3
DT_MAX = 1e-1

kernel_name = 'hybrid_s5_mlstm_macaron_decode_step'


def rmsnorm(x, g):
    xf = x.astype(jnp.float32)
    y = xf * lax.rsqrt(jnp.mean(xf * xf, axis=-1, keepdims=True) + EPS)
    return (y * g.astype(jnp.float32)).astype(x.dtype)


def swiglu(x, wg, wu, wd):
    return (jax.nn.silu(x @ wg) * (x @ wu)) @ wd


def _lin_combine(e1, e2):
    a1, b1 = e1
    a2, b2 = e2
    return a1 * a2, a2 * b1 + b2


def s5_mixer(u, h0_re, h0_im, A_re, A_im, log_dt, B_re, B_im, C_re, C_im, d_skip, w_glu):
    n, L, _ = u.shape
    f32 = jnp.float32
    A = lax.complex(A_re.astype(f32), A_im.astype(f32))
    dt = jnp.exp(log_dt.astype(f32))[:, None]
    Abar = jnp.exp(A * dt)
    Bbar = ((Abar - 1.0) / A)[:, :, None] * lax.complex(B_re.astype(f32), B_im.astype(f32))
    ug = u.astype(f32).reshape(n, L, S5_GROUPS, S5_GROUP)
    Bu = jnp.einsum('nlgh,gph->nlgp', ug.astype(jnp.complex64), Bbar)
    h0 = lax.complex(h0_re.astype(f32), h0_im.astype(f32))[:, None]
    b = jnp.concatenate([h0, Bu], axis=1)
    a = jnp.broadcast_to(Abar, b.shape)
    _, hs = lax.associative_scan(_lin_combine, (a, b), axis=1)
    hs = hs[:, 1:]
    Cc = lax.complex(C_re.astype(f32), C_im.astype(f32))
    y = jnp.real(jnp.einsum('nlgp,ghp->nlgh', hs, Cc)) + d_skip.astype(f32).reshape(S5_GROUPS, S5_GROUP) * ug
    y = jax.nn.gelu(y.reshape(n, L, S5_WIDTH)).astype(u.dtype)
    y = y * jax.nn.sigmoid(y @ w_glu)
    h_last = hs[:, -1]
    return y, jnp.real(h_last).astype(h0_re.dtype), jnp.imag(h_last).astype(h0_re.dtype)


def causal_conv(x, buf, w, bias):
    L = x.shape[1]
    xp = jnp.concatenate([buf.astype(x.dtype), x], axis=1)
    out = sum(xp[:, j:j + L] * w[j] for j in range(CONV_W)) + bias
    return jax.nn.silu(out), xp[:, -(CONV_W - 1):]


def mlstm_chunk(carry, xs):
    C0, n0, m0 = carry
    q, k, v, ig, lf = xs
    L = q.shape[1]
    b = jnp.cumsum(lf, axis=1).transpose(0, 2, 1)
    i_ = ig.transpose(0, 2, 1)
    causal = jnp.tril(jnp.ones((L, L), dtype=bool))
    logw = jnp.where(causal, b[..., :, None] - b[..., None, :] + i_[..., None, :], -jnp.inf)
    g = b + m0[..., None]
    m = jnp.maximum(g, jnp.max(logw, axis=-1))
    w = jnp.exp(logw - m[..., None])
    inter = jnp.exp(g - m)
    s = jnp.einsum('nlhk,nshk->nhls', q, k) * w
    num = jnp.einsum('nhls,nshv->nlhv', s, v) + inter.transpose(0, 2, 1)[..., None] * jnp.einsum('nhvk,nlhk->nlhv', C0, q)
    den = jnp.sum(s, axis=-1) + inter * jnp.einsum('nhk,nlhk->nhl', n0, q)
    denom = jnp.maximum(jnp.abs(den), jnp.exp(-m))
    h = num / denom.transpose(0, 2, 1)[..., None]
    bL = b[..., -1]
    m_new = m[..., -1]
    decay = jnp.exp(bL + m0 - m_new)
    w_end = jnp.exp(bL[..., None] - b + i_ - m_new[..., None])
    C = decay[..., None, None] * C0 + jnp.einsum('nhs,nshv,nshk->nhvk', w_end, v, k)
    n = decay[..., None] * n0 + jnp.einsum('nhs,nshk->nhk', w_end, k)
    return (C, n, m_new), h


def mlstm_run(q, k, v, ig, lf, carry, lead):
    N, L = q.shape[0], q.shape[1]
    xs_all = (q, k, v, ig, lf)
    outs = []
    if lead > 0:
        carry, hl = mlstm_chunk(carry, tuple(t[:, :lead] for t in xs_all))
        outs.append(hl)
    nf, rem = divmod(L - lead, CHUNK)
    if nf > 0:
        def to_chunks(t):
            t = t[:, lead:lead + nf * CHUNK]
            return jnp.moveaxis(t.reshape((N, nf, CHUNK) + t.shape[2:]), 1, 0)
        carry, hc = lax.scan(mlstm_chunk, carry, tuple(to_chunks(t) for t in xs_all))
        hc = jnp.moveaxis(hc, 0, 1)
        outs.append(hc.reshape((N, nf * CHUNK) + hc.shape[3:]))
    if rem > 0:
        carry, ht = mlstm_chunk(carry, tuple(t[:, L - rem:] for t in xs_all))
        outs.append(ht)
    return jnp.concatenate(outs, axis=1), carry


def mlstm_mixer(qk_pre, v_pre, o_pre, i_pre, f_pre, conv_buf, C0, n0, m0, conv_w, conv_b, b_i, b_f, norm_g, lead):
    N, L, _ = qk_pre.shape
    f32 = jnp.float32
    qk, new_buf = causal_conv(qk_pre, conv_buf, conv_w, conv_b)
    q = qk[..., :QK_WIDTH // 2].reshape(N, L, MLSTM_HEADS, MLSTM_DQK).astype(f32) * (MLSTM_DQK ** -0.5)
    k = qk[..., QK_WIDTH // 2:].reshape(N, L, MLSTM_HEADS, MLSTM_DQK).astype(f32)
    v = v_pre.reshape(N, L, MLSTM_HEADS, MLSTM_DV).astype(f32)
    ig = (i_pre + b_i).astype(f32)
    lf = jax.nn.log_sigmoid((f_pre + b_f).astype(f32))
    carry = (C0.astype(f32), n0.astype(f32), m0.astype(f32))
    h, (C, n, m) = mlstm_run(q, k, v, ig, lf, carry, lead)
    h = h * lax.rsqrt(jnp.mean(h * h, axis=-1, keepdims=True) + EPS)
    h = h.reshape(N, L, V_WIDTH) * norm_g.astype(f32)
    h = (h * jax.nn.sigmoid(o_pre.astype(f32))).astype(qk_pre.dtype)
    dt = C0.dtype
    return h, C.astype(dt), n.astype(dt), m.astype(dt), new_buf.astype(conv_buf.dtype)


def layer(h, st, p, lead):
    s5_re, s5_im, C0, n0, m0, buf = st
    h = h + 0.5 * swiglu(rmsnorm(h, p['ffn1_norm']), p['ffn1_w_gate'], p['ffn1_w_up'], p['ffn1_w_down'])
    u = rmsnorm(h, p['mix_norm'])
    z = u @ p['w_in']
    o0 = S5_WIDTH
    o1 = o0 + QK_WIDTH
    o2 = o1 + V_WIDTH
    o3 = o2 + V_WIDTH
    o4 = o3 + MLSTM_HEADS
    o5 = o4 + MLSTM_HEADS
    y_s5, ns_re, ns_im = s5_mixer(z[..., :o0], s5_re, s5_im, p['s5_A_re'], p['s5_A_im'], p['s5_log_dt'],
                                  p['s5_B_re'], p['s5_B_im'], p['s5_C_re'], p['s5_C_im'], p['s5_D'], p['s5_w_glu'])
    y_ml, nC, nn_, nm, nbuf = mlstm_mixer(z[..., o0:o1], z[..., o1:o2], z[..., o2:o3], z[..., o3:o4], z[..., o4:o5],
                                          buf, C0, n0, m0, p['mlstm_conv_w'], p['mlstm_conv_b'],
                                          p['mlstm_b_i'], p['mlstm_b_f'], p['mlstm_norm'], lead)
    gates = jax.nn.sigmoid(z[..., o5:])
    merged = gates[..., :D_MODEL] * (y_s5 @ p['w_branch_s5']) + gates[..., D_MODEL:] * (y_ml @ p['w_branch_mlstm'])
    h = h + merged @ p['w_out']
    h = h + 0.5 * swiglu(rmsnorm(h, p['ffn2_norm']), p['ffn2_w_gate'], p['ffn2_w_up'], p['ffn2_w_down'])
    return h, (ns_re, ns_im, nC, nn_, nm, nbuf)


def setup_inputs(seed: int = 0) -> dict:
    key = jax.random.key(seed)
    ks = iter(jax.random.split(key, 64))

    def nrm(shape, scale=1.0):
        return jax.random.normal(next(ks), shape, jnp.float32) * scale

    def gain(shape):
        return 1.0 + nrm(shape, 0.02)

    Ld = DEPTH
    G, P, Hc = S5_GROUPS, S5_STATE, S5_GROUP
    out = {}
    out['x_prompt'] = nrm((BATCH, SEQ, D_MODEL))
    out['x_sample'] = nrm((DEC_BATCH, DEC_SEQ, D_MODEL))
    out['state_s5_re'] = nrm((Ld, DEC_BATCH, G, P), 0.1)
    out['state_s5_im'] = nrm((Ld, DEC_BATCH, G, P), 0.1)
    out['state_mlstm_C'] = nrm((Ld, DEC_BATCH, MLSTM_HEADS, MLSTM_DV, MLSTM_DQK), 0.1)
    out['state_mlstm_n'] = jnp.abs(nrm((Ld, DEC_BATCH, MLSTM_HEADS, MLSTM_DQK), 0.1))
    out['state_mlstm_m'] = nrm((Ld, DEC_BATCH, MLSTM_HEADS), 0.5)
    out['state_mlstm_conv'] = nrm((Ld, DEC_BATCH, CONV_W - 1, QK_WIDTH))
    out['meta_tokens'] = nrm((N_META, D_MODEL))
    out['ffn1_norm'] = gain((Ld, D_MODEL))
    out['ffn1_w_gate'] = nrm((Ld, D_MODEL, D_FF), D_MODEL ** -0.5)
    out['ffn1_w_up'] = nrm((Ld, D_MODEL, D_FF), D_MODEL ** -0.5)
    out['ffn1_w_down'] = nrm((Ld, D_FF, D_MODEL), D_FF ** -0.5)
    out['mix_norm'] = gain((Ld, D_MODEL))
    out['w_in'] = nrm((Ld, D_MODEL, IN_WIDTH), D_MODEL ** -0.5)
    out['s5_A_re'] = -0.5 + nrm((Ld, G, P), 0.01)
    out['s5_A_im'] = jnp.pi * jnp.arange(P, dtype=jnp.float32) + nrm((Ld, G, P), 0.01)
    out['s5_log_dt'] = jax.random.uniform(next(ks), (Ld, G), jnp.float32, math.log(DT_MIN), math.log(DT_MAX))
    out['s5_B_re'] = nrm((Ld, G, P, Hc), (2 * Hc) ** -0.5)
    out['s5_B_im'] = nrm((Ld, G, P, Hc), (2 * Hc) ** -0.5)
    out['s5_C_re'] = nrm((Ld, G, Hc, P), P ** -0.5)
    out['s5_C_im'] = nrm((Ld, G, Hc, P), P ** -0.5)
    out['s5_D'] = nrm((Ld, S5_WIDTH))
    out['s5_w_glu'] = nrm((Ld, S5_WIDTH, S5_WIDTH), S5_WIDTH ** -0.5)
    out['mlstm_conv_w'] = nrm((Ld, CONV_W, QK_WIDTH), CONV_W ** -0.5)
    out['mlstm_conv_b'] = nrm((Ld, QK_WIDTH), 0.01)
    out['mlstm_b_i'] = nrm((Ld, MLSTM_HEADS), 0.1)
    out['mlstm_b_f'] = jnp.linspace(3.0, 6.0, MLSTM_HEADS, dtype=jnp.float32) + nrm((Ld, MLSTM_HEADS), 0.1)
    out['mlstm_norm'] = gain((Ld, V_WIDTH))
    out['w_branch_s5'] = nrm((Ld, S5_WIDTH, D_MODEL), S5_WIDTH ** -0.5)
    out['w_branch_mlstm'] = nrm((Ld, V_WIDTH, D_MODEL), V_WIDTH ** -0.5)
    out['w_out'] = nrm((Ld, D_MODEL, D_MODEL), D_MODEL ** -0.5)
    out['ffn2_norm'] = gain((Ld, D_MODEL))
    out['ffn2_w_gate'] = nrm((Ld, D_MODEL, D_FF), D_MODEL ** -0.5)
    out['ffn2_w_up'] = nrm((Ld, D_MODEL, D_FF), D_MODEL ** -0.5)
    out['ffn2_w_down'] = nrm((Ld, D_FF, D_MODEL), D_FF ** -0.5)
    out['final_norm'] = gain((D_MODEL,))
    return out


def reference(x_prompt, x_sample, state_s5_re, state_s5_im, state_mlstm_C, state_mlstm_n, state_mlstm_m,
              state_mlstm_conv, meta_tokens, ffn1_norm, ffn1_w_gate, ffn1_w_up, ffn1_w_down, mix_norm, w_in,
              s5_A_re, s5_A_im, s5_log_dt, s5_B_re, s5_B_im, s5_C_re, s5_C_im, s5_D, s5_w_glu,
              mlstm_conv_w, mlstm_conv_b, mlstm_b_i, mlstm_b_f, mlstm_norm, w_branch_s5, w_branch_mlstm,
              w_out, ffn2_norm, ffn2_w_gate, ffn2_w_up, ffn2_w_down, final_norm):
    nb = x_prompt.shape[0]
    meta = jnp.broadcast_to(meta_tokens.astype(x_prompt.dtype)[None], (nb, N_META, D_MODEL))
    hp = jnp.concatenate([meta, x_prompt], axis=1)
    hs = x_sample
    sdt = state_mlstm_C.dtype
    zero_state = (jnp.zeros((nb, S5_GROUPS, S5_STATE), state_s5_re.dtype),
                  jnp.zeros((nb, S5_GROUPS, S5_STATE), state_s5_im.dtype),
                  jnp.zeros((nb, MLSTM_HEADS, MLSTM_DV, MLSTM_DQK), sdt),
                  jnp.zeros((nb, MLSTM_HEADS, MLSTM_DQK), sdt),
                  jnp.zeros((nb, MLSTM_HEADS), sdt),
                  jnp.zeros((nb, CONV_W - 1, QK_WIDTH), state_mlstm_conv.dtype))
    p_states = []
    s_states = []
    for l in range(DEPTH):
        p = {'ffn1_norm': ffn1_norm[l], 'ffn1_w_gate': ffn1_w_gate[l], 'ffn1_w_up': ffn1_w_up[l],
             'ffn1_w_down': ffn1_w_down[l], 'mix_norm': mix_norm[l], 'w_in': w_in[l],
             's5_A_re': s5_A_re[l], 's5_A_im': s5_A_im[l], 's5_log_dt': s5_log_dt[l],
             's5_B_re': s5_B_re[l], 's5_B_im': s5_B_im[l], 's5_C_re': s5_C_re[l], 's5_C_im': s5_C_im[l],
             's5_D': s5_D[l], 's5_w_glu': s5_w_glu[l], 'mlstm_conv_w': mlstm_conv_w[l],
             'mlstm_conv_b': mlstm_conv_b[l], 'mlstm_b_i': mlstm_b_i[l], 'mlstm_b_f': mlstm_b_f[l],
             'mlstm_norm': mlstm_norm[l], 'w_branch_s5': w_branch_s5[l], 'w_branch_mlstm': w_branch_mlstm[l],
             'w_out': w_out[l], 'ffn2_norm': ffn2_norm[l], 'ffn2_w_gate': ffn2_w_gate[l],
             'ffn2_w_up': ffn2_w_up[l], 'ffn2_w_down': ffn2_w_down[l]}
        hp, stp = layer(hp, zero_state, p, N_META)
        st_in = (state_s5_re[l], state_s5_im[l], state_mlstm_C[l], state_mlstm_n[l], state_mlstm_m[l],
                 state_mlstm_conv[l])
        hs, sts = layer(hs, st_in, p, 0)
        p_states.append(stp)
        s_states.append(sts)
    y_prompt = rmsnorm(hp, final_norm)[:, N_META:]
    y_sample = rmsnorm(hs, final_norm)
    p_s5_re = jnp.stack([s[0] for s in p_states])
    p_s5_im = jnp.stack([s[1] for s in p_states])
    p_C = jnp.stack([s[2] for s in p_states])
    p_n = jnp.stack([s[3] for s in p_states])
    p_m = jnp.stack([s[4] for s in p_states])
    p_conv = jnp.stack([s[5] for s in p_states])
    s_s5_re = jnp.stack([s[0] for s in s_states])
    s_s5_im = jnp.stack([s[1] for s in s_states])
    s_C = jnp.stack([s[2] for s in s_states])
    s_n = jnp.stack([s[3] for s in s_states])
    s_m = jnp.stack([s[4] for s in s_states])
    s_conv = jnp.stack([s[5] for s in s_states])
    return (y_prompt, y_sample, p_s5_re, p_s5_im, p_C, p_n, p_m, p_conv,
            s_s5_re, s_s5_im, s_C, s_n, s_m, s_conv)
```

```python
import math
import numpy as np
import concourse.bass as bass
import concourse.mybir as mybir
from concourse.bass_utils import run_bass_kernel_spmd
from contextlib import ExitStack

F32 = mybir.dt.float32
F32R = mybir.dt.float32r
BF16 = mybir.dt.bfloat16
MMT = BF16
AF = mybir.ActivationFunctionType
ALU = mybir.AluOpType

D = 2048
DFF = 5632
NKD = D // 128
S5W = 1024
QKW = 1024
VW = 1024
NH = 4
DQK = 128
DV = 256
INW = 8200
O_S5, O_QK, O_V, O_O, O_I, O_F, O_G1, O_G2 = 0, 1024, 2048, 3072, 4096, 4100, 4104, 6152
EPS = 1e-6
NMETA = 16
SEQ = 2048
PLEN = NMETA + SEQ
NSEQ = 16
LS = 8
SC = NSEQ * LS
NTM = 392
LT = 64
NEGBIG = -1.0e30
TWO_PI = 2.0 * math.pi
MAGIC = 12582912.0
STRICT_SAME_ENGINE = True


def make_plan(pl):
    tiles = []
    p0 = 0
    rem = pl
    while rem > 0:
        pc = min(NTM, rem)
        tiles.append([pc, p0, 0])
        p0 += pc
        rem -= pc
    if tiles and tiles[-1][0] + SC <= NTM:
        tiles[-1][2] = SC
    else:
        tiles.append([0, p0, SC])
    return [tuple(t) for t in tiles]


def _layout(items):
    off = {}
    o = 0
    for n, w in items:
        off[n] = (o, w)
        o += w
    return off, o


CF_ITEMS = [
    ("ident", 128), ("maskP", 128), ("maskS", 128),
    ("g1", 16), ("gm", 16), ("g2", 16), ("gf", 16), ("s5D", 8), ("convw", 40), ("mlg", 8),
    ("sel", 512), ("bi", 1), ("bf", 1), ("flag", 1),
    ("rmP", 128), ("rmS", 128), ("raP", 128), ("raS", 128), ("m0S", 16),
    ("tau1", LT), ("m01", LT), ("Are", 32), ("Aim", 32), ("ldt", 32),
    ("h0r", 512), ("h0i", 512), ("n0S", 64), ("histS", 384),
]
CF_OFF, NCF = _layout(CF_ITEMS)
CR_ITEMS = [("ones", 128), ("seqm", 16), ("identb", 128), ("Cr", 1024), ("Ci", 1024)]
CR_OFF, NCR = _layout(CR_ITEMS)
SC_ITEMS = [("AreB", 1024), ("AimB", 1024), ("ldtB", 1024), ("BreB", 1024), ("BimB", 1024)]
SC_OFF, NSC = _layout(SC_ITEMS)


class Res:
    __slots__ = ("w", "r")

    def __init__(self):
        self.w = None
        self.r = []


def RL(n):
    return [Res() for _ in range(n)]


class KB:
    def __init__(self, nc, es):
        self.nc = nc
        self.es = es
        self.E = {"pe": nc.tensor, "act": nc.scalar, "dve": nc.vector, "pool": nc.gpsimd, "sp": nc.sync}
        self.semobj = {}
        self.cnt = {}
        for k in self.E:
            self.semobj[k] = es.enter_context(nc.semaphore("sem_" + k))
            self.cnt[k] = 0
        self.seen = {k: {} for k in self.E}
        self.ndma = 0
        self.misc = {"sp": [], "pool": []}
        self.misc_i = {"sp": 0, "pool": 0}
        for q_, n_ in (("sp", 8), ("pool", 4)):
            for i in range(n_):
                key = "m%s%d" % (q_, i)
                self.semobj[key] = es.enter_context(nc.semaphore("sem_" + key))
                self.cnt[key] = 0
                self.misc[q_].append(key)

    def new_dma_sem(self, key):
        self.semobj[key] = self.es.enter_context(self.nc.semaphore("sem_" + key))
        self.cnt[key] = 0

    def _wait(self, eng, ev):
        if ev is None:
            return
        key, val = ev
        if self.seen[eng].get(key, 0) >= val:
            return
        self.E[eng].wait_ge(self.semobj[key], val)
        self.seen[eng][key] = val

    def _deps(self, eng, rd, wr):
        for r in rd:
            self._wait(eng, r.w)
        strict = STRICT_SAME_ENGINE and eng != "pe"
        for w in wr:
            if w.w is not None and (strict or w.w[0] != eng):
                self._wait(eng, w.w)
            for ev in w.r:
                if strict or ev[0] != eng:
                    self._wait(eng, ev)

    def op(self, eng, fn, rd=(), wr=()):
        self._deps(eng, rd, wr)
        ins = fn(self.E[eng])
        self.cnt[eng] += 1
        ins.then_inc(self.semobj[eng], 1)
        ev = (eng, self.cnt[eng])
        for r in rd:
            r.r.append(ev)
        for w in wr:
            w.w = ev
            w.r = []
        return ev

    def dma(self, eng, out, in_, rd=(), wr=(), semkey=None, **kw):
        if semkey is None:
            semkey = self.misc[eng][self.misc_i[eng] % len(self.misc[eng])]
            self.misc_i[eng] += 1
        if self.cnt[semkey] > 0:
            self._wait(eng, (semkey, self.cnt[semkey]))
        self._deps(eng, rd, wr)
        ins = self.E[eng].dma_start(out=out, in_=in_, **kw)
        self.cnt[semkey] += 16
        ins.then_inc(self.semobj[semkey], 16)
        ev = (semkey, self.cnt[semkey])
        for r in rd:
            r.r.append(ev)
        for w in wr:
            w.w = ev
            w.r = []
        self.ndma += 1
        return ev

    def barrier(self):
        evs = [(k, self.cnt[k]) for k in self.semobj if self.cnt[k] > 0]
        for eng in ("pe", "act", "dve", "pool", "sp"):
            for ev in evs:
                self._wait(eng, ev)

    def mm(self, out, lhsT, rhs, start, stop, rd, wr):
        return self.op("pe", lambda e: e.matmul(out, lhsT, rhs, start=start, stop=stop), rd, wr)

    def tr(self, out, in_, ident, rd, wr):
        return self.op("pe", lambda e: e.transpose(out, in_, ident), rd, wr)

    def act(self, out, in_, func, rd, wr, **kw):
        return self.op("act", lambda e: e.activation(out, in_, func, **kw), rd, wr)

    def tt(self, out, in0, in1, op, rd, wr, eng="dve"):
        return self.op(eng, lambda e: e.tensor_tensor(out, in0, in1, op), rd, wr)

    def ts(self, out, in0, s1, s2, op0, op1, rd, wr, eng="dve"):
        if op1 is None:
            return self.op(eng, lambda e: e.tensor_scalar(out, in0, s1, None, op0), rd, wr)
        return self.op(eng, lambda e: e.tensor_scalar(out, in0, s1, s2, op0, op1), rd, wr)

    def stt(self, out, in0, sc, in1, op0, op1, rd, wr):
        return self.op("dve", lambda e: e.scalar_tensor_tensor(out, in0, sc, in1, op0, op1), rd, wr)

    def cp(self, out, in_, rd, wr, eng="dve"):
        return self.op(eng, lambda e: e.tensor_copy(out, in_), rd, wr)

    def scan(self, out, d0, d1, init, op0, op1, rd, wr):
        return self.op("dve", lambda e: e.tensor_tensor_scan(out, d0, d1, init, op0, op1), rd, wr)

    def recip(self, out, in_, rd, wr):
        return self.op("dve", lambda e: e.reciprocal(out, in_), rd, wr)

    def memset(self, ap, val, wr, eng="dve"):
        return self.op(eng, lambda e: e.memset(ap, val), (), wr)


def fr(ap):
    return ap


def make_plan_split(pl, split, tw):
    tiles = []
    p0 = 0
    while p0 < split:
        pc = min(tw, split - p0)
        tiles.append((pc, p0, 0, "pre"))
        p0 += pc
    while p0 < pl:
        pc = min(tw, pl - p0)
        tiles.append([pc, p0, 0, "main"])
        p0 += pc
    if tiles[-1][0] + SC <= NTM:
        tiles[-1][2] = SC
    else:
        tiles.append([0, pl, SC, "main"])
    return [tuple(t) for t in tiles]


def build_program(pl=PLEN, debug=False, stop=None, split=None, tw=NTM):
    lvl = {'ffn1': 0, 's5a': 1, 's5b': 2, 's5c': 3, 's5d': 4, 's5': 5, 'mlstm': 6}.get(stop, 9)
    if split is None:
        plan = [t + ("main",) for t in make_plan(pl)]
        npre = 0
    else:
        plan = make_plan_split(pl, split, tw)
        npre = split
        debug = False
    ncols = pl + SC
    nycols = pl - npre + SC
    nc = bass.Bass("TRN2", target_bir_lowering=False)
    dram = {}

    def din(name, shape):
        dram[name] = nc.dram_tensor(name, list(shape), F32, kind="ExternalInput").ap()
        return dram[name]

    def dout(name, shape):
        dram[name] = nc.dram_tensor(name, list(shape), F32, kind="ExternalOutput").ap()
        return dram[name]

    xT = din("xT", [D, ncols])
    W = {}
    for n, s in [("f1g", [D, DFF]), ("f1u", [D, DFF]), ("f1d", [DFF, D]), ("win", [D, INW]),
                 ("wglu", [S5W, S5W]), ("wbs", [S5W, D]), ("wbm", [VW, D]), ("wout", [D, D]),
                 ("f2g", [D, DFF]), ("f2u", [D, DFF]), ("f2d", [DFF, D])]:
        W[n] = din(n, s)
    cf_d = din("cf", [128, NCF])
    cr_d = din("cr", [128, NCR])
    sc_d = din("sc", [128, NSC])
    sC_d = din("sC", [NSEQ, NH, DV, DQK])
    yT = dout("yT", [D, nycols])
    o_ps5 = dout("o_ps5", [128, 64])
    o_pC = dout("o_pC", [128, NH * 2 * 128])
    o_pn = dout("o_pn", [128, NH])
    o_pm = dout("o_pm", [4, 1])
    o_pconv = dout("o_pconv", [128, 24])
    o_ss5 = dout("o_ss5", [128, 1024])
    o_sC = dout("o_sC", [NSEQ, NH, DV, DQK])
    o_sn = dout("o_sn", [128, 64])
    o_sm = dout("o_sm", [4, 16])
    o_sconv = dout("o_sconv", [128, 384])
    dbg = {}
    if debug:
        for n in ("d_h1", "d_ys5", "d_yml", "d_h2"):
            dbg[n] = dout(n, [D if n in ("d_h1", "d_h2") else 1024, ncols])

    es = ExitStack()
    with es:
        k = KB(nc, es)

        def sb(name, shape, dt=F32):
            return es.enter_context(nc.sbuf_tensor("s_" + name, list(shape), dt))

        cf = sb("cf", [128, NCF]); cf_r = Res()
        crt = sb("crt", [128, NCR], MMT); cr_r = Res()
        h = sb("h", [128, NKD, NTM]); h_r = RL(NKD)
        xn = sb("xn", [128, NKD, NTM], MMT); xn_r = RL(NKD)
        rstd = sb("rstd", [128, NTM]); rstd_r = Res()
        tA = sb("tA", [128, NTM]); tA_r = Res()
        tB = sb("tB", [128, NTM]); tB_r = Res()
        tC = sb("tC", [128, NTM]); tC_r = Res()
        tD = sb("tD", [128, NTM]); tD_r = Res()
        regX = sb("regX", [128, 8 * NTM], MMT)
        regF = sb("regF", [128, 12 * NTM + 8])
        mid = regX[:].rearrange("p (a b n) -> p a b n", a=2, b=4); mid_r = [RL(4), RL(4)]
        ys5 = sb("ys5", [128, 8, NTM], MMT); ys5_r = RL(8)
        yml = sb("yml", [128, 8, NTM], MMT); yml_r = RL(8)
        sq = yml[:, 0:2, :]; sq_r = RL(2)
        mrg = sb("mrg", [128, 8, NTM], MMT); mrg_r = RL(8)
        NRING = 6
        ring = [sb("ring%d" % i, [128, 2048], MMT) for i in range(NRING)]
        ring_r = RL(NRING)
        for i in range(NRING):
            k.new_dma_sem("rg%d" % i)
        ring_i = [0]
        psum = [es.enter_context(nc.psum_tensor("ps%d" % i, [128, 512], F32)) for i in range(8)]
        psum_r = RL(8)
        ps_i = [0]
        ct = sb("ct", [128, 32, LT]); st = sb("st", [128, 32, LT]); trig_r = Res()
        am16 = sb("am16", [128, 32, LT]); am8 = sb("am8", [128, 32, LS])
        lam = sb("lam", [128, 32]); ilam = sb("ilam", [128, 32])
        Rpr = sb("Rpr", [128, 32]); Rpi = sb("Rpi", [128, 32])
        Rp8r = sb("Rp8r", [128, 32]); Rp8i = sb("Rp8i", [128, 32])
        Btab = sb("Btab", [128, 8, 2, 128], MMT); btab_r = Res()
        injr = sb("injr", [128, 32]); inji = sb("inji", [128, 32]); inj_r = RL(8)
        u8 = yml[:, 4:6, :]; u8_r = RL(2)
        twr = regF[:, 0:4 * NTM]; twi = regF[:, 4 * NTM:8 * NTM]; tw_r = Res()
        xs = sb("xs", [128, 4, NTM], MMT); xs_r = RL(4)
        u3 = yml[:, 6:8, :]
        Bext = sb("Bext", [128, 3, 2, 128], MMT)
        ysp = mrg; ysp_r = mrg_r
        sS_r = Res()
        bt = sb("bt", [128, 8, 4]); bt_r = Res()
        Cp = sb("Cp", [128, NH, 2, 128]); Cp_r = RL(NH)
        np_ = sb("np", [128, NH]); np_r = Res()
        m0p = sb("m0p", [4, 1]); m0p_r = Res()
        hist = sb("hist", [128, 8, 3]); hist_r = RL(8)
        cbuf = regF[:, 8 * NTM:8 * NTM + 2 * (NTM + 3)].rearrange("p (a n) -> p a n", a=2); cbuf_r = RL(2)
        cbs = sb("cbs", [128, 2, NSEQ, LS + 3]); cbs_r = RL(2)
        cacc = regF[:, 8 * NTM + 2 * (NTM + 3):8 * NTM + 2 * (NTM + 3) + 2 * NTM].rearrange("p (a n) -> p a n", a=2); cacc_r = RL(2)
        qT = sb("qT", [128, NTM], MMT); qT_r = Res()
        kT = sb("kT", [128, NTM], MMT); kT_r = Res()
        vtok = sb("vtok", [128, 3, DV], MMT); vtok_r = RL(3)
        sigo = sb("sigo", [128, 2, NTM]); sigo_r = RL(2)
        rows = regF[0:4, 0:8 * NTM].rearrange("p (r n) -> p r n", r=8); rows_r = Res()
        RB = sb("RB", [4, 3, 128]); RB_r = Res()
        aT = sb("aT", [128, 4]); wT = sb("wT", [128, 4]); awT_r = Res()
        Et = sb("Et", [128, 128]); Et_r = Res()
        Wt = sb("Wt", [128, 128]); Wt_r = Res()
        SW = sb("SW", [128, 128], MMT); SW_r = Res()
        qs = sb("qs", [128, 128], MMT); qs_r = Res()
        nq = sb("nq", [128, 128], MMT); nq_r = Res()
        CT = sb("CT", [128, 2, DV], MMT); CT_r = RL(2)
        Cs = sb("Cs", [128, NSEQ, 2, 128]); Cs_r = RL(NSEQ)
        kw = sb("kw", [128, 128], MMT); kw_r = Res()
        kwm = sb("kwm", [128, 2, 128], MMT); kwm_r = RL(2)
        dab = sb("dab", [128, 128]); dab_r = Res()
        rec = sb("rec", [128, 128]); rec_r = Res()
        hT = sb("hT", [128, 2, 128]); hT_r = RL(2)
        hsq = sb("hsq", [128, 2, 128], MMT); hsq_r = RL(2)
        rsh = sb("rsh", [128, 128]); rsh_r = Res()
        nS = sb("nS", [128, NH, NSEQ]); nS_r = Res()
        mS = sb("mS", [4, NSEQ]); mS_r = Res()

        def C(name, lo=0, hi=None, p0=0, p1=128):
            o, w = CF_OFF[name]
            if hi is None:
                hi = w
            return cf[p0:p1, o + lo:o + hi]

        def CR(name, lo=0, hi=None, p0=0, p1=128):
            o, w = CR_OFF[name]
            if hi is None:
                hi = w
            return crt[p0:p1, o + lo:o + hi]

        ident = C("ident")
        ones_r = CR("ones")
        identb = CR("identb")

        def ps():
            i = ps_i[0] % 4
            ps_i[0] += 1
            return psum[i], psum_r[i]

        def psL(i):
            return psum[4 + i], psum_r[4 + i]

        def wtile(Wap, row0, kt, col0, ncol):
            i = ring_i[0] % NRING
            ring_i[0] += 1
            view = ring[i][:, 0:kt * ncol].rearrange("p (k c) -> p k c", k=kt)
            src = Wap[row0:row0 + 128 * kt, col0:col0 + ncol].rearrange("(k p) c -> p k c", p=128)
            k.dma("pool", view, src, rd=(), wr=[ring_r[i]], semkey="rg%d" % i)
            return view, ring_r[i]

        def linear(Wap, row0, nkc, col0, ncols_, rhs_fn, evac_fn, N):
            KT = min(nkc, 8)
            nkh = nkc // KT
            BC = 2048 // KT
            c = 0
            j = 0
            while c < ncols_:
                bc = min(BC, ncols_ - c)
                tiles_ = [wtile(Wap, row0 + 128 * KT * hh, KT, col0 + c, bc) for hh in range(nkh)]
                subs = []
                cc = 0
                while cc < bc:
                    m = min(128, bc - cc)
                    subs.append((cc, m, ps()))
                    cc += m
                for hh in range(nkh):
                    tv, tr_ = tiles_[hh]
                    for (cc, m, (pt, pr)) in subs:
                        for kk in range(KT):
                            kc = hh * KT + kk
                            rap, rres = rhs_fn(kc)
                            k.mm(pt[0:m, 0:N], tv[:, kk, cc:cc + m], rap, kc == 0, kc == nkc - 1,
                                 rd=[tr_, rres], wr=[pr])
                for (cc, m, (pt, pr)) in subs:
                    evac_fn(j, pt, pr, m)
                    j += 1
                c += bc

        k.dma("sp", cf[:], cf_d, wr=[cf_r])
        for c0_ in range(0, NCR, 1024):
            c1_ = min(NCR, c0_ + 1024)
            k.dma("pool", crt[:, c0_:c1_], cr_d[:, c0_:c1_], wr=[cr_r])
        if True:
            sct = h[:].rearrange("p a b -> p (a b)")
            assert NSC <= NKD * NTM
            sct_r = Res(); sw_r = Res()
            k.dma("sp", sct[:, 0:NSC], sc_d, wr=[sct_r])
            _pc = [tA[:, 0:128], tA[:, 128:256], tB[:, 0:128], tB[:, 128:256], tC[:, 0:128], tC[:, 128:256],
                   tD[:, 0:128], tD[:, 128:256], rstd[:, 0:128], rstd[:, 128:256]]

            def sincos(theta, s_out, c_out, t0, t1, rdl, wrl):
                for (shift, dst) in ((0.0, s_out), (0.5 * math.pi, c_out)):
                    k.ts(t0, theta, 1.0 / TWO_PI, shift / TWO_PI, ALU.mult, ALU.add, rd=rdl, wr=wrl)
                    k.ts(t0, t0, MAGIC, None, ALU.add, None, rd=wrl, wr=wrl)
                    k.ts(t0, t0, -MAGIC, None, ALU.add, None, rd=wrl, wr=wrl)
                    k.stt(t1, t0, -TWO_PI, theta, ALU.mult, ALU.add, rd=rdl + wrl, wr=wrl)
                    k.ts(t1, t1, shift, None, ALU.add, None, rd=wrl, wr=wrl)
                    k.ts(t1, t1, math.pi, -math.pi, ALU.min, ALU.max, rd=wrl, wr=wrl)
                    k.act(dst, t1, AF.Sin, rd=wrl, wr=wrl)

            rs = [sct_r, sw_r]
            ws = [sw_r]
            for c8_ in range(8):
                def S(name, c8_=c8_):
                    o, w = SC_OFF[name]
                    return sct[:, o + 128 * c8_:o + 128 * c8_ + 128]
                X = lambda i: _pc[i]
                k.act(X(0), S("ldtB"), AF.Exp, rd=rs, wr=ws)
                k.tt(X(1), S("AimB"), X(0), ALU.mult, rd=rs, wr=ws)
                k.tt(X(2), S("AreB"), X(0), ALU.mult, rd=rs, wr=ws)
                k.act(X(2), X(2), AF.Exp, rd=rs, wr=ws)
                sincos(X(1), X(3), X(4), X(5), X(6), rs, ws)
                k.tt(X(3), X(3), X(2), ALU.mult, rd=rs, wr=ws)
                k.tt(X(4), X(4), X(2), ALU.mult, rd=rs, wr=ws)
                k.ts(X(4), X(4), -1.0, None, ALU.add, None, rd=rs, wr=ws)
                k.tt(X(5), S("AreB"), S("AreB"), ALU.mult, rd=rs, wr=ws)
                k.tt(X(6), S("AimB"), S("AimB"), ALU.mult, rd=rs, wr=ws)
                k.tt(X(5), X(5), X(6), ALU.add, rd=rs, wr=ws)
                k.recip(X(5), X(5), rd=rs, wr=ws)
                k.tt(X(6), X(4), S("AreB"), ALU.mult, rd=rs, wr=ws)
                k.tt(X(7), X(3), S("AimB"), ALU.mult, rd=rs, wr=ws)
                k.tt(X(6), X(6), X(7), ALU.add, rd=rs, wr=ws)
                k.tt(X(6), X(6), X(5), ALU.mult, rd=rs, wr=ws)
                k.tt(X(7), X(3), S("AreB"), ALU.mult, rd=rs, wr=ws)
                k.tt(X(8), X(4), S("AimB"), ALU.mult, rd=rs, wr=ws)
                k.tt(X(7), X(7), X(8), ALU.subtract, rd=rs, wr=ws)
                k.tt(X(7), X(7), X(5), ALU.mult, rd=rs, wr=ws)
                k.tt(X(8), X(6), S("BreB"), ALU.mult, rd=rs, wr=ws)
                k.tt(X(9), X(7), S("BimB"), ALU.mult, rd=rs, wr=ws)
                k.tt(Btab[:, c8_, 0, :], X(8), X(9), ALU.subtract, rd=rs, wr=[btab_r])
                k.tt(X(8), X(6), S("BimB"), ALU.mult, rd=rs, wr=ws)
                k.tt(X(9), X(7), S("BreB"), ALU.mult, rd=rs, wr=ws)
                k.tt(Btab[:, c8_, 1, :], X(8), X(9), ALU.add, rd=rs + [btab_r], wr=[btab_r])
                s3 = c8_ % 3
                k.cp(Bext[32 * s3:32 * s3 + 32, c8_ // 3, :, :], fr(Btab[96:128, c8_, :, :]), rd=[btab_r], wr=[btab_r])
            rs = [cf_r, sw_r, trig_r]
            ws = [sw_r, trig_r]
            dtL = tD[:, 0:32]; thL = tD[:, 32:64]; adL = tD[:, 64:96]
            k.act(dtL, C("ldt"), AF.Exp, rd=rs, wr=ws)
            k.tt(thL, C("Aim"), dtL, ALU.mult, rd=rs, wr=ws)
            k.tt(adL, C("Are"), dtL, ALU.mult, rd=rs, wr=ws)
            k.act(lam[:], adL, AF.Exp, rd=rs, wr=ws)
            k.act(ilam[:], adL, AF.Exp, rd=rs, wr=ws, scale=-1.0)
            for hq in range(8):
                qs_ = slice(4 * hq, 4 * hq + 4)
                ang = tA[:, 0:4 * LT]
                k.tt(ang.rearrange("p (q t) -> p q t", q=4), thL[:, qs_].unsqueeze(2).broadcast_to([128, 4, LT]),
                     C("tau1").unsqueeze(1).broadcast_to([128, 4, LT]), ALU.mult, rd=rs, wr=ws)
                sincos(ang, st[:, qs_, :].rearrange("p q t -> p (q t)"), ct[:, qs_, :].rearrange("p q t -> p (q t)"),
                       tB[:, 0:4 * LT], tC[:, 0:4 * LT], rs, ws)
            k.tt(am16[:], lam[:].unsqueeze(2).broadcast_to([128, 32, LT]),
                 C("m01").unsqueeze(1).broadcast_to([128, 32, LT]), ALU.mult, rd=rs, wr=ws)
            k.cp(am8[:], am16[:, :, 0:LS], rd=rs, wr=ws)
            k.tt(Rpr[:], lam[:], ct[:, :, LT - 1], ALU.mult, rd=rs, wr=ws)
            k.tt(Rpi[:], lam[:], st[:, :, LT - 1], ALU.mult, rd=rs, wr=ws)
            k.tt(Rp8r[:], lam[:], ct[:, :, LS - 1], ALU.mult, rd=rs, wr=ws)
            k.tt(Rp8i[:], lam[:], st[:, :, LS - 1], ALU.mult, rd=rs, wr=ws)
            k.memset(injr[:], 0.0, wr=inj_r)
            k.memset(inji[:], 0.0, wr=inj_r)
            k.memset(Cp[:].rearrange("p a b c -> p (a b c)"), 0.0, wr=Cp_r)
            k.memset(np_[:], 0.0, wr=[np_r])
            k.memset(m0p[:], 0.0, wr=[m0p_r])
            k.memset(hist[:].rearrange("p a b -> p (a b)"), 0.0, wr=hist_r)
            k.barrier()
        k.barrier()

        def rms_stats(N, src_fn):
            pt, pr = ps()
            for c in range(NKD):
                sap, sres = src_fn(c)
                k.act(sq[:, c % 2, 0:N], sap, AF.Square, rd=[sres], wr=[sq_r[c % 2]])
                k.mm(pt[:, 0:N], ones_r, sq[:, c % 2, 0:N], c == 0, c == NKD - 1, rd=[cr_r, sq_r[c % 2]], wr=[pr])
            k.ts(tA[:, 0:N], pt[:, 0:N], 1.0 / D, EPS, ALU.mult, ALU.add, rd=[pr], wr=[tA_r])
            k.act(tA[:, 0:N], tA[:, 0:N], AF.Sqrt, rd=[tA_r], wr=[tA_r])
            k.recip(rstd[:, 0:N], tA[:, 0:N], rd=[tA_r], wr=[rstd_r])

        def apply_norm(N, gname):
            for c in range(NKD):
                k.stt(xn[:, c, 0:N], h[:, c, 0:N], C(gname, c, c + 1), rstd[:, 0:N], ALU.mult, ALU.mult,
                      rd=[h_r[c], cf_r, rstd_r], wr=[xn_r[c]])

        def ffn(N, wg, wu, wd):
            xrhs = lambda kc: (xn[:, kc, 0:N], xn_r[kc])
            for p in range(DFF // 512):
                mb = p % 2
                gs = sb_tmp_g

                def ev_gate(j, pt, pr, m, gs=gs):
                    k.act(gs[j][0][:, 0:N], pt[:, 0:N], AF.Silu, rd=[pr], wr=[gs[j][1]])

                def ev_up(j, pt, pr, m, gs=gs, mb=mb):
                    k.tt(mid[:, mb, j, 0:N], pt[:, 0:N], gs[j][0][:, 0:N], ALU.mult, rd=[pr, gs[j][1]],
                         wr=[mid_r[mb][j]])
                for half in range(2):
                    linear(wg, 0, NKD, 512 * p + 256 * half, 256, xrhs,
                           lambda j, pt, pr, m, half=half: ev_gate(2 * half + j, pt, pr, m), N)
                    linear(wu, 0, NKD, 512 * p + 256 * half, 256, xrhs,
                           lambda j, pt, pr, m, half=half: ev_up(2 * half + j, pt, pr, m), N)

                def ev_down(j, pt, pr, m):
                    k.stt(h[:, j, 0:N], pt[:, 0:N], 0.5, h[:, j, 0:N], ALU.mult, ALU.add, rd=[pr, h_r[j]], wr=[h_r[j]])
                linear(wd, 512 * p, 4, 0, D, lambda kc, mb=mb: (mid[:, mb, kc, 0:N], mid_r[mb][kc]), ev_down, N)

        sb_tmp_g = [(tA, tA_r), (tB, tB_r), (tC, tC_r), (tD, tD_r)]

        col0 = 0
        scol0 = pl
        for ti, (pc, p0, scn, mode) in enumerate(plan):
            N = pc + scn
            state_only = (mode == "pre")
            if pc > 0:
                k.dma("sp", h[:, :, 0:pc], xT[:, p0:p0 + pc].rearrange("(c p) t -> p c t", p=128), wr=h_r)
            if scn > 0:
                k.dma("sp", h[:, :, pc:pc + scn], xT[:, scol0:scol0 + scn].rearrange("(c p) t -> p c t", p=128), wr=h_r)
            rms_stats(N, lambda c: (h[:, c, 0:N], h_r[c]))
            apply_norm(N, "g1")
            ffn(N, W["f1g"], W["f1u"], W["f1d"])
            if debug:
                if pc > 0:
                    k.dma("sp", dbg["d_h1"][:, p0:p0 + pc].rearrange("(c p) t -> p c t", p=128), h[:, :, 0:pc], rd=h_r)
                if scn > 0:
                    k.dma("sp", dbg["d_h1"][:, scol0:scol0 + scn].rearrange("(c p) t -> p c t", p=128), h[:, :, pc:N], rd=h_r)
            if stop == "ffn1":
                continue
            rms_stats(N, lambda c: (h[:, c, 0:N], h_r[c]))
            apply_norm(N, "gm")
            xrhs = lambda kc: (xn[:, kc, 0:N], xn_r[kc])

            k.barrier()
            segs = []
            if pc > 0:
                pmain = (pc // LT) * LT
                if pmain > 0:
                    segs.append(("P", 0, pmain, pmain // LT, LT))
                if pc > pmain:
                    assert (pc - pmain) % LS == 0
                    segs.append(("P", pmain, pc - pmain, (pc - pmain) // LS, LS))
            if scn > 0:
                segs.append(("S", pc, scn, NSEQ, LS))
            for c8 in range(8):
                ub = c8 % 2
                holder = {}

                s3 = c8 % 3

                def ev_u(j, pt, pr, m, ub=ub, s3=s3):
                    k.cp(u8[:, ub, 0:N], pt[:, 0:N], rd=[pr], wr=[u8_r[ub]])
                    k.cp(u3[32 * s3:32 * s3 + 32, ub, 0:N], pt[96:128, 0:N], rd=[pr], wr=[u8_r[ub]])
                    if not state_only:
                        k.cp(sigo[:, ub, 0:N], pt[:, 0:N], rd=[pr], wr=[sigo_r[ub]])
                linear(W["win"], 0, NKD, O_S5 + 128 * c8, 128, xrhs, ev_u, N)
                if lvl <= 1:
                    continue
                for j in range(4):
                    q = 4 * c8 + j
                    pre, prr = ps()
                    pim, pir = ps()
                    if j < 3:
                        lre = Btab[32 * j:32 * j + 32, c8, 0, :]; lim = Btab[32 * j:32 * j + 32, c8, 1, :]
                        urhs = u8[32 * j:32 * j + 32, ub, 0:N]
                    else:
                        lre = Bext[32 * s3:32 * s3 + 32, c8 // 3, 0, :]; lim = Bext[32 * s3:32 * s3 + 32, c8 // 3, 1, :]
                        urhs = u3[32 * s3:32 * s3 + 32, ub, 0:N]
                    k.mm(pre[:, 0:N], lre, urhs, True, True, rd=[btab_r, u8_r[ub]], wr=[prr])
                    k.mm(pim[:, 0:N], lim, urhs, True, True, rd=[btab_r, u8_r[ub]], wr=[pir])
                    for (kind, s0, sn, nsub, L) in segs:
                        def V3(ap2):
                            return ap2.rearrange("p (s l) -> p s l", l=L)
                        cb = ct[:, q, 0:L].unsqueeze(1).broadcast_to([128, nsub, L])
                        sbb = st[:, q, 0:L].unsqueeze(1).broadcast_to([128, nsub, L])
                        t1 = V3(tA[:, s0:s0 + sn]); t2 = V3(tB[:, s0:s0 + sn])
                        tr4 = twr[:, 4 * s0:4 * (s0 + sn)].rearrange("p (s j l) -> p s j l", j=4, l=L)
                        ti4 = twi[:, 4 * s0:4 * (s0 + sn)].rearrange("p (s j l) -> p s j l", j=4, l=L)
                        k.tt(t1, V3(pre[:, s0:s0 + sn]), cb, ALU.mult, rd=[prr, trig_r], wr=[tA_r])
                        k.tt(t2, V3(pim[:, s0:s0 + sn]), sbb, ALU.mult, rd=[pir, trig_r], wr=[tB_r])
                        k.tt(tr4[:, :, j, :], t1, t2, ALU.add, rd=[tA_r, tB_r], wr=[tw_r])
                        k.tt(t1, V3(pim[:, s0:s0 + sn]), cb, ALU.mult, rd=[pir, trig_r], wr=[tA_r])
                        k.tt(t2, V3(pre[:, s0:s0 + sn]), sbb, ALU.mult, rd=[prr, trig_r], wr=[tB_r])
                        k.tt(ti4[:, :, j, :], t1, t2, ALU.subtract, rd=[tA_r, tB_r], wr=[tw_r])
                for (kind, s0, sn, nsub, L) in segs:
                    tr4 = twr[:, 4 * s0:4 * (s0 + sn)].rearrange("p (s j l) -> p s j l", j=4, l=L)
                    ti4 = twi[:, 4 * s0:4 * (s0 + sn)].rearrange("p (s j l) -> p s j l", j=4, l=L)
                    q4 = slice(4 * c8, 4 * c8 + 4)
                    if kind == "P":
                        amf = (am16 if L == LT else am8)[:, q4, :].rearrange("p j l -> p (j l)")
                        Rr_, Ri_ = (Rpr, Rpi) if L == LT else (Rp8r, Rp8i)
                        for sbi in range(nsub):
                            k.tt(tr4[:, sbi, :, 0], tr4[:, sbi, :, 0], injr[:, q4], ALU.add, rd=[tw_r, inj_r[c8]], wr=[tw_r])
                            k.tt(ti4[:, sbi, :, 0], ti4[:, sbi, :, 0], inji[:, q4], ALU.add, rd=[tw_r, inj_r[c8]], wr=[tw_r])
                            fr_ = tr4[:, sbi, :, :].rearrange("p j l -> p (j l)")
                            fi_ = ti4[:, sbi, :, :].rearrange("p j l -> p (j l)")
                            k.scan(fr_, amf, fr_, 0.0, ALU.mult, ALU.add, rd=[tw_r, trig_r], wr=[tw_r])
                            k.scan(fi_, amf, fi_, 0.0, ALU.mult, ALU.add, rd=[tw_r, trig_r], wr=[tw_r])
                            glr = tr4[:, sbi, :, L - 1]; gli = ti4[:, sbi, :, L - 1]
                            b = lambda i: bt[:, i, :]
                            k.tt(b(0), Rr_[:, q4], glr, ALU.mult, rd=[tw_r, trig_r], wr=[bt_r])
                            k.tt(b(1), Ri_[:, q4], gli, ALU.mult, rd=[tw_r, trig_r], wr=[bt_r])
                            k.tt(b(2), Rr_[:, q4], gli, ALU.mult, rd=[tw_r, trig_r], wr=[bt_r])
                            k.tt(b(3), Ri_[:, q4], glr, ALU.mult, rd=[tw_r, trig_r], wr=[bt_r])
                            k.tt(injr[:, q4], b(0), b(1), ALU.subtract, rd=[bt_r], wr=[inj_r[c8]])
                            k.tt(inji[:, q4], b(2), b(3), ALU.add, rd=[bt_r], wr=[inj_r[c8]])
                    else:
                        h0r = C("h0r").rearrange("p (q s) -> p q s", q=32)[:, q4, :].rearrange("p j s -> p s j")
                        h0i = C("h0i").rearrange("p (q s) -> p q s", q=32)[:, q4, :].rearrange("p j s -> p s j")
                        lb = lam[:, q4].unsqueeze(1).broadcast_to([128, NSEQ, 4])
                        b16 = lambda i: bt[:, 2 * i:2 * i + 2, :].rearrange("p a b -> p (a b)")
                        k.tt(tA[:, 0:64].rearrange("p (s j) -> p s j", j=4), h0r, lb, ALU.mult, rd=[cf_r, trig_r], wr=[tA_r])
                        k.tt(tr4[:, :, :, 0], tr4[:, :, :, 0], tA[:, 0:64].rearrange("p (s j) -> p s j", j=4), ALU.add,
                             rd=[tw_r, tA_r], wr=[tw_r])
                        k.tt(tA[:, 0:64].rearrange("p (s j) -> p s j", j=4), h0i, lb, ALU.mult, rd=[cf_r, trig_r], wr=[tA_r])
                        k.tt(ti4[:, :, :, 0], ti4[:, :, :, 0], tA[:, 0:64].rearrange("p (s j) -> p s j", j=4), ALU.add,
                             rd=[tw_r, tA_r], wr=[tw_r])
                        amf = am8[:, q4, :].rearrange("p j l -> p (j l)")
                        for sq_ in range(NSEQ):
                            fr_ = tr4[:, sq_, :, :].rearrange("p j l -> p (j l)")
                            fi_ = ti4[:, sq_, :, :].rearrange("p j l -> p (j l)")
                            k.scan(fr_, amf, fr_, 0.0, ALU.mult, ALU.add, rd=[tw_r, trig_r], wr=[tw_r])
                            k.scan(fi_, amf, fi_, 0.0, ALU.mult, ALU.add, rd=[tw_r, trig_r], wr=[tw_r])
                        glr = tr4[:, :, :, L - 1]; gli = ti4[:, :, :, L - 1]
                        cb = ct[:, q4, LS - 1].unsqueeze(1).broadcast_to([128, NSEQ, 4])
                        sbb = st[:, q4, LS - 1].unsqueeze(1).broadcast_to([128, NSEQ, 4])
                        T = lambda i: tA[:, 64 * i:64 * i + 64].rearrange("p (s j) -> p s j", j=4)
                        k.tt(T(0), glr, cb, ALU.mult, rd=[tw_r, trig_r], wr=[tA_r])
                        k.tt(T(1), gli, sbb, ALU.mult, rd=[tw_r, trig_r], wr=[tA_r])
                        k.tt(T(2), gli, cb, ALU.mult, rd=[tw_r, trig_r], wr=[tA_r])
                        k.tt(T(3), glr, sbb, ALU.mult, rd=[tw_r, trig_r], wr=[tA_r])
                        k.tt(tB[:, 0:64].rearrange("p (j s) -> p s j", j=4), T(0), T(1), ALU.subtract, rd=[tA_r], wr=[tB_r])
                        k.tt(tB[:, 64:128].rearrange("p (j s) -> p s j", j=4), T(2), T(3), ALU.add, rd=[tA_r], wr=[tB_r])
                        k.dma("sp", o_ss5[:, 64 * c8:64 * c8 + 64], tB[:, 0:64], rd=[tB_r])
                        k.dma("sp", o_ss5[:, 512 + 64 * c8:512 + 64 * c8 + 64], tB[:, 64:128], rd=[tB_r])
                if lvl <= 2 or state_only:
                    continue
                yps, ypr = psL(0)
                for j in range(4):
                    q = 4 * c8 + j
                    for (kind, s0, sn, nsub, L) in segs:
                        def V3(ap2):
                            return ap2.rearrange("p (s l) -> p s l", l=L)
                        tr4 = twr[:, 4 * s0:4 * (s0 + sn)].rearrange("p (s j l) -> p s j l", j=4, l=L)
                        ti4 = twi[:, 4 * s0:4 * (s0 + sn)].rearrange("p (s j l) -> p s j l", j=4, l=L)
                        cb = ct[:, q, 0:L].unsqueeze(1).broadcast_to([128, nsub, L])
                        sbb = st[:, q, 0:L].unsqueeze(1).broadcast_to([128, nsub, L])
                        gr = tr4[:, :, j, :]; gi = ti4[:, :, j, :]
                        k.tt(V3(xs[:, 0, s0:s0 + sn]), cb, gr, ALU.mult, rd=[tw_r, trig_r], wr=[xs_r[0]])
                        k.stt(V3(xs[:, 1, s0:s0 + sn]), sbb, -1.0, gi, ALU.mult, ALU.mult, rd=[tw_r, trig_r], wr=[xs_r[1]])
                        k.stt(V3(xs[:, 2, s0:s0 + sn]), sbb, -1.0, gr, ALU.mult, ALU.mult, rd=[tw_r, trig_r], wr=[xs_r[2]])
                        k.stt(V3(xs[:, 3, s0:s0 + sn]), cb, -1.0, gi, ALU.mult, ALU.mult, rd=[tw_r, trig_r], wr=[xs_r[3]])
                    Crq = CR("Cr").rearrange("p (q m) -> p q m", q=32)[:, q, :]
                    Ciq = CR("Ci").rearrange("p (q m) -> p q m", q=32)[:, q, :]
                    if j < 3:
                        for i4, lh in enumerate((Crq, Crq, Ciq, Ciq)):
                            k.mm(yps[32 * j:32 * j + 32, 0:N], lh, xs[:, i4, 0:N], i4 == 0, i4 == 3, rd=[cr_r, xs_r[i4]], wr=[ypr])
                    else:
                        y2, y2r = psL(1)
                        for i4, lh in enumerate((Crq, Crq, Ciq, Ciq)):
                            k.mm(y2[0:32, 0:N], lh, xs[:, i4, 0:N], i4 == 0, i4 == 3, rd=[cr_r, xs_r[i4]], wr=[y2r])
                        k.act(yps[96:128, 0:N], y2[0:32, 0:N], AF.Copy, rd=[y2r], wr=[ypr])
                if lvl <= 3:
                    continue
                k.stt(tC[:, 0:N], sigo[:, ub, 0:N], C("s5D", c8, c8 + 1), yps[:, 0:N], ALU.mult, ALU.add,
                      rd=[sigo_r[ub], cf_r, ypr], wr=[tC_r])
                k.act(tD[:, 0:N], tC[:, 0:N], AF.Square, rd=[tC_r], wr=[tD_r])
                k.ts(tD[:, 0:N], tD[:, 0:N], 0.044715, 1.0, ALU.mult, ALU.add, rd=[tD_r], wr=[tD_r])
                k.tt(tD[:, 0:N], tD[:, 0:N], tC[:, 0:N], ALU.mult, rd=[tD_r, tC_r], wr=[tD_r])
                k.act(tD[:, 0:N], tD[:, 0:N], AF.Sigmoid, rd=[tD_r], wr=[tD_r], scale=2.0 * math.sqrt(2.0 / math.pi))
                k.tt(ysp[:, c8, 0:N], tC[:, 0:N], tD[:, 0:N], ALU.mult, rd=[tC_r, tD_r], wr=[ysp_r[c8]])
            if lvl <= 4:
                continue
            def ev_glu(j, pt, pr, m):
                k.act(tA[:, 0:N], pt[:, 0:N], AF.Sigmoid, rd=[pr], wr=[tA_r])
                k.tt(ys5[:, j, 0:N], fr(ysp[:, j, 0:N]), tA[:, 0:N], ALU.mult, rd=[ysp_r[j], tA_r], wr=[ys5_r[j]])
            if not state_only:
                linear(W["wglu"], 0, 8, 0, S5W, lambda kc: (ysp[:, kc, 0:N], ysp_r[kc]), ev_glu, N)
            if debug:
                if pc > 0:
                    k.dma("pool", dbg["d_ys5"][:, p0:p0 + pc].rearrange("(c p) t -> p c t", p=128), fr(ys5[:, :, 0:pc]), rd=ys5_r)
                if scn > 0:
                    k.dma("pool", dbg["d_ys5"][:, scol0:scol0 + scn].rearrange("(c p) t -> p c t", p=128), fr(ys5[:, :, pc:N]), rd=ys5_r)

            if stop == "s5":
                continue
            k.barrier()
            R = lambda i: rows[:, i, 0:N]
            pi_, pir_ = ps()
            pf_, pfr_ = ps()
            wif = [wtile(W["win"], 1024 * hh, 8, O_I, 8) for hh in range(2)]
            for kc in range(NKD):
                tv, tr_ = wif[kc // 8]
                k.mm(pi_[0:4, 0:N], tv[:, kc % 8, 0:4], xn[:, kc, 0:N], kc == 0, kc == NKD - 1, rd=[tr_, xn_r[kc]], wr=[pir_])
            for kc in range(NKD):
                tv, tr_ = wif[kc // 8]
                k.mm(pf_[0:4, 0:N], tv[:, kc % 8, 4:8], xn[:, kc, 0:N], kc == 0, kc == NKD - 1, rd=[tr_, xn_r[kc]], wr=[pfr_])
            k.ts(R(0), pf_[0:4, 0:N], C("bf", p0=0, p1=4), None, ALU.add, None, rd=[pfr_, cf_r], wr=[rows_r])
            k.act(R(0), R(0), AF.Exp, rd=[rows_r], wr=[rows_r], scale=-1.0)
            k.act(R(0), R(0), AF.Ln, rd=[rows_r], wr=[rows_r], bias=1.0)
            k.ts(R(1), pi_[0:4, 0:N], C("bi", p0=0, p1=4), None, ALU.add, None, rd=[pir_, cf_r], wr=[rows_r])
            chunks = []
            c = 0
            while c < pc:
                L = min(128, pc - c)
                chunks.append(("P", c, L))
                c += L
            if scn > 0:
                chunks.append(("S", pc, scn))
            for (kind, c0, L) in chunks:
                cs = slice(c0, c0 + L)
                rm = C("rmP" if kind == "P" else "rmS", 0, L, 0, 4)
                ra = C("raP" if kind == "P" else "raS", 0, L, 0, 4)
                k.scan(rows[:, 2, cs], rm, rows[:, 0, cs], 0.0, ALU.mult, ALU.subtract, rd=[rows_r, cf_r], wr=[rows_r])
                k.tt(rows[:, 3, cs], rows[:, 1, cs], rows[:, 2, cs], ALU.subtract, rd=[rows_r], wr=[rows_r])
                k.scan(rows[:, 4, cs], ra, rows[:, 3, cs], NEGBIG, ALU.add, ALU.max, rd=[rows_r, cf_r], wr=[rows_r])
                if kind == "P":
                    k.ts(rows[:, 4, cs], rows[:, 4, cs], m0p[:, 0:1], None, ALU.max, None, rd=[rows_r, m0p_r], wr=[rows_r])
                    k.ts(rows[:, 5, cs], rows[:, 4, cs], -1.0, m0p[:, 0:1], ALU.mult, ALU.add, rd=[rows_r, m0p_r], wr=[rows_r])
                else:
                    m0row = C("m0S", p0=0, p1=4).unsqueeze(2).broadcast_to([4, NSEQ, LS])
                    v3 = lambda i: rows[:, i, cs].rearrange("p (s l) -> p s l", l=LS)
                    k.tt(v3(4), v3(4), m0row, ALU.max, rd=[rows_r, cf_r], wr=[rows_r])
                    k.tt(v3(5), m0row, v3(4), ALU.subtract, rd=[rows_r, cf_r], wr=[rows_r])
                k.tt(rows[:, 6, cs], rows[:, 2, cs], rows[:, 4, cs], ALU.add, rd=[rows_r], wr=[rows_r])
                if kind == "P":
                    k.ts(rows[:, 7, cs], rows[:, 3, cs], rows[:, 4, c0 + L - 1:c0 + L], None, ALU.subtract, None, rd=[rows_r], wr=[rows_r])
                else:
                    v3 = lambda i: rows[:, i, cs].rearrange("p (s l) -> p s l", l=LS)
                    ml = v3(4)[:, :, LS - 1:LS].broadcast_to([4, NSEQ, LS])
                    k.tt(v3(7), v3(3), ml, ALU.subtract, rd=[rows_r], wr=[rows_r])
                k.ts(rows[:, 7, cs], rows[:, 7, cs], 0.0, None, ALU.min, None, rd=[rows_r], wr=[rows_r])
                k.act(rows[:, 7, cs], rows[:, 7, cs], AF.Exp, rd=[rows_r], wr=[rows_r])
                if kind == "P":
                    k.cp(m0p[:, 0:1], rows[:, 6, c0 + L - 1:c0 + L], rd=[rows_r], wr=[m0p_r])
                else:
                    k.cp(mS[:, :], rows[:, 6, cs].rearrange("p (s l) -> p s l", l=LS)[:, :, LS - 1], rd=[rows_r], wr=[mS_r])
            def prompt_state_update(hd, cs, L, vb, pb, pbr):
                pk, pkr = ps()
                pkb = pk[:].bitcast(MMT)
                k.tr(pkb[0:L, 0:128], kT[:, cs], identb, rd=[kT_r, cr_r], wr=[pkr])
                k.ts(kw[0:L, :], pkb[0:L, 0:128], wT[0:L, hd:hd + 1], None, ALU.mult, None, rd=[pkr, awT_r], wr=[kw_r])
                dcol = pb[:, 128 + L - 1:128 + L]
                for vc in range(2):
                    pcu, pcur = ps()
                    k.mm(pcu[:, 0:128], vtok[0:L, vb, 128 * vc:128 * vc + 128], kw[0:L, :], True, True,
                         rd=[vtok_r[vb], kw_r], wr=[pcur])
                    k.stt(Cp[:, hd, vc, :], Cp[:, hd, vc, :], dcol, pcu[:, 0:128], ALU.mult, ALU.add,
                          rd=[Cp_r[hd], pbr, pcur], wr=[Cp_r[hd]])
                pnu, pnur = ps()
                k.mm(pnu[:, 0:2], kw[0:L, :], ones_r[0:L, 0:2], True, True, rd=[kw_r, cr_r], wr=[pnur])
                k.stt(np_[:, hd:hd + 1], np_[:, hd:hd + 1], dcol, pnu[:, 0:1], ALU.mult, ALU.add,
                      rd=[np_r, pbr, pnur], wr=[np_r])

            for hd in range(NH):
                for qk in range(2):
                    idx = qk * 4 + hd

                    def ev_qk(j, pt, pr, m, qk=qk, idx=idx):
                        if pc > 0:
                            k.cp(cbuf[:, qk, 0:3], hist[:, idx, :], rd=[hist_r[idx]], wr=[cbuf_r[qk]])
                            k.act(cbuf[:, qk, 3:3 + pc], pt[:, 0:pc], AF.Copy, rd=[pr], wr=[cbuf_r[qk]])
                            k.cp(hist[:, idx, :], cbuf[:, qk, pc:pc + 3], rd=[cbuf_r[qk]], wr=[hist_r[idx]])
                        if scn > 0:
                            hs = C("histS").rearrange("p (i s j) -> p i s j", i=8, s=NSEQ)[:, idx, :, :]
                            k.cp(cbs[:, qk, :, 0:3], hs, rd=[cf_r], wr=[cbs_r[qk]])
                            k.act(cbs[:, qk, :, 3:3 + LS], pt[:, pc:pc + scn].rearrange("p (s l) -> p s l", l=LS), AF.Copy,
                                  rd=[pr], wr=[cbs_r[qk]])
                    linear(W["win"], 0, NKD, O_QK + 512 * qk + 128 * hd, 128, xrhs, ev_qk, N)
                    wc = lambda jj, idx=idx: C("convw", 5 * idx + jj, 5 * idx + jj + 1)
                    if state_only and qk == 0:
                        continue
                    if pc > 0:
                        k.ts(cacc[:, qk, 0:pc], cbuf[:, qk, 0:pc], wc(0), wc(4), ALU.mult, ALU.add, rd=[cbuf_r[qk], cf_r], wr=[cacc_r[qk]])
                        for jj in range(1, 4):
                            k.stt(cacc[:, qk, 0:pc], cbuf[:, qk, jj:jj + pc], wc(jj), cacc[:, qk, 0:pc], ALU.mult, ALU.add,
                                  rd=[cbuf_r[qk], cf_r, cacc_r[qk]], wr=[cacc_r[qk]])
                    if scn > 0:
                        ca3 = cacc[:, qk, pc:pc + scn].rearrange("p (s l) -> p s l", l=LS)
                        k.ts(ca3, cbs[:, qk, :, 0:LS], wc(0), wc(4), ALU.mult, ALU.add, rd=[cbs_r[qk], cf_r], wr=[cacc_r[qk]])
                        for jj in range(1, 4):
                            k.stt(ca3, cbs[:, qk, :, jj:jj + LS], wc(jj), ca3, ALU.mult, ALU.add,
                                  rd=[cbs_r[qk], cf_r, cacc_r[qk]], wr=[cacc_r[qk]])
                        if hd == NH - 1 or True:
                            k.dma("sp", o_sconv[:, 48 * idx:48 * idx + 48].rearrange("p (s j) -> p s j", j=3),
                                  cbs[:, qk, :, LS:LS + 3], rd=[cbs_r[qk]])
                    k.act(tA[:, 0:N], cacc[:, qk, 0:N], AF.Sigmoid, rd=[cacc_r[qk]], wr=[tA_r])
                    dst, dst_r = (qT, qT_r) if qk == 0 else (kT, kT_r)
                    k.stt(dst[:, 0:N], cacc[:, qk, 0:N], (DQK ** -0.5) if qk == 0 else 1.0, tA[:, 0:N], ALU.mult, ALU.mult,
                          rd=[cacc_r[qk], tA_r], wr=[dst_r])
                def ev_o(j, pt, pr, m):
                    k.act(sigo[:, j, 0:N], pt[:, 0:N], AF.Sigmoid, rd=[pr], wr=[sigo_r[j]])
                if not state_only:
                    linear(W["win"], 0, NKD, O_O + 256 * hd, 256, xrhs, ev_o, N)
                wv = [wtile(W["win"], 1024 * hh, 8, O_V + 256 * hd, 256) for hh in range(2)]
                for ci, (kind, c0, L) in enumerate(chunks):
                    cs = slice(c0, c0 + L)
                    pa, par = ps()
                    k.tr(pa[0:L, 0:4], rows[:, 3, cs], ident[0:4, 0:4], rd=[rows_r, cf_r], wr=[par])
                    k.tr(pa[0:L, 4:8], rows[:, 7, cs], ident[0:4, 0:4], rd=[rows_r, cf_r], wr=[par])
                    k.cp(aT[0:L, :], pa[0:L, 0:4], rd=[par], wr=[awT_r])
                    k.cp(wT[0:L, :], pa[0:L, 4:8], rd=[par], wr=[awT_r])
                    k.ts(RB[:, 0, 0:L], rows[:, 4, cs], -1.0, None, ALU.mult, None, rd=[rows_r], wr=[RB_r])
                    k.act(RB[:, 1, 0:L], rows[:, 5, cs], AF.Exp, rd=[rows_r], wr=[RB_r])
                    k.act(RB[:, 2, 0:L], rows[:, 6, cs], AF.Exp, rd=[rows_r], wr=[RB_r], scale=-1.0)
                    vb = ci % 3
                    pv, pvr = ps()
                    for kc in range(NKD):
                        tv, tr_ = wv[kc // 8]
                        k.mm(pv[0:L, 0:DV], xn[:, kc, cs], tv[:, kc % 8, :], kc == 0, kc == NKD - 1, rd=[tr_, xn_r[kc]], wr=[pvr])
                    k.cp(vtok[0:L, vb, :], pv[0:L, 0:DV], rd=[pvr], wr=[vtok_r[vb]])
                    pb, pbr = psL(0)
                    sel = C("sel", 128 * hd, 128 * hd + 128, 0, 4)
                    for i3 in range(3):
                        k.mm(pb[:, 128 * i3:128 * i3 + L], sel, RB[:, i3, 0:L], True, True, rd=[cf_r, RB_r], wr=[pbr])
                    if state_only:
                        prompt_state_update(hd, cs, L, vb, pb, pbr)
                        continue
                    pS, pSr = ps()
                    k.mm(pS[0:L, 0:L], kT[:, cs], qT[:, cs], True, True, rd=[kT_r, qT_r], wr=[pSr])
                    k.ts(Et[0:L, 0:L], pb[0:L, 0:L], aT[0:L, hd:hd + 1], 0.0, ALU.add, ALU.min, rd=[pbr, awT_r], wr=[Et_r])
                    k.act(Wt[0:L, 0:L], Et[0:L, 0:L], AF.Exp, rd=[Et_r], wr=[Wt_r])
                    mask = C("maskP" if kind == "P" else "maskS", 0, L, 0, L)
                    k.tt(Wt[0:L, 0:L], Wt[0:L, 0:L], mask, ALU.mult, rd=[Wt_r, cf_r], wr=[Wt_r])
                    k.tt(SW[0:L, 0:L], pS[0:L, 0:L], Wt[0:L, 0:L], ALU.mult, rd=[pSr, Wt_r], wr=[SW_r])
                    k.tt(qs[:, 0:L], fr(qT[:, cs]), pb[:, 128:128 + L], ALU.mult, rd=[qT_r, pbr], wr=[qs_r])
                    if kind == "P":
                        k.ts(nq[:, 0:L], fr(qs[:, 0:L]), np_[:, hd:hd + 1], None, ALU.mult, None, rd=[qs_r, np_r], wr=[nq_r])
                    else:
                        n0 = C("n0S").rearrange("p (h s) -> p h s", h=NH)[:, hd, :].unsqueeze(2).broadcast_to([128, NSEQ, LS])
                        k.tt(nq[:, 0:L].rearrange("p (s l) -> p s l", l=LS), fr(qs[:, 0:L]).rearrange("p (s l) -> p s l", l=LS),
                             n0, ALU.mult, rd=[qs_r, cf_r], wr=[nq_r])
                    pd, pdr = psL(3)
                    k.mm(pd[:, 0:L], ones_r[0:L, :], SW[0:L, 0:L], True, False, rd=[cr_r, SW_r], wr=[pdr])
                    k.mm(pd[:, 0:L], ones_r, nq[:, 0:L], False, True, rd=[cr_r, nq_r], wr=[pdr])
                    pn = [psL(1), psL(2)]
                    if kind == "P":
                        for vc in range(2):
                            ptc, ptcr = ps()
                            k.tr(ptc[:, 0:128], Cp[:, hd, vc, :], ident, rd=[Cp_r[hd], cf_r], wr=[ptcr])
                            k.cp(CT[:, 0, 128 * vc:128 * vc + 128], ptc[:, 0:128], rd=[ptcr], wr=[CT_r[0]])
                        for vc in range(2):
                            pnt, pnr = pn[vc]
                            k.mm(pnt[:, 0:L], vtok[0:L, vb, 128 * vc:128 * vc + 128], SW[0:L, 0:L], True, False,
                                 rd=[vtok_r[vb], SW_r], wr=[pnr])
                            k.mm(pnt[:, 0:L], CT[:, 0, 128 * vc:128 * vc + 128], qs[:, 0:L], False, True,
                                 rd=[CT_r[0], qs_r], wr=[pnr])
                    else:
                        for vc in range(2):
                            pnt, pnr = pn[vc]
                            k.mm(pnt[:, 0:L], vtok[0:L, vb, 128 * vc:128 * vc + 128], SW[0:L, 0:L], True, False,
                                 rd=[vtok_r[vb], SW_r], wr=[pnr])
                        pk, pkr = ps()
                        pkb = pk[:].bitcast(MMT)
                        k.tr(pkb[0:L, 0:128], kT[:, cs], identb, rd=[kT_r, cr_r], wr=[pkr])
                        k.ts(kw[0:L, :], pkb[0:L, 0:128], wT[0:L, hd:hd + 1], None, ALU.mult, None, rd=[pkr, awT_r], wr=[kw_r])
                        for g_ in range(2):
                            gs_ = slice(8 * g_, 8 * g_ + 8)
                            for vc in range(2):
                                k.dma("sp", Cs[:, gs_, vc, :], sC_d[gs_, hd, 128 * vc:128 * vc + 128, :].rearrange("s p kk -> p s kk"),
                                      wr=Cs_r[gs_])
                        for s_ in range(NSEQ):
                            cb_ = s_
                            tb = s_ % 2
                            for vc in range(2):
                                ptc, ptcr = ps()
                                k.tr(ptc[:, 0:128], Cs[:, cb_, vc, :], ident, rd=[Cs_r[cb_], cf_r], wr=[ptcr])
                                k.cp(CT[:, tb, 128 * vc:128 * vc + 128], ptc[:, 0:128], rd=[ptcr], wr=[CT_r[tb]])
                            for vc in range(2):
                                pnt, pnr = pn[vc]
                                k.mm(pnt[:, LS * s_:LS * s_ + LS], CT[:, tb, 128 * vc:128 * vc + 128], qs[:, LS * s_:LS * s_ + LS],
                                     False, s_ == NSEQ - 1, rd=[CT_r[tb], qs_r], wr=[pnr])
                            k.ts(kwm[:, tb, :], fr(kw[:, :]), fr(CR("seqm", s_, s_ + 1)), None, ALU.mult, None,
                                 rd=[kw_r, cr_r], wr=[kwm_r[tb]])
                            for vc in range(2):
                                pcu, pcur = ps()
                                k.mm(pcu[:, 0:128], vtok[:, vb, 128 * vc:128 * vc + 128], kwm[:, tb, :], True, True,
                                     rd=[vtok_r[vb], kwm_r[tb]], wr=[pcur])
                                k.stt(Cs[:, cb_, vc, :], Cs[:, cb_, vc, :], pb[:, 128 + LS * s_ + LS - 1:128 + LS * s_ + LS],
                                      pcu[:, 0:128], ALU.mult, ALU.add, rd=[Cs_r[cb_], pbr, pcur], wr=[Cs_r[cb_]])
                        for g_ in range(2):
                            gs_ = slice(8 * g_, 8 * g_ + 8)
                            for vc in range(2):
                                k.dma("sp", o_sC[gs_, hd, 128 * vc:128 * vc + 128, :].rearrange("s p kk -> p s kk"), Cs[:, gs_, vc, :],
                                      rd=Cs_r[gs_])
                    k.act(dab[:, 0:L], pd[:, 0:L], AF.Abs, rd=[pdr], wr=[dab_r])
                    k.tt(dab[:, 0:L], dab[:, 0:L], pb[:, 256:256 + L], ALU.max, rd=[dab_r, pbr], wr=[dab_r])
                    k.recip(rec[:, 0:L], dab[:, 0:L], rd=[dab_r], wr=[rec_r])
                    for vc in range(2):
                        pnt, pnr = pn[vc]
                        k.tt(hT[:, vc, 0:L], pnt[:, 0:L], rec[:, 0:L], ALU.mult, rd=[pnr, rec_r], wr=[hT_r[vc]])
                        k.act(hsq[:, vc, 0:L], hT[:, vc, 0:L], AF.Square, rd=[hT_r[vc]], wr=[hsq_r[vc]])
                    ph, phr = ps()
                    for vc in range(2):
                        k.mm(ph[:, 0:L], ones_r, hsq[:, vc, 0:L], vc == 0, vc == 1, rd=[cr_r, hsq_r[vc]], wr=[phr])
                    k.ts(rsh[:, 0:L], ph[:, 0:L], 1.0 / DV, EPS, ALU.mult, ALU.add, rd=[phr], wr=[rsh_r])
                    k.act(rsh[:, 0:L], rsh[:, 0:L], AF.Sqrt, rd=[rsh_r], wr=[rsh_r])
                    k.recip(rsh[:, 0:L], rsh[:, 0:L], rd=[rsh_r], wr=[rsh_r])
                    for vc in range(2):
                        cidx = 2 * hd + vc
                        k.stt(hT[:, vc, 0:L], hT[:, vc, 0:L], C("mlg", cidx, cidx + 1), rsh[:, 0:L], ALU.mult, ALU.mult,
                              rd=[hT_r[vc], cf_r, rsh_r], wr=[hT_r[vc]])
                        k.tt(yml[:, cidx, cs], hT[:, vc, 0:L], sigo[:, vc, cs], ALU.mult, rd=[hT_r[vc], sigo_r[vc]], wr=[yml_r[cidx]])
                    if kind == "P":
                        prompt_state_update(hd, cs, L, vb, pb, pbr)
                    else:
                        pnu, pnur = ps()
                        k.mm(pnu[:, 0:NSEQ], kw[:, :], CR("seqm"), True, True, rd=[kw_r, cr_r], wr=[pnur])
                        dec = pb[:, 128:256].rearrange("p (s l) -> p s l", l=LS)[:, :, LS - 1]
                        n0h = C("n0S").rearrange("p (h s) -> p h s", h=NH)[:, hd, :]
                        k.tt(nS[:, hd, :], n0h, dec, ALU.mult, rd=[cf_r, pbr], wr=[nS_r])
                        k.tt(nS[:, hd, :], nS[:, hd, :], pnu[:, 0:NSEQ], ALU.add, rd=[nS_r, pnur], wr=[nS_r])
            if debug:
                if pc > 0:
                    k.dma("pool", dbg["d_yml"][:, p0:p0 + pc].rearrange("(c p) t -> p c t", p=128), fr(yml[:, :, 0:pc]), rd=yml_r)
                if scn > 0:
                    k.dma("pool", dbg["d_yml"][:, scol0:scol0 + scn].rearrange("(c p) t -> p c t", p=128), fr(yml[:, :, pc:N]), rd=yml_r)

            if stop == "mlstm":
                continue
            if state_only:
                if ti + 1 < len(plan) and plan[ti + 1][3] != "pre":
                    fl = C("flag")
                    k.ts(injr[:], injr[:], fl, None, ALU.mult, None, rd=inj_r + [cf_r], wr=inj_r)
                    k.ts(inji[:], inji[:], fl, None, ALU.mult, None, rd=inj_r + [cf_r], wr=inj_r)
                    cpf = Cp[:].rearrange("p a b c -> p (a b c)")
                    k.ts(cpf, cpf, fl, None, ALU.mult, None, rd=Cp_r + [cf_r], wr=Cp_r)
                    k.ts(np_[:], np_[:], fl, None, ALU.mult, None, rd=[np_r, cf_r], wr=[np_r])
                    k.ts(m0p[:], m0p[:], C("flag", p0=0, p1=4), None, ALU.mult, None, rd=[m0p_r, cf_r], wr=[m0p_r])
                    hf = hist[:].rearrange("p a b -> p (a b)")
                    k.ts(hf, hf, fl, None, ALU.mult, None, rd=hist_r + [cf_r], wr=hist_r)
                continue
            k.barrier()
            for half in range(2):
                for blk in range(4):
                    cbase = 8 * half + 2 * blk

                    def ev_g(j, pt, pr, m, dst):
                        k.act(dst[j][0][:, 0:N], pt[:, 0:N], AF.Sigmoid, rd=[pr], wr=[dst[j][1]])
                    g1 = [(tA, tA_r), (tB, tB_r)]
                    g2 = [(tC, tC_r), (tD, tD_r)]
                    linear(W["win"], 0, NKD, O_G1 + 128 * cbase, 256, xrhs, lambda j, pt, pr, m: ev_g(j, pt, pr, m, g1), N)

                    def ev_a(j, pt, pr, m):
                        k.tt(g1[j][0][:, 0:N], pt[:, 0:N], g1[j][0][:, 0:N], ALU.mult, rd=[pr, g1[j][1]], wr=[g1[j][1]])
                    linear(W["wbs"], 0, 8, 128 * cbase, 256, lambda kc: (ys5[:, kc, 0:N], ys5_r[kc]), ev_a, N)
                    linear(W["win"], 0, NKD, O_G2 + 128 * cbase, 256, xrhs, lambda j, pt, pr, m: ev_g(j, pt, pr, m, g2), N)

                    def ev_b(j, pt, pr, m):
                        k.tt(g2[j][0][:, 0:N], pt[:, 0:N], g2[j][0][:, 0:N], ALU.mult, rd=[pr, g2[j][1]], wr=[g2[j][1]])
                        k.tt(mrg[:, 2 * blk + j, 0:N], g1[j][0][:, 0:N], g2[j][0][:, 0:N], ALU.add,
                             rd=[g1[j][1], g2[j][1]], wr=[mrg_r[2 * blk + j]])
                    linear(W["wbm"], 0, 8, 128 * cbase, 256, lambda kc: (yml[:, kc, 0:N], yml_r[kc]), ev_b, N)

                def ev_out(j, pt, pr, m):
                    k.tt(h[:, j, 0:N], pt[:, 0:N], h[:, j, 0:N], ALU.add, rd=[pr, h_r[j]], wr=[h_r[j]])
                linear(W["wout"], 1024 * half, 8, 0, D, lambda kc: (mrg[:, kc, 0:N], mrg_r[kc]), ev_out, N)
            if debug:
                if pc > 0:
                    k.dma("sp", dbg["d_h2"][:, p0:p0 + pc].rearrange("(c p) t -> p c t", p=128), h[:, :, 0:pc], rd=h_r)
                if scn > 0:
                    k.dma("sp", dbg["d_h2"][:, scol0:scol0 + scn].rearrange("(c p) t -> p c t", p=128), h[:, :, pc:N], rd=h_r)

            k.barrier()
            rms_stats(N, lambda c: (h[:, c, 0:N], h_r[c]))
            apply_norm(N, "g2")
            ffn(N, W["f2g"], W["f2u"], W["f2d"])
            rms_stats(N, lambda c: (h[:, c, 0:N], h_r[c]))
            for c in range(NKD):
                yo, yo_r = (tC, tC_r) if c % 2 == 0 else (tD, tD_r)
                k.stt(yo[:, 0:N], h[:, c, 0:N], C("gf", c, c + 1), rstd[:, 0:N], ALU.mult, ALU.mult,
                      rd=[h_r[c], cf_r, rstd_r], wr=[yo_r])
                if pc > 0:
                    k.dma("sp", yT[128 * c:128 * c + 128, p0 - npre:p0 - npre + pc], yo[:, 0:pc], rd=[yo_r])
                if scn > 0:
                    k.dma("sp", yT[128 * c:128 * c + 128, pl - npre:pl - npre + scn], yo[:, pc:N], rd=[yo_r])

        k.tt(tA[:, 0:32], injr[:], ilam[:], ALU.mult, rd=inj_r + [trig_r], wr=[tA_r])
        k.tt(tA[:, 32:64], inji[:], ilam[:], ALU.mult, rd=inj_r + [trig_r], wr=[tA_r])
        k.dma("sp", o_ps5, tA[:, 0:64], rd=[tA_r])
        k.dma("sp", o_pC, Cp[:].rearrange("p a b c -> p (a b c)"), rd=Cp_r)
        k.dma("sp", o_pn, np_[:], rd=[np_r])
        k.dma("sp", o_pm, m0p[:], rd=[m0p_r])
        k.dma("sp", o_pconv, hist[:].rearrange("p a b -> p (a b)"), rd=hist_r)
        k.dma("sp", o_sn, nS[:].rearrange("p a b -> p (a b)"), rd=[nS_r])
        k.dma("sp", o_sm, mS[:], rd=[mS_r])
        for key in k.misc["sp"] + k.misc["pool"]:
            if k.cnt[key] > 0:
                k._wait("sp", (key, k.cnt[key]))
        k.barrier()
    return nc, plan


def _consts_common(p):
    cf = np.zeros((128, NCF), np.float32)

    def put(name, arr, p0=0):
        o, w = CF_OFF[name]
        arr = np.asarray(arr, np.float32)
        cf[p0:p0 + arr.shape[0], o:o + arr.shape[1]] = arr
    put("ident", np.eye(128))
    s_ = np.arange(128)[:, None]; l_ = np.arange(128)[None, :]
    put("maskP", (s_ <= l_))
    put("maskS", (s_ <= l_) & ((s_ // LS) == (l_ // LS)))
    for nm, key in (("g1", "ffn1_norm"), ("gm", "mix_norm"), ("g2", "ffn2_norm")):
        put(nm, p[key][0].reshape(NKD, 128).T)
    put("gf", p["final_norm"].reshape(NKD, 128).T)
    put("s5D", p["s5_D"][0].reshape(8, 128).T)
    cw = np.zeros((128, 40), np.float32)
    for idx in range(8):
        ch = slice(128 * idx, 128 * idx + 128)
        for jj in range(4):
            cw[:, 5 * idx + jj] = p["mlstm_conv_w"][0][jj, ch]
        cw[:, 5 * idx + 4] = p["mlstm_conv_b"][0][ch]
    put("convw", cw)
    put("mlg", p["mlstm_norm"][0].reshape(8, 128).T)
    sel = np.zeros((4, 512), np.float32)
    for hd in range(4):
        sel[hd, 128 * hd:128 * hd + 128] = 1.0
    put("sel", sel)
    put("bi", p["mlstm_b_i"][0].reshape(4, 1))
    put("bf", p["mlstm_b_f"][0].reshape(4, 1))
    rmP = np.ones((4, 128), np.float32); rmP[:, 0] = 0
    rmS = np.ones((4, 128), np.float32); rmS[:, ::LS] = 0
    raP = np.zeros((4, 128), np.float32); raP[:, 0] = NEGBIG
    raS = np.zeros((4, 128), np.float32); raS[:, ::LS] = NEGBIG
    put("rmP", rmP); put("rmS", rmS); put("raP", raP); put("raS", raS)
    put("tau1", np.tile(np.arange(1, LT + 1, dtype=np.float32)[None], (128, 1)))
    m01 = np.ones((128, LT), np.float32); m01[:, 0] = 0
    put("m01", m01)
    def L1(a):
        return a.reshape(32, 2, 64).transpose(1, 2, 0).reshape(128, 32)
    put("Are", L1(p["s5_A_re"][0])); put("Aim", L1(p["s5_A_im"][0]))
    put("ldt", L1(np.repeat(p["s5_log_dt"][0][:, None], 64, axis=1)))
    ctabs = {}
    for nm, key in (("Cr", "s5_C_re"), ("Ci", "s5_C_im")):
        Cg = p[key][0]
        t = np.zeros((2, 64, 32, 2, 16), np.float32)
        for gi in range(2):
            t[gi, :, :, gi, :] = Cg[gi::2].transpose(2, 0, 1)
        ctabs[nm] = t.reshape(128, 1024)
    sc = np.zeros((128, NSC), np.float32)

    def LB_bcast(a):
        t = a.reshape(8, 4, 2, 64)
        t = t.transpose(1, 0, 2, 3)
        t = np.broadcast_to(t[:, None, None], (4, 2, 16, 8, 2, 64))
        return t.reshape(128, 1024)

    def LB_B(B):
        t = np.zeros((4, 2, 16, 8, 2, 64), np.float32)
        Bq = B.reshape(8, 4, 2, 64, 16)
        for gi in range(2):
            t[:, gi, :, :, gi, :] = Bq[:, :, gi].transpose(1, 3, 0, 2)
        return t.reshape(128, 1024)
    for nm, arr in (("AreB", LB_bcast(p["s5_A_re"][0])), ("AimB", LB_bcast(p["s5_A_im"][0])),
                    ("ldtB", LB_bcast(np.repeat(p["s5_log_dt"][0][:, None], 64, axis=1))),
                    ("BreB", LB_B(p["s5_B_re"][0])), ("BimB", LB_B(p["s5_B_im"][0]))):
        o, w = SC_OFF[nm]
        sc[:, o:o + w] = arr
    cr = np.zeros((128, NCR), np.float32)
    o, w = CR_OFF["ones"]; cr[:, o:o + w] = 1.0
    o, w = CR_OFF["seqm"]
    cr[:, o:o + w] = (np.arange(128)[:, None] // LS == np.arange(NSEQ)[None, :])
    o, w = CR_OFF["identb"]; cr[:, o:o + w] = np.eye(128, dtype=np.float32)
    for nm in ("Cr", "Ci"):
        o, w = CR_OFF[nm]; cr[:, o:o + w] = ctabs[nm]
    return cf, cr, sc


def _core_consts(cf_common, st, c, flag=1.0):
    cf = cf_common.copy()
    o_, w_ = CF_OFF["flag"]
    cf[:, o_:o_ + w_] = flag
    sl = slice(NSEQ * c, NSEQ * c + NSEQ)

    def put(name, arr, p0=0):
        o, w = CF_OFF[name]
        arr = np.asarray(arr, np.float32)
        cf[p0:p0 + arr.shape[0], o:o + arr.shape[1]] = arr
    put("m0S", st["state_mlstm_m"][0][sl].T)
    for nm, key in (("h0r", "state_s5_re"), ("h0i", "state_s5_im")):
        a = st[key][0][sl]
        t = a.reshape(NSEQ, 32, 2, 64).transpose(2, 3, 1, 0)
        put(nm, t.reshape(128, 512))
    n0 = st["state_mlstm_n"][0][sl]
    put("n0S", n0.transpose(2, 1, 0).reshape(128, 64))
    cv = st["state_mlstm_conv"][0][sl]
    t = cv.reshape(NSEQ, 3, 8, 128).transpose(3, 2, 0, 1)
    put("histS", t.reshape(128, 384))
    return cf


_PROG = {}
SPLIT = PLEN // 2
TILEW = 384


def kernel(**inputs):
    inp = {k_: np.asarray(v) for k_, v in inputs.items()}
    ncores = 8
    if "prog" not in _PROG:
        _PROG["prog"] = build_program(PLEN, debug=False, split=SPLIT, tw=TILEW)
    nc, plan = _PROG["prog"]
    cf_common, cr, sc = _consts_common(inp)
    wmap = {"f1g": inp["ffn1_w_gate"][0], "f1u": inp["ffn1_w_up"][0], "f1d": inp["ffn1_w_down"][0],
            "win": inp["w_in"][0], "wglu": inp["s5_w_glu"][0], "wbs": inp["w_branch_s5"][0],
            "wbm": inp["w_branch_mlstm"][0], "wout": inp["w_out"][0],
            "f2g": inp["ffn2_w_gate"][0], "f2u": inp["ffn2_w_up"][0], "f2d": inp["ffn2_w_down"][0]}
    wmap = {k_: np.ascontiguousarray(v, dtype=np.float32) for k_, v in wmap.items()}
    in_maps = []
    for c in range(ncores):
        b, half = c // 2, c % 2
        sl = slice(NSEQ * c, NSEQ * c + NSEQ)
        hpT = np.concatenate([inp["meta_tokens"].T, inp["x_prompt"][b].T], axis=1).astype(np.float32)
        xT = np.zeros((D, PLEN + SC), np.float32)
        if half == 0:
            xT[:, SPLIT:PLEN] = hpT[:, 0:SPLIT]
        else:
            xT[:, 0:PLEN] = hpT
        xT[:, PLEN:] = inp["x_sample"][sl].reshape(SC, D).T
        m = dict(wmap)
        m["xT"] = xT
        m["cf"] = _core_consts(cf_common, inp, c, flag=float(half))
        m["cr"] = cr
        m["sc"] = sc
        m["sC"] = np.ascontiguousarray(inp["state_mlstm_C"][0][sl], dtype=np.float32)
        in_maps.append(m)
    res = run_bass_kernel_spmd(nc, in_maps, core_ids=list(range(ncores)))
    return assemble_split(res.results)


def assemble_split(R):
    ncores = len(R)
    nm = PLEN - SPLIT
    y_prompt = np.stack([np.concatenate([R[2 * b]["yT"][:, NMETA:nm].T, R[2 * b + 1]["yT"][:, 0:nm].T], axis=0)
                         for b in range(4)])
    y_sample = np.concatenate([R[c]["yT"][:, nm:].T.reshape(NSEQ, LS, D) for c in range(ncores)])
    P = [R[2 * b + 1] for b in range(4)]
    rest = _assemble_states(P, R)
    outs = (y_prompt, y_sample) + rest
    return tuple(np.ascontiguousarray(o, dtype=np.float32) for o in outs)


def _assemble_states(P, R):
    ncores = len(R)

    def s5p(r, o):
        return r["o_ps5"][:, o:o + 32].reshape(2, 64, 32).transpose(2, 0, 1).reshape(64, 64)
    p_re = np.stack([s5p(r, 0) for r in P])[None]
    p_im = np.stack([s5p(r, 32) for r in P])[None]
    p_C = np.stack([r["o_pC"].reshape(128, NH, 2, 128).transpose(1, 2, 0, 3).reshape(NH, DV, DQK) for r in P])[None]
    p_n = np.stack([r["o_pn"].T for r in P])[None]
    p_m = np.stack([r["o_pm"][:, 0] for r in P])[None]
    p_cv = np.stack([r["o_pconv"].reshape(128, 8, 3).transpose(2, 1, 0).reshape(3, QKW) for r in P])[None]

    def s5s(c, o):
        t = R[c]["o_ss5"][:, o:o + 512].reshape(2, 64, 32, NSEQ)
        return t.transpose(3, 2, 0, 1).reshape(NSEQ, 64, 64)
    s_re = np.concatenate([s5s(c, 0) for c in range(ncores)])[None]
    s_im = np.concatenate([s5s(c, 512) for c in range(ncores)])[None]
    s_C = np.concatenate([R[c]["o_sC"] for c in range(ncores)])[None]
    s_n = np.concatenate([R[c]["o_sn"].reshape(128, NH, NSEQ).transpose(2, 1, 0) for c in range(ncores)])[None]
    s_m = np.concatenate([R[c]["o_sm"].T for c in range(ncores)])[None]
    s_cv = np.concatenate([R[c]["o_sconv"].reshape(128, 8, NSEQ, 3).transpose(2, 3, 1, 0).reshape(NSEQ, 3, QKW)
                           for c in range(ncores)])[None]
    return (p_re, p_im, p_C, p_n, p_m, p_cv, s_re, s_im, s_C, s_n, s_m, s_cv)


def assemble(R, nprompt=4, pl=PLEN):
    ncores = len(R)
    y_prompt = np.stack([R[c]["yT"][:, NMETA:pl].T for c in range(nprompt)])
    y_sample = np.concatenate([R[c]["yT"][:, pl:].T.reshape(NSEQ, LS, D) for c in range(ncores)])
    rest = _assemble_states([R[c] for c in range(nprompt)], R)
    outs = (y_prompt, y_sample) + rest
    return tuple(np.ascontiguousarray(o, dtype=np.float32) for o in outs)
```

```python
import math
import numpy as np
import concourse.bass as bass
import concourse.mybir as mybir
from concourse.bass_utils import run_bass_kernel_spmd
from contextlib import ExitStack

F32 = mybir.dt.float32
F32R = mybir.dt.float32r
BF16 = mybir.dt.bfloat16
MMT = BF16
AF = mybir.ActivationFunctionType
ALU = mybir.AluOpType

D = 2048
DFF = 5632
NKD = D // 128
S5W = 1024
QKW = 1024
VW = 1024
NH = 4
DQK = 128
DV = 256
INW = 8200
O_S5, O_QK, O_V, O_O, O_I, O_F, O_G1, O_G2 = 0, 1024, 2048, 3072, 4096, 4100, 4104, 6152
EPS = 1e-6
NMETA = 16
SEQ = 2048
PLEN = NMETA + SEQ
NSEQ = 16
LS = 8
SC = NSEQ * LS
NTM = 392
LT = 64
NEGBIG = -1.0e30
TWO_PI = 2.0 * math.pi
MAGIC = 12582912.0
STRICT_SAME_ENGINE = True


def make_plan(pl):
    tiles = []
    p0 = 0
    rem = pl
    while rem > 0:
        pc = min(NTM, rem)
        tiles.append([pc, p0, 0])
        p0 += pc
        rem -= pc
    if tiles and tiles[-1][0] + SC <= NTM:
        tiles[-1][2] = SC
    else:
        tiles.append([0, p0, SC])
    return [tuple(t) for t in tiles]


def _layout(items):
    off = {}
    o = 0
    for n, w in items:
        off[n] = (o, w)
        o += w
    return off, o


CF_ITEMS = [
    ("ident", 128), ("maskP", 128), ("maskS", 128),
    ("g1", 16), ("gm", 16), ("g2", 16), ("gf", 16), ("s5D", 8), ("convw", 40), ("mlg", 8),
    ("sel", 512), ("bi", 1), ("bf", 1), ("flag", 1),
    ("rmP", 128), ("rmS", 128), ("raP", 128), ("raS", 128), ("m0S", 16),
    ("tau1", LT), ("m01", LT), ("Are", 32), ("Aim", 32), ("ldt", 32),
    ("h0r", 512), ("h0i", 512), ("n0S", 64), ("histS", 384),
]
CF_OFF, NCF = _layout(CF_ITEMS)
CR_ITEMS = [("ones", 128), ("seqm", 16), ("identb", 128), ("Cr", 1024), ("Ci", 1024)]
CR_OFF, NCR = _layout(CR_ITEMS)
SC_ITEMS = [("AreB", 1024), ("AimB", 1024), ("ldtB", 1024), ("BreB", 1024), ("BimB", 1024)]
SC_OFF, NSC = _layout(SC_ITEMS)


class Res:
    __slots__ = ("w", "r")

    def __init__(self):
        self.w = None
        self.r = []


def RL(n):
    return [Res() for _ in range(n)]


class KB:
    def __init__(self, nc, es):
        self.nc = nc
        self.es = es
        self.E = {"pe": nc.tensor, "act": nc.scalar, "dve": nc.vector, "pool": nc.gpsimd, "sp": nc.sync}
        self.semobj = {}
        self.cnt = {}
        for k in self.E:
            self.semobj[k] = es.enter_context(nc.semaphore("sem_" + k))
            self.cnt[k] = 0
        self.seen = {k: {} for k in self.E}
        self.ndma = 0
        self.misc = {"sp": [], "pool": []}
        self.misc_i = {"sp": 0, "pool": 0}
        for q_, n_ in (("sp", 8), ("pool", 4)):
            for i in range(n_):
                key = "m%s%d" % (q_, i)
                self.semobj[key] = es.enter_context(nc.semaphore("sem_" + key))
                self.cnt[key] = 0
                self.misc[q_].append(key)

    def new_dma_sem(self, key):
        self.semobj[key] = self.es.enter_context(self.nc.semaphore("sem_" + key))
        self.cnt[key] = 0

    def _wait(self, eng, ev):
        if ev is None:
            return
        key, val = ev
        if self.seen[eng].get(key, 0) >= val:
            return
        self.E[eng].wait_ge(self.semobj[key], val)
        self.seen[eng][key] = val

    def _deps(self, eng, rd, wr):
        for r in rd:
            self._wait(eng, r.w)
        strict = STRICT_SAME_ENGINE and eng != "pe"
        for w in wr:
            if w.w is not None and (strict or w.w[0] != eng):
                self._wait(eng, w.w)
            for ev in w.r:
                if strict or ev[0] != eng:
                    self._wait(eng, ev)

    def op(self, eng, fn, rd=(), wr=()):
        self._deps(eng, rd, wr)
        ins = fn(self.E[eng])
        self.cnt[eng] += 1
        ins.then_inc(self.semobj[eng], 1)
        ev = (eng, self.cnt[eng])
        for r in rd:
            r.r.append(ev)
        for w in wr:
            w.w = ev
            w.r = []
        return ev

    def dma(self, eng, out, in_, rd=(), wr=(), semkey=None, **kw):
        if semkey is None:
            semkey = self.misc[eng][self.misc_i[eng] % len(self.misc[eng])]
            self.misc_i[eng] += 1
        if self.cnt[semkey] > 0:
            self._wait(eng, (semkey, self.cnt[semkey]))
        self._deps(eng, rd, wr)
        ins = self.E[eng].dma_start(out=out, in_=in_, **kw)
        self.cnt[semkey] += 16
        ins.then_inc(self.semobj[semkey], 16)
        ev = (semkey, self.cnt[semkey])
        for r in rd:
            r.r.append(ev)
        for w in wr:
            w.w = ev
            w.r = []
        self.ndma += 1
        return ev

    def barrier(self, full=False):
        evs = [(k, self.cnt[k]) for k in self.semobj if self.cnt[k] > 0]
        for eng in (("pe", "act", "dve", "pool", "sp") if full else ("pe", "act", "dve")):
            for ev in evs:
                self._wait(eng, ev)

    def mm(self, out, lhsT, rhs, start, stop, rd, wr):
        return self.op("pe", lambda e: e.matmul(out, lhsT, rhs, start=start, stop=stop), rd, wr)

    def tr(self, out, in_, ident, rd, wr):
        return self.op("pe", lambda e: e.transpose(out, in_, ident), rd, wr)

    def act(self, out, in_, func, rd, wr, **kw):
        return self.op("act", lambda e: e.activation(out, in_, func, **kw), rd, wr)

    def tt(self, out, in0, in1, op, rd, wr, eng="dve"):
        return self.op(eng, lambda e: e.tensor_tensor(out, in0, in1, op), rd, wr)

    def ts(self, out, in0, s1, s2, op0, op1, rd, wr, eng="dve"):
        if op1 is None:
            return self.op(eng, lambda e: e.tensor_scalar(out, in0, s1, None, op0), rd, wr)
        return self.op(eng, lambda e: e.tensor_scalar(out, in0, s1, s2, op0, op1), rd, wr)

    def stt(self, out, in0, sc, in1, op0, op1, rd, wr):
        return self.op("dve", lambda e: e.scalar_tensor_tensor(out, in0, sc, in1, op0, op1), rd, wr)

    def cp(self, out, in_, rd, wr, eng="dve"):
        return self.op(eng, lambda e: e.tensor_copy(out, in_), rd, wr)

    def scan(self, out, d0, d1, init, op0, op1, rd, wr):
        return self.op("dve", lambda e: e.tensor_tensor_scan(out, d0, d1, init, op0, op1), rd, wr)

    def recip(self, out, in_, rd, wr):
        return self.op("dve", lambda e: e.reciprocal(out, in_), rd, wr)

    def memset(self, ap, val, wr, eng="dve"):
        return self.op(eng, lambda e: e.memset(ap, val), (), wr)


def fr(ap):
    return ap


def make_plan_split(pl, split, tw):
    tiles = []
    p0 = 0
    while p0 < split:
        pc = min(tw, split - p0)
        tiles.append((pc, p0, 0, "pre"))
        p0 += pc
    while p0 < pl:
        pc = min(tw, pl - p0)
        tiles.append([pc, p0, 0, "main"])
        p0 += pc
    if tiles[-1][0] + SC <= NTM:
        tiles[-1][2] = SC
    else:
        tiles.append([0, pl, SC, "main"])
    return [tuple(t) for t in tiles]


def build_program(pl=PLEN, debug=False, stop=None, split=None, tw=NTM):
    lvl = {'ffn1': 0, 's5a': 1, 's5b': 2, 's5c': 3, 's5d': 4, 's5': 5, 'mlstm': 6}.get(stop, 9)
    if split is None:
        plan = [t + ("main",) for t in make_plan(pl)]
        npre = 0
    else:
        plan = make_plan_split(pl, split, tw)
        npre = split
        debug = False
    ncols = pl + SC
    nycols = pl - npre + SC
    nc = bass.Bass("TRN2", target_bir_lowering=False)
    dram = {}

    def din(name, shape):
        dram[name] = nc.dram_tensor(name, list(shape), F32, kind="ExternalInput").ap()
        return dram[name]

    def dout(name, shape):
        dram[name] = nc.dram_tensor(name, list(shape), F32, kind="ExternalOutput").ap()
        return dram[name]

    xT = din("xT", [D, ncols])
    W = {}
    for n, s in [("f1g", [D, DFF]), ("f1u", [D, DFF]), ("f1d", [DFF, D]), ("win", [D, INW]),
                 ("wglu", [S5W, S5W]), ("wbs", [S5W, D]), ("wbm", [VW, D]), ("wout", [D, D]),
                 ("f2g", [D, DFF]), ("f2u", [D, DFF]), ("f2d", [DFF, D])]:
        W[n] = din(n, s)
    cf_d = din("cf", [128, NCF])
    cr_d = din("cr", [128, NCR])
    sc_d = din("sc", [128, NSC])
    sC_d = din("sC", [NSEQ, NH, DV, DQK])
    yT = dout("yT", [D, nycols])
    o_ps5 = dout("o_ps5", [128, 64])
    o_pC = dout("o_pC", [128, NH * 2 * 128])
    o_pn = dout("o_pn", [128, NH])
    o_pm = dout("o_pm", [4, 1])
    o_pconv = dout("o_pconv", [128, 24])
    o_ss5 = dout("o_ss5", [128, 1024])
    o_sC = dout("o_sC", [NSEQ, NH, DV, DQK])
    o_sn = dout("o_sn", [128, 64])
    o_sm = dout("o_sm", [4, 16])
    o_sconv = dout("o_sconv", [128, 384])
    dbg = {}
    if debug:
        for n in ("d_h1", "d_ys5", "d_yml", "d_h2"):
            dbg[n] = dout(n, [D if n in ("d_h1", "d_h2") else 1024, ncols])

    es = ExitStack()
    with es:
        k = KB(nc, es)

        def sb(name, shape, dt=F32):
            return es.enter_context(nc.sbuf_tensor("s_" + name, list(shape), dt))

        cf = sb("cf", [128, NCF]); cf_r = Res()
        crt = sb("crt", [128, NCR], MMT); cr_r = Res()
        h = sb("h", [128, NKD, NTM]); h_r = RL(NKD)
        xn = sb("xn", [128, NKD, NTM], MMT); xn_r = RL(NKD)
        rstd = sb("rstd", [128, NTM]); rstd_r = Res()
        tA = sb("tA", [128, NTM]); tA_r = Res()
        tB = sb("tB", [128, NTM]); tB_r = Res()
        tC = sb("tC", [128, NTM]); tC_r = Res()
        tD = sb("tD", [128, NTM]); tD_r = Res()
        regX = sb("regX", [128, 8 * NTM], MMT)
        regF = sb("regF", [128, 12 * NTM + 8])
        mid = regX[:].rearrange("p (a b n) -> p a b n", a=2, b=4); mid_r = [RL(4), RL(4)]
        ys5 = sb("ys5", [128, 8, NTM], MMT); ys5_r = RL(8)
        yml = sb("yml", [128, 8, NTM], MMT); yml_r = RL(8)
        sq = yml[:, 0:2, :]; sq_r = RL(2)
        mrg = sb("mrg", [128, 8, NTM], MMT); mrg_r = RL(8)
        NRING = 8
        ring = [sb("ring%d" % i, [128, 2048], MMT) for i in range(NRING)]
        ring_r = RL(NRING)
        for i in range(NRING):
            k.new_dma_sem("rg%d" % i)
        ring_i = [0]
        psum = [es.enter_context(nc.psum_tensor("ps%d" % i, [128, 512], F32)) for i in range(8)]
        psum_r = RL(8)
        ps_i = [0]
        ct = sb("ct", [128, 32, LT]); st = sb("st", [128, 32, LT]); trig_r = Res()
        am16 = sb("am16", [128, 32, LT]); am8 = sb("am8", [128, 32, LS])
        lam = sb("lam", [128, 32]); ilam = sb("ilam", [128, 32])
        Rpr = sb("Rpr", [128, 32]); Rpi = sb("Rpi", [128, 32])
        Rp8r = sb("Rp8r", [128, 32]); Rp8i = sb("Rp8i", [128, 32])
        Btab = sb("Btab", [128, 8, 2, 128], MMT); btab_r = Res()
        injr = sb("injr", [128, 32]); inji = sb("inji", [128, 32]); inj_r = RL(8)
        u8 = yml[:, 4:6, :]; u8_r = RL(2)
        twr = regF[:, 0:4 * NTM]; twi = regF[:, 4 * NTM:8 * NTM]; tw_r = Res()
        xs = sb("xs", [128, 4, NTM], MMT); xs_r = RL(4)
        u3 = yml[:, 6:8, :]
        Bext = sb("Bext", [128, 3, 2, 128], MMT)
        ysp = mrg; ysp_r = mrg_r
        sS_r = Res()
        bt = sb("bt", [128, 8, 4]); bt_r = Res()
        Cp = sb("Cp", [128, NH, 2, 128]); Cp_r = RL(NH)
        np_ = sb("np", [128, NH]); np_r = Res()
        m0p = sb("m0p", [4, 1]); m0p_r = Res()
        hist = sb("hist", [128, 8, 3]); hist_r = RL(8)
        cbuf = regF[:, 8 * NTM:8 * NTM + 2 * (NTM + 3)].rearrange("p (a n) -> p a n", a=2); cbuf_r = RL(2)
        cbs = sb("cbs", [128, 2, NSEQ, LS + 3]); cbs_r = RL(2)
        cacc = regF[:, 8 * NTM + 2 * (NTM + 3):8 * NTM + 2 * (NTM + 3) + 2 * NTM].rearrange("p (a n) -> p a n", a=2); cacc_r = RL(2)
        qT = sb("qT", [128, NTM], MMT); qT_r = Res()
        kT = sb("kT", [128, NTM], MMT); kT_r = Res()
        vtok = sb("vtok", [128, 3, DV], MMT); vtok_r = RL(3)
        sigo = sb("sigo", [128, 2, NTM]); sigo_r = RL(2)
        rows = regF[0:4, 0:8 * NTM].rearrange("p (r n) -> p r n", r=8); rows_r = Res()
        RB = sb("RB", [4, 3, 128]); RB_r = Res()
        aT = sb("aT", [128, 4]); wT = sb("wT", [128, 4]); awT_r = Res()
        Et = sb("Et", [128, 128]); Et_r = Res()
        Wt = sb("Wt", [128, 128]); Wt_r = Res()
        SW = sb("SW", [128, 128], MMT); SW_r = Res()
        qs = sb("qs", [128, 128], MMT); qs_r = Res()
        nq = sb("nq", [128, 128], MMT); nq_r = Res()
        CT = sb("CT", [128, 2, DV], MMT); CT_r = RL(2)
        Cs = sb("Cs", [128, NSEQ, 2, 128]); Cs_r = RL(NSEQ)
        kw = sb("kw", [128, 128], MMT); kw_r = Res()
        kwm = sb("kwm", [128, 2, 128], MMT); kwm_r = RL(2)
        dab = sb("dab", [128, 128]); dab_r = Res()
        rec = sb("rec", [128, 128]); rec_r = Res()
        hT = sb("hT", [128, 2, 128]); hT_r = RL(2)
        hsq = sb("hsq", [128, 2, 128], MMT); hsq_r = RL(2)
        rsh = sb("rsh", [128, 128]); rsh_r = Res()
        nS = sb("nS", [128, NH, NSEQ]); nS_r = Res()
        mS = sb("mS", [4, NSEQ]); mS_r = Res()

        def C(name, lo=0, hi=None, p0=0, p1=128):
            o, w = CF_OFF[name]
            if hi is None:
                hi = w
            return cf[p0:p1, o + lo:o + hi]

        def CR(name, lo=0, hi=None, p0=0, p1=128):
            o, w = CR_OFF[name]
            if hi is None:
                hi = w
            return crt[p0:p1, o + lo:o + hi]

        ident = C("ident")
        ones_r = CR("ones")
        identb = CR("identb")

        def ps():
            i = ps_i[0] % 4
            ps_i[0] += 1
            return psum[i], psum_r[i]

        def psL(i):
            return psum[4 + i], psum_r[4 + i]

        def wtile(Wap, row0, kt, col0, ncol):
            i = ring_i[0] % NRING
            ring_i[0] += 1
            view = ring[i][:, 0:kt * ncol].rearrange("p (k c) -> p k c", k=kt)
            src = Wap[row0:row0 + 128 * kt, col0:col0 + ncol].rearrange("(k p) c -> p k c", p=128)
            k.dma("pool", view, src, rd=(), wr=[ring_r[i]], semkey="rg%d" % i)
            return view, ring_r[i]

        def linear(Wap, row0, nkc, col0, ncols_, rhs_fn, evac_fn, N):
            KT = min(nkc, 8)
            nkh = nkc // KT
            BC = 2048 // KT
            c = 0
            j = 0
            while c < ncols_:
                bc = min(BC, ncols_ - c)
                tiles_ = [wtile(Wap, row0 + 128 * KT * hh, KT, col0 + c, bc) for hh in range(nkh)]
                subs = []
                cc = 0
                while cc < bc:
                    m = min(128, bc - cc)
                    subs.append((cc, m, ps()))
                    cc += m
                for hh in range(nkh):
                    tv, tr_ = tiles_[hh]
                    for (cc, m, (pt, pr)) in subs:
                        for kk in range(KT):
                            kc = hh * KT + kk
                            rap, rres = rhs_fn(kc)
                            k.mm(pt[0:m, 0:N], tv[:, kk, cc:cc + m], rap, kc == 0, kc == nkc - 1,
                                 rd=[tr_, rres], wr=[pr])
                for (cc, m, (pt, pr)) in subs:
                    evac_fn(j, pt, pr, m)
                    j += 1
                c += bc

        k.dma("sp", cf[:], cf_d, wr=[cf_r])
        for c0_ in range(0, NCR, 1024):
            c1_ = min(NCR, c0_ + 1024)
            k.dma("pool", crt[:, c0_:c1_], cr_d[:, c0_:c1_], wr=[cr_r])
        if True:
            sct = h[:].rearrange("p a b -> p (a b)")
            assert NSC <= NKD * NTM
            sct_r = Res(); sw_r = Res()
            k.dma("sp", sct[:, 0:NSC], sc_d, wr=[sct_r])
            _pc = [tA[:, 0:128], tA[:, 128:256], tB[:, 0:128], tB[:, 128:256], tC[:, 0:128], tC[:, 128:256],
                   tD[:, 0:128], tD[:, 128:256], rstd[:, 0:128], rstd[:, 128:256]]

            def sincos(theta, s_out, c_out, t0, t1, rdl, wrl):
                for (shift, dst) in ((0.0, s_out), (0.5 * math.pi, c_out)):
                    k.ts(t0, theta, 1.0 / TWO_PI, shift / TWO_PI, ALU.mult, ALU.add, rd=rdl, wr=wrl)
                    k.ts(t0, t0, MAGIC, None, ALU.add, None, rd=wrl, wr=wrl)
                    k.ts(t0, t0, -MAGIC, None, ALU.add, None, rd=wrl, wr=wrl)
                    k.stt(t1, t0, -TWO_PI, theta, ALU.mult, ALU.add, rd=rdl + wrl, wr=wrl)
                    k.ts(t1, t1, shift, None, ALU.add, None, rd=wrl, wr=wrl)
                    k.ts(t1, t1, math.pi, -math.pi, ALU.min, ALU.max, rd=wrl, wr=wrl)
                    k.act(dst, t1, AF.Sin, rd=wrl, wr=wrl)

            rs = [sct_r, sw_r]
            ws = [sw_r]
            for c8_ in range(8):
                def S(name, c8_=c8_):
                    o, w = SC_OFF[name]
                    return sct[:, o + 128 * c8_:o + 128 * c8_ + 128]
                X = lambda i: _pc[i]
                k.act(X(0), S("ldtB"), AF.Exp, rd=rs, wr=ws)
                k.tt(X(1), S("AimB"), X(0), ALU.mult, rd=rs, wr=ws)
                k.tt(X(2), S("AreB"), X(0), ALU.mult, rd=rs, wr=ws)
                k.act(X(2), X(2), AF.Exp, rd=rs, wr=ws)
                sincos(X(1), X(3), X(4), X(5), X(6), rs, ws)
                k.tt(X(3), X(3), X(2), ALU.mult, rd=rs, wr=ws)
                k.tt(X(4), X(4), X(2), ALU.mult, rd=rs, wr=ws)
                k.ts(X(4), X(4), -1.0, None, ALU.add, None, rd=rs, wr=ws)
                k.tt(X(5), S("AreB"), S("AreB"), ALU.mult, rd=rs, wr=ws)
                k.tt(X(6), S("AimB"), S("AimB"), ALU.mult, rd=rs, wr=ws)
                k.tt(X(5), X(5), X(6), ALU.add, rd=rs, wr=ws)
                k.recip(X(5), X(5), rd=rs, wr=ws)
                k.tt(X(6), X(4), S("AreB"), ALU.mult, rd=rs, wr=ws)
                k.tt(X(7), X(3), S("AimB"), ALU.mult, rd=rs, wr=ws)
                k.tt(X(6), X(6), X(7), ALU.add, rd=rs, wr=ws)
                k.tt(X(6), X(6), X(5), ALU.mult, rd=rs, wr=ws)
                k.tt(X(7), X(3), S("AreB"), ALU.mult, rd=rs, wr=ws)
                k.tt(X(8), X(4), S("AimB"), ALU.mult, rd=rs, wr=ws)
                k.tt(X(7), X(7), X(8), ALU.subtract, rd=rs, wr=ws)
                k.tt(X(7), X(7), X(5), ALU.mult, rd=rs, wr=ws)
                k.tt(X(8), X(6), S("BreB"), ALU.mult, rd=rs, wr=ws)
                k.tt(X(9), X(7), S("BimB"), ALU.mult, rd=rs, wr=ws)
                k.tt(Btab[:, c8_, 0, :], X(8), X(9), ALU.subtract, rd=rs, wr=[btab_r])
                k.tt(X(8), X(6), S("BimB"), ALU.mult, rd=rs, wr=ws)
                k.tt(X(9), X(7), S("BreB"), ALU.mult, rd=rs, wr=ws)
                k.tt(Btab[:, c8_, 1, :], X(8), X(9), ALU.add, rd=rs + [btab_r], wr=[btab_r])
                s3 = c8_ % 3
                k.cp(Bext[32 * s3:32 * s3 + 32, c8_ // 3, :, :], fr(Btab[96:128, c8_, :, :]), rd=[btab_r], wr=[btab_r])
            rs = [cf_r, sw_r, trig_r]
            ws = [sw_r, trig_r]
            dtL = tD[:, 0:32]; thL = tD[:, 32:64]; adL = tD[:, 64:96]
            k.act(dtL, C("ldt"), AF.Exp, rd=rs, wr=ws)
            k.tt(thL, C("Aim"), dtL, ALU.mult, rd=rs, wr=ws)
            k.tt(adL, C("Are"), dtL, ALU.mult, rd=rs, wr=ws)
            k.act(lam[:], adL, AF.Exp, rd=rs, wr=ws)
            k.act(ilam[:], adL, AF.Exp, rd=rs, wr=ws, scale=-1.0)
            for hq in range(8):
                qs_ = slice(4 * hq, 4 * hq + 4)
                ang = tA[:, 0:4 * LT]
                k.tt(ang.rearrange("p (q t) -> p q t", q=4), thL[:, qs_].unsqueeze(2).broadcast_to([128, 4, LT]),
                     C("tau1").unsqueeze(1).broadcast_to([128, 4, LT]), ALU.mult, rd=rs, wr=ws)
                sincos(ang, st[:, qs_, :].rearrange("p q t -> p (q t)"), ct[:, qs_, :].rearrange("p q t -> p (q t)"),
                       tB[:, 0:4 * LT], tC[:, 0:4 * LT], rs, ws)
            k.tt(am16[:], lam[:].unsqueeze(2).broadcast_to([128, 32, LT]),
                 C("m01").unsqueeze(1).broadcast_to([128, 32, LT]), ALU.mult, rd=rs, wr=ws)
            k.cp(am8[:], am16[:, :, 0:LS], rd=rs, wr=ws)
            k.tt(Rpr[:], lam[:], ct[:, :, LT - 1], ALU.mult, rd=rs, wr=ws)
            k.tt(Rpi[:], lam[:], st[:, :, LT - 1], ALU.mult, rd=rs, wr=ws)
            k.tt(Rp8r[:], lam[:], ct[:, :, LS - 1], ALU.mult, rd=rs, wr=ws)
            k.tt(Rp8i[:], lam[:], st[:, :, LS - 1], ALU.mult, rd=rs, wr=ws)
            k.memset(injr[:], 0.0, wr=inj_r)
            k.memset(inji[:], 0.0, wr=inj_r)
            k.memset(Cp[:].rearrange("p a b c -> p (a b c)"), 0.0, wr=Cp_r)
            k.memset(np_[:], 0.0, wr=[np_r])
            k.memset(m0p[:], 0.0, wr=[m0p_r])
            k.memset(hist[:].rearrange("p a b -> p (a b)"), 0.0, wr=hist_r)
            k.barrier(full=True)
        k.barrier(full=True)

        def rms_stats(N, src_fn):
            pt, pr = ps()
            for c in range(NKD):
                sap, sres = src_fn(c)
                k.act(sq[:, c % 2, 0:N], sap, AF.Square, rd=[sres], wr=[sq_r[c % 2]])
                k.mm(pt[:, 0:N], ones_r, sq[:, c % 2, 0:N], c == 0, c == NKD - 1, rd=[cr_r, sq_r[c % 2]], wr=[pr])
            k.ts(tA[:, 0:N], pt[:, 0:N], 1.0 / D, EPS, ALU.mult, ALU.add, rd=[pr], wr=[tA_r])
            k.act(tA[:, 0:N], tA[:, 0:N], AF.Sqrt, rd=[tA_r], wr=[tA_r])
            k.recip(rstd[:, 0:N], tA[:, 0:N], rd=[tA_r], wr=[rstd_r])

        def apply_norm(N, gname):
            for c in range(NKD):
                k.stt(xn[:, c, 0:N], h[:, c, 0:N], C(gname, c, c + 1), rstd[:, 0:N], ALU.mult, ALU.mult,
                      rd=[h_r[c], cf_r, rstd_r], wr=[xn_r[c]])

        def ffn(N, wg, wu, wd):
            xrhs = lambda kc: (xn[:, kc, 0:N], xn_r[kc])
            for p in range(DFF // 512):
                mb = p % 2
                gs = sb_tmp_g

                def ev_gate(j, pt, pr, m, gs=gs):
                    k.act(gs[j][0][:, 0:N], pt[:, 0:N], AF.Silu, rd=[pr], wr=[gs[j][1]])

                def ev_up(j, pt, pr, m, gs=gs, mb=mb):
                    k.tt(mid[:, mb, j, 0:N], pt[:, 0:N], gs[j][0][:, 0:N], ALU.mult, rd=[pr, gs[j][1]],
                         wr=[mid_r[mb][j]])
                for half in range(2):
                    linear(wg, 0, NKD, 512 * p + 256 * half, 256, xrhs,
                           lambda j, pt, pr, m, half=half: ev_gate(2 * half + j, pt, pr, m), N)
                    linear(wu, 0, NKD, 512 * p + 256 * half, 256, xrhs,
                           lambda j, pt, pr, m, half=half: ev_up(2 * half + j, pt, pr, m), N)

                def ev_down(j, pt, pr, m):
                    k.stt(h[:, j, 0:N], pt[:, 0:N], 0.5, h[:, j, 0:N], ALU.mult, ALU.add, rd=[pr, h_r[j]], wr=[h_r[j]])
                linear(wd, 512 * p, 4, 0, D, lambda kc, mb=mb: (mid[:, mb, kc, 0:N], mid_r[mb][kc]), ev_down, N)

        sb_tmp_g = [(tA, tA_r), (tB, tB_r), (tC, tC_r), (tD, tD_r)]

        col0 = 0
        scol0 = pl
        for ti, (pc, p0, scn, mode) in enumerate(plan):
            N = pc + scn
            state_only = (mode == "pre")
            if pc > 0:
                k.dma("sp", h[:, :, 0:pc], xT[:, p0:p0 + pc].rearrange("(c p) t -> p c t", p=128), wr=h_r)
            if scn > 0:
                k.dma("sp", h[:, :, pc:pc + scn], xT[:, scol0:scol0 + scn].rearrange("(c p) t -> p c t", p=128), wr=h_r)
            rms_stats(N, lambda c: (h[:, c, 0:N], h_r[c]))
            apply_norm(N, "g1")
            ffn(N, W["f1g"], W["f1u"], W["f1d"])
            if debug:
                if pc > 0:
                    k.dma("sp", dbg["d_h1"][:, p0:p0 + pc].rearrange("(c p) t -> p c t", p=128), h[:, :, 0:pc], rd=h_r)
                if scn > 0:
                    k.dma("sp", dbg["d_h1"][:, scol0:scol0 + scn].rearrange("(c p) t -> p c t", p=128), h[:, :, pc:N], rd=h_r)
            if stop == "ffn1":
                continue
            rms_stats(N, lambda c: (h[:, c, 0:N], h_r[c]))
            apply_norm(N, "gm")
            xrhs = lambda kc: (xn[:, kc, 0:N], xn_r[kc])

            k.barrier()
            segs = []
            if pc > 0:
                pmain = (pc // LT) * LT
                if pmain > 0:
                    segs.append(("P", 0, pmain, pmain // LT, LT))
                if pc > pmain:
                    assert (pc - pmain) % LS == 0
                    segs.append(("P", pmain, pc - pmain, (pc - pmain) // LS, LS))
            if scn > 0:
                segs.append(("S", pc, scn, NSEQ, LS))
            for c8 in range(8):
                ub = c8 % 2
                holder = {}

                s3 = c8 % 3

                def ev_u(j, pt, pr, m, ub=ub, s3=s3):
                    k.cp(u8[:, ub, 0:N], pt[:, 0:N], rd=[pr], wr=[u8_r[ub]])
                    k.cp(u3[32 * s3:32 * s3 + 32, ub, 0:N], pt[96:128, 0:N], rd=[pr], wr=[u8_r[ub]])
                    if not state_only:
                        k.cp(sigo[:, ub, 0:N], pt[:, 0:N], rd=[pr], wr=[sigo_r[ub]])
                linear(W["win"], 0, NKD, O_S5 + 128 * c8, 128, xrhs, ev_u, N)
                if lvl <= 1:
                    continue
                for j in range(4):
                    q = 4 * c8 + j
                    pre, prr = ps()
                    pim, pir = ps()
                    if j < 3:
                        lre = Btab[32 * j:32 * j + 32, c8, 0, :]; lim = Btab[32 * j:32 * j + 32, c8, 1, :]
                        urhs = u8[32 * j:32 * j + 32, ub, 0:N]
                    else:
                        lre = Bext[32 * s3:32 * s3 + 32, c8 // 3, 0, :]; lim = Bext[32 * s3:32 * s3 + 32, c8 // 3, 1, :]
                        urhs = u3[32 * s3:32 * s3 + 32, ub, 0:N]
                    k.mm(pre[:, 0:N], lre, urhs, True, True, rd=[btab_r, u8_r[ub]], wr=[prr])
                    k.mm(pim[:, 0:N], lim, urhs, True, True, rd=[btab_r, u8_r[ub]], wr=[pir])
                    for (kind, s0, sn, nsub, L) in segs:
                        def V3(ap2):
                            return ap2.rearrange("p (s l) -> p s l", l=L)
                        cb = ct[:, q, 0:L].unsqueeze(1).broadcast_to([128, nsub, L])
                        sbb = st[:, q, 0:L].unsqueeze(1).broadcast_to([128, nsub, L])
                        t1 = V3(tA[:, s0:s0 + sn]); t2 = V3(tB[:, s0:s0 + sn])
                        tr4 = twr[:, 4 * s0:4 * (s0 + sn)].rearrange("p (s j l) -> p s j l", j=4, l=L)
                        ti4 = twi[:, 4 * s0:4 * (s0 + sn)].rearrange("p (s j l) -> p s j l", j=4, l=L)
                        k.tt(t1, V3(pre[:, s0:s0 + sn]), cb, ALU.mult, rd=[prr, trig_r], wr=[tA_r])
                        k.tt(t2, V3(pim[:, s0:s0 + sn]), sbb, ALU.mult, rd=[pir, trig_r], wr=[tB_r])
                        k.tt(tr4[:, :, j, :], t1, t2, ALU.add, rd=[tA_r, tB_r], wr=[tw_r])
                        k.tt(t1, V3(pim[:, s0:s0 + sn]), cb, ALU.mult, rd=[pir, trig_r], wr=[tA_r])
                        k.tt(t2, V3(pre[:, s0:s0 + sn]), sbb, ALU.mult, rd=[prr, trig_r], wr=[tB_r])
                        k.tt(ti4[:, :, j, :], t1, t2, ALU.subtract, rd=[tA_r, tB_r], wr=[tw_r])
                for (kind, s0, sn, nsub, L) in segs:
                    tr4 = twr[:, 4 * s0:4 * (s0 + sn)].rearrange("p (s j l) -> p s j l", j=4, l=L)
                    ti4 = twi[:, 4 * s0:4 * (s0 + sn)].rearrange("p (s j l) -> p s j l", j=4, l=L)
                    q4 = slice(4 * c8, 4 * c8 + 4)
                    if kind == "P":
                        amf = (am16 if L == LT else am8)[:, q4, :].rearrange("p j l -> p (j l)")
                        Rr_, Ri_ = (Rpr, Rpi) if L == LT else (Rp8r, Rp8i)
                        for sbi in range(nsub):
                            k.tt(tr4[:, sbi, :, 0], tr4[:, sbi, :, 0], injr[:, q4], ALU.add, rd=[tw_r, inj_r[c8]], wr=[tw_r])
                            k.tt(ti4[:, sbi, :, 0], ti4[:, sbi, :, 0], inji[:, q4], ALU.add, rd=[tw_r, inj_r[c8]], wr=[tw_r])
                            fr_ = tr4[:, sbi, :, :].rearrange("p j l -> p (j l)")
                            fi_ = ti4[:, sbi, :, :].rearrange("p j l -> p (j l)")
                            k.scan(fr_, amf, fr_, 0.0, ALU.mult, ALU.add, rd=[tw_r, trig_r], wr=[tw_r])
                            k.scan(fi_, amf, fi_, 0.0, ALU.mult, ALU.add, rd=[tw_r, trig_r], wr=[tw_r])
                            glr = tr4[:, sbi, :, L - 1]; gli = ti4[:, sbi, :, L - 1]
                            b = lambda i: bt[:, i, :]
                            k.tt(b(0), Rr_[:, q4], glr, ALU.mult, rd=[tw_r, trig_r], wr=[bt_r])
                            k.tt(b(1), Ri_[:, q4], gli, ALU.mult, rd=[tw_r, trig_r], wr=[bt_r])
                            k.tt(b(2), Rr_[:, q4], gli, ALU.mult, rd=[tw_r, trig_r], wr=[bt_r])
                            k.tt(b(3), Ri_[:, q4], glr, ALU.mult, rd=[tw_r, trig_r], wr=[bt_r])
                            k.tt(injr[:, q4], b(0), b(1), ALU.subtract, rd=[bt_r], wr=[inj_r[c8]])
                            k.tt(inji[:, q4], b(2), b(3), ALU.add, rd=[bt_r], wr=[inj_r[c8]])
                    else:
                        h0r = C("h0r").rearrange("p (q s) -> p q s", q=32)[:, q4, :].rearrange("p j s -> p s j")
                        h0i = C("h0i").rearrange("p (q s) -> p q s", q=32)[:, q4, :].rearrange("p j s -> p s j")
                        lb = lam[:, q4].unsqueeze(1).broadcast_to([128, NSEQ, 4])
                        b16 = lambda i: bt[:, 2 * i:2 * i + 2, :].rearrange("p a b -> p (a b)")
                        k.tt(tA[:, 0:64].rearrange("p (s j) -> p s j", j=4), h0r, lb, ALU.mult, rd=[cf_r, trig_r], wr=[tA_r])
                        k.tt(tr4[:, :, :, 0], tr4[:, :, :, 0], tA[:, 0:64].rearrange("p (s j) -> p s j", j=4), ALU.add,
                             rd=[tw_r, tA_r], wr=[tw_r])
                        k.tt(tA[:, 0:64].rearrange("p (s j) -> p s j", j=4), h0i, lb, ALU.mult, rd=[cf_r, trig_r], wr=[tA_r])
                        k.tt(ti4[:, :, :, 0], ti4[:, :, :, 0], tA[:, 0:64].rearrange("p (s j) -> p s j", j=4), ALU.add,
                             rd=[tw_r, tA_r], wr=[tw_r])
                        amf = am8[:, q4, :].rearrange("p j l -> p (j l)")
                        for sq_ in range(NSEQ):
                            fr_ = tr4[:, sq_, :, :].rearrange("p j l -> p (j l)")
                            fi_ = ti4[:, sq_, :, :].rearrange("p j l -> p (j l)")
                            k.scan(fr_, amf, fr_, 0.0, ALU.mult, ALU.add, rd=[tw_r, trig_r], wr=[tw_r])
                            k.scan(fi_, amf, fi_, 0.0, ALU.mult, ALU.add, rd=[tw_r, trig_r], wr=[tw_r])
                        glr = tr4[:, :, :, L - 1]; gli = ti4[:, :, :, L - 1]
                        cb = ct[:, q4, LS - 1].unsqueeze(1).broadcast_to([128, NSEQ, 4])
                        sbb = st[:, q4, LS - 1].unsqueeze(1).broadcast_to([128, NSEQ, 4])
                        T = lambda i: tA[:, 64 * i:64 * i + 64].rearrange("p (s j) -> p s j", j=4)
                        k.tt(T(0), glr, cb, ALU.mult, rd=[tw_r, trig_r], wr=[tA_r])
                        k.tt(T(1), gli, sbb, ALU.mult, rd=[tw_r, trig_r], wr=[tA_r])
                        k.tt(T(2), gli, cb, ALU.mult, rd=[tw_r, trig_r], wr=[tA_r])
                        k.tt(T(3), glr, sbb, ALU.mult, rd=[tw_r, trig_r], wr=[tA_r])
                        k.tt(tB[:, 0:64].rearrange("p (j s) -> p s j", j=4), T(0), T(1), ALU.subtract, rd=[tA_r], wr=[tB_r])
                        k.tt(tB[:, 64:128].rearrange("p (j s) -> p s j", j=4), T(2), T(3), ALU.add, rd=[tA_r], wr=[tB_r])
                        k.dma("sp", o_ss5[:, 64 * c8:64 * c8 + 64], tB[:, 0:64], rd=[tB_r])
                        k.dma("sp", o_ss5[:, 512 + 64 * c8:512 + 64 * c8 + 64], tB[:, 64:128], rd=[tB_r])
                if lvl <= 2 or state_only:
                    continue
                yps, ypr = psL(0)
                for j in range(4):
                    q = 4 * c8 + j
                    for (kind, s0, sn, nsub, L) in segs:
                        def V3(ap2):
                            return ap2.rearrange("p (s l) -> p s l", l=L)
                        tr4 = twr[:, 4 * s0:4 * (s0 + sn)].rearrange("p (s j l) -> p s j l", j=4, l=L)
                        ti4 = twi[:, 4 * s0:4 * (s0 + sn)].rearrange("p (s j l) -> p s j l", j=4, l=L)
                        cb = ct[:, q, 0:L].unsqueeze(1).broadcast_to([128, nsub, L])
                        sbb = st[:, q, 0:L].unsqueeze(1).broadcast_to([128, nsub, L])
                        gr = tr4[:, :, j, :]; gi = ti4[:, :, j, :]
                        k.tt(V3(xs[:, 0, s0:s0 + sn]), cb, gr, ALU.mult, rd=[tw_r, trig_r], wr=[xs_r[0]])
                        k.stt(V3(xs[:, 1, s0:s0 + sn]), sbb, -1.0, gi, ALU.mult, ALU.mult, rd=[tw_r, trig_r], wr=[xs_r[1]])
                        k.stt(V3(xs[:, 2, s0:s0 + sn]), sbb, -1.0, gr, ALU.mult, ALU.mult, rd=[tw_r, trig_r], wr=[xs_r[2]])
                        k.stt(V3(xs[:, 3, s0:s0 + sn]), cb, -1.0, gi, ALU.mult, ALU.mult, rd=[tw_r, trig_r], wr=[xs_r[3]])
                    Crq = CR("Cr").rearrange("p (q m) -> p q m", q=32)[:, q, :]
                    Ciq = CR("Ci").rearrange("p (q m) -> p q m", q=32)[:, q, :]
                    if j < 3:
                        for i4, lh in enumerate((Crq, Crq, Ciq, Ciq)):
                            k.mm(yps[32 * j:32 * j + 32, 0:N], lh, xs[:, i4, 0:N], i4 == 0, i4 == 3, rd=[cr_r, xs_r[i4]], wr=[ypr])
                    else:
                        y2, y2r = psL(1)
                        for i4, lh in enumerate((Crq, Crq, Ciq, Ciq)):
                            k.mm(y2[0:32, 0:N], lh, xs[:, i4, 0:N], i4 == 0, i4 == 3, rd=[cr_r, xs_r[i4]], wr=[y2r])
                        k.act(yps[96:128, 0:N], y2[0:32, 0:N], AF.Copy, rd=[y2r], wr=[ypr])
                if lvl <= 3:
                    continue
                k.stt(tC[:, 0:N], sigo[:, ub, 0:N], C("s5D", c8, c8 + 1), yps[:, 0:N], ALU.mult, ALU.add,
                      rd=[sigo_r[ub], cf_r, ypr], wr=[tC_r])
                k.act(tD[:, 0:N], tC[:, 0:N], AF.Square, rd=[tC_r], wr=[tD_r])
                k.ts(tD[:, 0:N], tD[:, 0:N], 0.044715, 1.0, ALU.mult, ALU.add, rd=[tD_r], wr=[tD_r])
                k.tt(tD[:, 0:N], tD[:, 0:N], tC[:, 0:N], ALU.mult, rd=[tD_r, tC_r], wr=[tD_r])
                k.act(tD[:, 0:N], tD[:, 0:N], AF.Sigmoid, rd=[tD_r], wr=[tD_r], scale=2.0 * math.sqrt(2.0 / math.pi))
                k.tt(ysp[:, c8, 0:N], tC[:, 0:N], tD[:, 0:N], ALU.mult, rd=[tC_r, tD_r], wr=[ysp_r[c8]])
            if lvl <= 4:
                continue
            def ev_glu(j, pt, pr, m):
                k.act(tA[:, 0:N], pt[:, 0:N], AF.Sigmoid, rd=[pr], wr=[tA_r])
                k.tt(ys5[:, j, 0:N], fr(ysp[:, j, 0:N]), tA[:, 0:N], ALU.mult, rd=[ysp_r[j], tA_r], wr=[ys5_r[j]])
            if not state_only:
                linear(W["wglu"], 0, 8, 0, S5W, lambda kc: (ysp[:, kc, 0:N], ysp_r[kc]), ev_glu, N)
            if debug:
                if pc > 0:
                    k.dma("pool", dbg["d_ys5"][:, p0:p0 + pc].rearrange("(c p) t -> p c t", p=128), fr(ys5[:, :, 0:pc]), rd=ys5_r)
                if scn > 0:
                    k.dma("pool", dbg["d_ys5"][:, scol0:scol0 + scn].rearrange("(c p) t -> p c t", p=128), fr(ys5[:, :, pc:N]), rd=ys5_r)

            if stop == "s5":
                continue
            k.barrier()
            R = lambda i: rows[:, i, 0:N]
            pi_, pir_ = ps()
            pf_, pfr_ = ps()
            wif = [wtile(W["win"], 1024 * hh, 8, O_I, 8) for hh in range(2)]
            for kc in range(NKD):
                tv, tr_ = wif[kc // 8]
                k.mm(pi_[0:4, 0:N], tv[:, kc % 8, 0:4], xn[:, kc, 0:N], kc == 0, kc == NKD - 1, rd=[tr_, xn_r[kc]], wr=[pir_])
            for kc in range(NKD):
                tv, tr_ = wif[kc // 8]
                k.mm(pf_[0:4, 0:N], tv[:, kc % 8, 4:8], xn[:, kc, 0:N], kc == 0, kc == NKD - 1, rd=[tr_, xn_r[kc]], wr=[pfr_])
            k.ts(R(0), pf_[0:4, 0:N], C("bf", p0=0, p1=4), None, ALU.add, None, rd=[pfr_, cf_r], wr=[rows_r])
            k.act(R(0), R(0), AF.Exp, rd=[rows_r], wr=[rows_r], scale=-1.0)
            k.act(R(0), R(0), AF.Ln, rd=[rows_r], wr=[rows_r], bias=1.0)
            k.ts(R(1), pi_[0:4, 0:N], C("bi", p0=0, p1=4), None, ALU.add, None, rd=[pir_, cf_r], wr=[rows_r])
            chunks = []
            c = 0
            while c < pc:
                L = min(128, pc - c)
                chunks.append(("P", c, L))
                c += L
            if scn > 0:
                chunks.append(("S", pc, scn))
            for (kind, c0, L) in chunks:
                cs = slice(c0, c0 + L)
                rm = C("rmP" if kind == "P" else "rmS", 0, L, 0, 4)
                ra = C("raP" if kind == "P" else "raS", 0, L, 0, 4)
                k.scan(rows[:, 2, cs], rm, rows[:, 0, cs], 0.0, ALU.mult, ALU.subtract, rd=[rows_r, cf_r], wr=[rows_r])
                k.tt(rows[:, 3, cs], rows[:, 1, cs], rows[:, 2, cs], ALU.subtract, rd=[rows_r], wr=[rows_r])
                k.scan(rows[:, 4, cs], ra, rows[:, 3, cs], NEGBIG, ALU.add, ALU.max, rd=[rows_r, cf_r], wr=[rows_r])
                if kind == "P":
                    k.ts(rows[:, 4, cs], rows[:, 4, cs], m0p[:, 0:1], None, ALU.max, None, rd=[rows_r, m0p_r], wr=[rows_r])
                    k.ts(rows[:, 5, cs], rows[:, 4, cs], -1.0, m0p[:, 0:1], ALU.mult, ALU.add, rd=[rows_r, m0p_r], wr=[rows_r])
                else:
                    m0row = C("m0S", p0=0, p1=4).unsqueeze(2).broadcast_to([4, NSEQ, LS])
                    v3 = lambda i: rows[:, i, cs].rearrange("p (s l) -> p s l", l=LS)
                    k.tt(v3(4), v3(4), m0row, ALU.max, rd=[rows_r, cf_r], wr=[rows_r])
                    k.tt(v3(5), m0row, v3(4), ALU.subtract, rd=[rows_r, cf_r], wr=[rows_r])
                k.tt(rows[:, 6, cs], rows[:, 2, cs], rows[:, 4, cs], ALU.add, rd=[rows_r], wr=[rows_r])
                if kind == "P":
                    k.ts(rows[:, 7, cs], rows[:, 3, cs], rows[:, 4, c0 + L - 1:c0 + L], None, ALU.subtract, None, rd=[rows_r], wr=[rows_r])
                else:
                    v3 = lambda i: rows[:, i, cs].rearrange("p (s l) -> p s l", l=LS)
                    ml = v3(4)[:, :, LS - 1:LS].broadcast_to([4, NSEQ, LS])
                    k.tt(v3(7), v3(3), ml, ALU.subtract, rd=[rows_r], wr=[rows_r])
                k.ts(rows[:, 7, cs], rows[:, 7, cs], 0.0, None, ALU.min, None, rd=[rows_r], wr=[rows_r])
                k.act(rows[:, 7, cs], rows[:, 7, cs], AF.Exp, rd=[rows_r], wr=[rows_r])
                if kind == "P":
                    k.cp(m0p[:, 0:1], rows[:, 6, c0 + L - 1:c0 + L], rd=[rows_r], wr=[m0p_r])
                else:
                    k.cp(mS[:, :], rows[:, 6, cs].rearrange("p (s l) -> p s l", l=LS)[:, :, LS - 1], rd=[rows_r], wr=[mS_r])
            def prompt_state_update(hd, cs, L, vb, pb, pbr):
                pk, pkr = ps()
                pkb = pk[:].bitcast(MMT)
                k.tr(pkb[0:L, 0:128], kT[:, cs], identb, rd=[kT_r, cr_r], wr=[pkr])
                k.ts(kw[0:L, :], pkb[0:L, 0:128], wT[0:L, hd:hd + 1], None, ALU.mult, None, rd=[pkr, awT_r], wr=[kw_r])
                dcol = pb[:, 128 + L - 1:128 + L]
                for vc in range(2):
                    pcu, pcur = ps()
                    k.mm(pcu[:, 0:128], vtok[0:L, vb, 128 * vc:128 * vc + 128], kw[0:L, :], True, True,
                         rd=[vtok_r[vb], kw_r], wr=[pcur])
                    k.stt(Cp[:, hd, vc, :], Cp[:, hd, vc, :], dcol, pcu[:, 0:128], ALU.mult, ALU.add,
                          rd=[Cp_r[hd], pbr, pcur], wr=[Cp_r[hd]])
                pnu, pnur = ps()
                k.mm(pnu[:, 0:2], kw[0:L, :], ones_r[0:L, 0:2], True, True, rd=[kw_r, cr_r], wr=[pnur])
                k.stt(np_[:, hd:hd + 1], np_[:, hd:hd + 1], dcol, pnu[:, 0:1], ALU.mult, ALU.add,
                      rd=[np_r, pbr, pnur], wr=[np_r])

            for hd in range(NH):
                for qk in range(2):
                    idx = qk * 4 + hd

                    def ev_qk(j, pt, pr, m, qk=qk, idx=idx):
                        if pc > 0:
                            k.cp(cbuf[:, qk, 0:3], hist[:, idx, :], rd=[hist_r[idx]], wr=[cbuf_r[qk]])
                            k.act(cbuf[:, qk, 3:3 + pc], pt[:, 0:pc], AF.Copy, rd=[pr], wr=[cbuf_r[qk]])
                            k.cp(hist[:, idx, :], cbuf[:, qk, pc:pc + 3], rd=[cbuf_r[qk]], wr=[hist_r[idx]])
                        if scn > 0:
                            hs = C("histS").rearrange("p (i s j) -> p i s j", i=8, s=NSEQ)[:, idx, :, :]
                            k.cp(cbs[:, qk, :, 0:3], hs, rd=[cf_r], wr=[cbs_r[qk]])
                            k.act(cbs[:, qk, :, 3:3 + LS], pt[:, pc:pc + scn].rearrange("p (s l) -> p s l", l=LS), AF.Copy,
                                  rd=[pr], wr=[cbs_r[qk]])
                    linear(W["win"], 0, NKD, O_QK + 512 * qk + 128 * hd, 128, xrhs, ev_qk, N)
                    wc = lambda jj, idx=idx: C("convw", 5 * idx + jj, 5 * idx + jj + 1)
                    if state_only and qk == 0:
                        continue
                    if pc > 0:
                        k.ts(cacc[:, qk, 0:pc], cbuf[:, qk, 0:pc], wc(0), wc(4), ALU.mult, ALU.add, rd=[cbuf_r[qk], cf_r], wr=[cacc_r[qk]])
                        for jj in range(1, 4):
                            k.stt(cacc[:, qk, 0:pc], cbuf[:, qk, jj:jj + pc], wc(jj), cacc[:, qk, 0:pc], ALU.mult, ALU.add,
                                  rd=[cbuf_r[qk], cf_r, cacc_r[qk]], wr=[cacc_r[qk]])
                    if scn > 0:
                        ca3 = cacc[:, qk, pc:pc + scn].rearrange("p (s l) -> p s l", l=LS)
                        k.ts(ca3, cbs[:, qk, :, 0:LS], wc(0), wc(4), ALU.mult, ALU.add, rd=[cbs_r[qk], cf_r], wr=[cacc_r[qk]])
                        for jj in range(1, 4):
                            k.stt(ca3, cbs[:, qk, :, jj:jj + LS], wc(jj), ca3, ALU.mult, ALU.add,
                                  rd=[cbs_r[qk], cf_r, cacc_r[qk]], wr=[cacc_r[qk]])
                        if hd == NH - 1 or True:
                            k.dma("sp", o_sconv[:, 48 * idx:48 * idx + 48].rearrange("p (s j) -> p s j", j=3),
                                  cbs[:, qk, :, LS:LS + 3], rd=[cbs_r[qk]])
                    k.act(tA[:, 0:N], cacc[:, qk, 0:N], AF.Sigmoid, rd=[cacc_r[qk]], wr=[tA_r])
                    dst, dst_r = (qT, qT_r) if qk == 0 else (kT, kT_r)
                    k.stt(dst[:, 0:N], cacc[:, qk, 0:N], (DQK ** -0.5) if qk == 0 else 1.0, tA[:, 0:N], ALU.mult, ALU.mult,
                          rd=[cacc_r[qk], tA_r], wr=[dst_r])
                def ev_o(j, pt, pr, m):
                    k.act(sigo[:, j, 0:N], pt[:, 0:N], AF.Sigmoid, rd=[pr], wr=[sigo_r[j]])
                if not state_only:
                    linear(W["win"], 0, NKD, O_O + 256 * hd, 256, xrhs, ev_o, N)
                wv = [wtile(W["win"], 1024 * hh, 8, O_V + 256 * hd, 256) for hh in range(2)]
                for ci, (kind, c0, L) in enumerate(chunks):
                    cs = slice(c0, c0 + L)
                    pa, par = ps()
                    k.tr(pa[0:L, 0:4], rows[:, 3, cs], ident[0:4, 0:4], rd=[rows_r, cf_r], wr=[par])
                    k.tr(pa[0:L, 4:8], rows[:, 7, cs], ident[0:4, 0:4], rd=[rows_r, cf_r], wr=[par])
                    k.cp(aT[0:L, :], pa[0:L, 0:4], rd=[par], wr=[awT_r])
                    k.cp(wT[0:L, :], pa[0:L, 4:8], rd=[par], wr=[awT_r])
                    k.ts(RB[:, 0, 0:L], rows[:, 4, cs], -1.0, None, ALU.mult, None, rd=[rows_r], wr=[RB_r])
                    k.act(RB[:, 1, 0:L], rows[:, 5, cs], AF.Exp, rd=[rows_r], wr=[RB_r])
                    k.act(RB[:, 2, 0:L], rows[:, 6, cs], AF.Exp, rd=[rows_r], wr=[RB_r], scale=-1.0)
                    vb = ci % 3
                    pv, pvr = ps()
                    for kc in range(NKD):
                        tv, tr_ = wv[kc // 8]
                        k.mm(pv[0:L, 0:DV], xn[:, kc, cs], tv[:, kc % 8, :], kc == 0, kc == NKD - 1, rd=[tr_, xn_r[kc]], wr=[pvr])
                    k.cp(vtok[0:L, vb, :], pv[0:L, 0:DV], rd=[pvr], wr=[vtok_r[vb]])
                    pb, pbr = psL(0)
                    sel = C("sel", 128 * hd, 128 * hd + 128, 0, 4)
                    for i3 in range(3):
                        k.mm(pb[:, 128 * i3:128 * i3 + L], sel, RB[:, i3, 0:L], True, True, rd=[cf_r, RB_r], wr=[pbr])
                    if state_only:
                        prompt_state_update(hd, cs, L, vb, pb, pbr)
                        continue
                    pS, pSr = ps()
                    k.mm(pS[0:L, 0:L], kT[:, cs], qT[:, cs], True, True, rd=[kT_r, qT_r], wr=[pSr])
                    k.ts(Et[0:L, 0:L], pb[0:L, 0:L], aT[0:L, hd:hd + 1], 0.0, ALU.add, ALU.min, rd=[pbr, awT_r], wr=[Et_r])
                    k.act(Wt[0:L, 0:L], Et[0:L, 0:L], AF.Exp, rd=[Et_r], wr=[Wt_r])
                    mask = C("maskP" if kind == "P" else "maskS", 0, L, 0, L)
                    k.tt(Wt[0:L, 0:L], Wt[0:L, 0:L], mask, ALU.mult, rd=[Wt_r, cf_r], wr=[Wt_r])
                    k.tt(SW[0:L, 0:L], pS[0:L, 0:L], Wt[0:L, 0:L], ALU.mult, rd=[pSr, Wt_r], wr=[SW_r])
                    k.tt(qs[:, 0:L], fr(qT[:, cs]), pb[:, 128:128 + L], ALU.mult, rd=[qT_r, pbr], wr=[qs_r])
                    if kind == "P":
                        k.ts(nq[:, 0:L], fr(qs[:, 0:L]), np_[:, hd:hd + 1], None, ALU.mult, None, rd=[qs_r, np_r], wr=[nq_r])
                    else:
                        n0 = C("n0S").rearrange("p (h s) -> p h s", h=NH)[:, hd, :].unsqueeze(2).broadcast_to([128, NSEQ, LS])
                        k.tt(nq[:, 0:L].rearrange("p (s l) -> p s l", l=LS), fr(qs[:, 0:L]).rearrange("p (s l) -> p s l", l=LS),
                             n0, ALU.mult, rd=[qs_r, cf_r], wr=[nq_r])
                    pd, pdr = psL(3)
                    k.mm(pd[:, 0:L], ones_r[0:L, :], SW[0:L, 0:L], True, False, rd=[cr_r, SW_r], wr=[pdr])
                    k.mm(pd[:, 0:L], ones_r, nq[:, 0:L], False, True, rd=[cr_r, nq_r], wr=[pdr])
                    pn = [psL(1), psL(2)]
                    if kind == "P":
                        for vc in range(2):
                            ptc, ptcr = ps()
                            k.tr(ptc[:, 0:128], Cp[:, hd, vc, :], ident, rd=[Cp_r[hd], cf_r], wr=[ptcr])
                            k.cp(CT[:, 0, 128 * vc:128 * vc + 128], ptc[:, 0:128], rd=[ptcr], wr=[CT_r[0]])
                        for vc in range(2):
                            pnt, pnr = pn[vc]
                            k.mm(pnt[:, 0:L], vtok[0:L, vb, 128 * vc:128 * vc + 128], SW[0:L, 0:L], True, False,
                                 rd=[vtok_r[vb], SW_r], wr=[pnr])
                            k.mm(pnt[:, 0:L], CT[:, 0, 128 * vc:128 * vc + 128], qs[:, 0:L], False, True,
                                 rd=[CT_r[0], qs_r], wr=[pnr])
                    else:
                        for vc in range(2):
                            pnt, pnr = pn[vc]
                            k.mm(pnt[:, 0:L], vtok[0:L, vb, 128 * vc:128 * vc + 128], SW[0:L, 0:L], True, False,
                                 rd=[vtok_r[vb], SW_r], wr=[pnr])
                        pk, pkr = ps()
                        pkb = pk[:].bitcast(MMT)
                        k.tr(pkb[0:L, 0:128], kT[:, cs], identb, rd=[kT_r, cr_r], wr=[pkr])
                        k.ts(kw[0:L, :], pkb[0:L, 0:128], wT[0:L, hd:hd + 1], None, ALU.mult, None, rd=[pkr, awT_r], wr=[kw_r])
                        for g_ in range(2):
                            gs_ = slice(8 * g_, 8 * g_ + 8)
                            for vc in range(2):
                                k.dma("sp", Cs[:, gs_, vc, :], sC_d[gs_, hd, 128 * vc:128 * vc + 128, :].rearrange("s p kk -> p s kk"),
                                      wr=Cs_r[gs_])
                        for s_ in range(NSEQ):
                            cb_ = s_
                            tb = s_ % 2
                            for vc in range(2):
                                ptc, ptcr = ps()
                                k.tr(ptc[:, 0:128], Cs[:, cb_, vc, :], ident, rd=[Cs_r[cb_], cf_r], wr=[ptcr])
                                k.cp(CT[:, tb, 128 * vc:128 * vc + 128], ptc[:, 0:128], rd=[ptcr], wr=[CT_r[tb]])
                            for vc in range(2):
                                pnt, pnr = pn[vc]
                                k.mm(pnt[:, LS * s_:LS * s_ + LS], CT[:, tb, 128 * vc:128 * vc + 128], qs[:, LS * s_:LS * s_ + LS],
                                     False, s_ == NSEQ - 1, rd=[CT_r[tb], qs_r], wr=[pnr])
                            k.ts(kwm[:, tb, :], fr(kw[:, :]), fr(CR("seqm", s_, s_ + 1)), None, ALU.mult, None,
                                 rd=[kw_r, cr_r], wr=[kwm_r[tb]])
                            for vc in range(2):
                                pcu, pcur = ps()
                                k.mm(pcu[:, 0:128], vtok[:, vb, 128 * vc:128 * vc + 128], kwm[:, tb, :], True, True,
                                     rd=[vtok_r[vb], kwm_r[tb]], wr=[pcur])
                                k.stt(Cs[:, cb_, vc, :], Cs[:, cb_, vc, :], pb[:, 128 + LS * s_ + LS - 1:128 + LS * s_ + LS],
                                      pcu[:, 0:128], ALU.mult, ALU.add, rd=[Cs_r[cb_], pbr, pcur], wr=[Cs_r[cb_]])
                        for g_ in range(2):
                            gs_ = slice(8 * g_, 8 * g_ + 8)
                            for vc in range(2):
                                k.dma("sp", o_sC[gs_, hd, 128 * vc:128 * vc + 128, :].rearrange("s p kk -> p s kk"), Cs[:, gs_, vc, :],
                                      rd=Cs_r[gs_])
                    k.act(dab[:, 0:L], pd[:, 0:L], AF.Abs, rd=[pdr], wr=[dab_r])
                    k.tt(dab[:, 0:L], dab[:, 0:L], pb[:, 256:256 + L], ALU.max, rd=[dab_r, pbr], wr=[dab_r])
                    k.recip(rec[:, 0:L], dab[:, 0:L], rd=[dab_r], wr=[rec_r])
                    for vc in range(2):
                        pnt, pnr = pn[vc]
                        k.tt(hT[:, vc, 0:L], pnt[:, 0:L], rec[:, 0:L], ALU.mult, rd=[pnr, rec_r], wr=[hT_r[vc]])
                        k.act(hsq[:, vc, 0:L], hT[:, vc, 0:L], AF.Square, rd=[hT_r[vc]], wr=[hsq_r[vc]])
                    ph, phr = ps()
                    for vc in range(2):
                        k.mm(ph[:, 0:L], ones_r, hsq[:, vc, 0:L], vc == 0, vc == 1, rd=[cr_r, hsq_r[vc]], wr=[phr])
                    k.ts(rsh[:, 0:L], ph[:, 0:L], 1.0 / DV, EPS, ALU.mult, ALU.add, rd=[phr], wr=[rsh_r])
                    k.act(rsh[:, 0:L], rsh[:, 0:L], AF.Sqrt, rd=[rsh_r], wr=[rsh_r])
                    k.recip(rsh[:, 0:L], rsh[:, 0:L], rd=[rsh_r], wr=[rsh_r])
                    for vc in range(2):
                        cidx = 2 * hd + vc
                        k.stt(hT[:, vc, 0:L], hT[:, vc, 0:L], C("mlg", cidx, cidx + 1), rsh[:, 0:L], ALU.mult, ALU.mult,
                              rd=[hT_r[vc], cf_r, rsh_r], wr=[hT_r[vc]])
                        k.tt(yml[:, cidx, cs], hT[:, vc, 0:L], sigo[:, vc, cs], ALU.mult, rd=[hT_r[vc], sigo_r[vc]], wr=[yml_r[cidx]])
                    if kind == "P":
                        prompt_state_update(hd, cs, L, vb, pb, pbr)
                    else:
                        pnu, pnur = ps()
                        k.mm(pnu[:, 0:NSEQ], kw[:, :], CR("seqm"), True, True, rd=[kw_r, cr_r], wr=[pnur])
                        dec = pb[:, 128:256].rearrange("p (s l) -> p s l", l=LS)[:, :, LS - 1]
                        n0h = C("n0S").rearrange("p (h s) -> p h s", h=NH)[:, hd, :]
                        k.tt(nS[:, hd, :], n0h, dec, ALU.mult, rd=[cf_r, pbr], wr=[nS_r])
                        k.tt(nS[:, hd, :], nS[:, hd, :], pnu[:, 0:NSEQ], ALU.add, rd=[nS_r, pnur], wr=[nS_r])
            if debug:
                if pc > 0:
                    k.dma("pool", dbg["d_yml"][:, p0:p0 + pc].rearrange("(c p) t -> p c t", p=128), fr(yml[:, :, 0:pc]), rd=yml_r)
                if scn > 0:
                    k.dma("pool", dbg["d_yml"][:, scol0:scol0 + scn].rearrange("(c p) t -> p c t", p=128), fr(yml[:, :, pc:N]), rd=yml_r)

            if stop == "mlstm":
                continue
            if state_only:
                if ti + 1 < len(plan) and plan[ti + 1][3] != "pre":
                    fl = C("flag")
                    k.ts(injr[:], injr[:], fl, None, ALU.mult, None, rd=inj_r + [cf_r], wr=inj_r)
                    k.ts(inji[:], inji[:], fl, None, ALU.mult, None, rd=inj_r + [cf_r], wr=inj_r)
                    cpf = Cp[:].rearrange("p a b c -> p (a b c)")
                    k.ts(cpf, cpf, fl, None, ALU.mult, None, rd=Cp_r + [cf_r], wr=Cp_r)
                    k.ts(np_[:], np_[:], fl, None, ALU.mult, None, rd=[np_r, cf_r], wr=[np_r])
                    k.ts(m0p[:], m0p[:], C("flag", p0=0, p1=4), None, ALU.mult, None, rd=[m0p_r, cf_r], wr=[m0p_r])
                    hf = hist[:].rearrange("p a b -> p (a b)")
                    k.ts(hf, hf, fl, None, ALU.mult, None, rd=hist_r + [cf_r], wr=hist_r)
                continue
            k.barrier()
            for half in range(2):
                for blk in range(4):
                    cbase = 8 * half + 2 * blk

                    def ev_g(j, pt, pr, m, dst):
                        k.act(dst[j][0][:, 0:N], pt[:, 0:N], AF.Sigmoid, rd=[pr], wr=[dst[j][1]])
                    g1 = [(tA, tA_r), (tB, tB_r)]
                    g2 = [(tC, tC_r), (tD, tD_r)]
                    linear(W["win"], 0, NKD, O_G1 + 128 * cbase, 256, xrhs, lambda j, pt, pr, m: ev_g(j, pt, pr, m, g1), N)

                    def ev_a(j, pt, pr, m):
                        k.tt(g1[j][0][:, 0:N], pt[:, 0:N], g1[j][0][:, 0:N], ALU.mult, rd=[pr, g1[j][1]], wr=[g1[j][1]])
                    linear(W["wbs"], 0, 8, 128 * cbase, 256, lambda kc: (ys5[:, kc, 0:N], ys5_r[kc]), ev_a, N)
                    linear(W["win"], 0, NKD, O_G2 + 128 * cbase, 256, xrhs, lambda j, pt, pr, m: ev_g(j, pt, pr, m, g2), N)

                    def ev_b(j, pt, pr, m):
                        k.tt(g2[j][0][:, 0:N], pt[:, 0:N], g2[j][0][:, 0:N], ALU.mult, rd=[pr, g2[j][1]], wr=[g2[j][1]])
                        k.tt(mrg[:, 2 * blk + j, 0:N], g1[j][0][:, 0:N], g2[j][0][:, 0:N], ALU.add,
                             rd=[g1[j][1], g2[j][1]], wr=[mrg_r[2 * blk + j]])
                    linear(W["wbm"], 0, 8, 128 * cbase, 256, lambda kc: (yml[:, kc, 0:N], yml_r[kc]), ev_b, N)

                def ev_out(j, pt, pr, m):
                    k.tt(h[:, j, 0:N], pt[:, 0:N], h[:, j, 0:N], ALU.add, rd=[pr, h_r[j]], wr=[h_r[j]])
                linear(W["wout"], 1024 * half, 8, 0, D, lambda kc: (mrg[:, kc, 0:N], mrg_r[kc]), ev_out, N)
            if debug:
                if pc > 0:
                    k.dma("sp", dbg["d_h2"][:, p0:p0 + pc].rearrange("(c p) t -> p c t", p=128), h[:, :, 0:pc], rd=h_r)
                if scn > 0:
                    k.dma("sp", dbg["d_h2"][:, scol0:scol0 + scn].rearrange("(c p) t -> p c t", p=128), h[:, :, pc:N], rd=h_r)

            k.barrier()
            rms_stats(N, lambda c: (h[:, c, 0:N], h_r[c]))
            apply_norm(N, "g2")
            ffn(N, W["f2g"], W["f2u"], W["f2d"])
            rms_stats(N, lambda c: (h[:, c, 0:N], h_r[c]))
            for c in range(NKD):
                yo, yo_r = (tC, tC_r) if c % 2 == 0 else (tD, tD_r)
                k.stt(yo[:, 0:N], h[:, c, 0:N], C("gf", c, c + 1), rstd[:, 0:N], ALU.mult, ALU.mult,
                      rd=[h_r[c], cf_r, rstd_r], wr=[yo_r])
                if pc > 0:
                    k.dma("sp", yT[128 * c:128 * c + 128, p0 - npre:p0 - npre + pc], yo[:, 0:pc], rd=[yo_r])
                if scn > 0:
                    k.dma("sp", yT[128 * c:128 * c + 128, pl - npre:pl - npre + scn], yo[:, pc:N], rd=[yo_r])

        k.tt(tA[:, 0:32], injr[:], ilam[:], ALU.mult, rd=inj_r + [trig_r], wr=[tA_r])
        k.tt(tA[:, 32:64], inji[:], ilam[:], ALU.mult, rd=inj_r + [trig_r], wr=[tA_r])
        k.dma("sp", o_ps5, tA[:, 0:64], rd=[tA_r])
        k.dma("sp", o_pC, Cp[:].rearrange("p a b c -> p (a b c)"), rd=Cp_r)
        k.dma("sp", o_pn, np_[:], rd=[np_r])
        k.dma("sp", o_pm, m0p[:], rd=[m0p_r])
        k.dma("sp", o_pconv, hist[:].rearrange("p a b -> p (a b)"), rd=hist_r)
        k.dma("sp", o_sn, nS[:].rearrange("p a b -> p (a b)"), rd=[nS_r])
        k.dma("sp", o_sm, mS[:], rd=[mS_r])
        for key in k.misc["sp"] + k.misc["pool"]:
            if k.cnt[key] > 0:
                k._wait("sp", (key, k.cnt[key]))
        k.barrier(full=True)
    return nc, plan


def _consts_common(p):
    cf = np.zeros((128, NCF), np.float32)

    def put(name, arr, p0=0):
        o, w = CF_OFF[name]
        arr = np.asarray(arr, np.float32)
        cf[p0:p0 + arr.shape[0], o:o + arr.shape[1]] = arr
    put("ident", np.eye(128))
    s_ = np.arange(128)[:, None]; l_ = np.arange(128)[None, :]
    put("maskP", (s_ <= l_))
    put("maskS", (s_ <= l_) & ((s_ // LS) == (l_ // LS)))
    for nm, key in (("g1", "ffn1_norm"), ("gm", "mix_norm"), ("g2", "ffn2_norm")):
        put(nm, p[key][0].reshape(NKD, 128).T)
    put("gf", p["final_norm"].reshape(NKD, 128).T)
    put("s5D", p["s5_D"][0].reshape(8, 128).T)
    cw = np.zeros((128, 40), np.float32)
    for idx in range(8):
        ch = slice(128 * idx, 128 * idx + 128)
        for jj in range(4):
            cw[:, 5 * idx + jj] = p["mlstm_conv_w"][0][jj, ch]
        cw[:, 5 * idx + 4] = p["mlstm_conv_b"][0][ch]
    put("convw", cw)
    put("mlg", p["mlstm_norm"][0].reshape(8, 128).T)
    sel = np.zeros((4, 512), np.float32)
    for hd in range(4):
        sel[hd, 128 * hd:128 * hd + 128] = 1.0
    put("sel", sel)
    put("bi", p["mlstm_b_i"][0].reshape(4, 1))
    put("bf", p["mlstm_b_f"][0].reshape(4, 1))
    rmP = np.ones((4, 128), np.float32); rmP[:, 0] = 0
    rmS = np.ones((4, 128), np.float32); rmS[:, ::LS] = 0
    raP = np.zeros((4, 128), np.float32); raP[:, 0] = NEGBIG
    raS = np.zeros((4, 128), np.float32); raS[:, ::LS] = NEGBIG
    put("rmP", rmP); put("rmS", rmS); put("raP", raP); put("raS", raS)
    put("tau1", np.tile(np.arange(1, LT + 1, dtype=np.float32)[None], (128, 1)))
    m01 = np.ones((128, LT), np.float32); m01[:, 0] = 0
    put("m01", m01)
    def L1(a):
        return a.reshape(32, 2, 64).transpose(1, 2, 0).reshape(128, 32)
    put("Are", L1(p["s5_A_re"][0])); put("Aim", L1(p["s5_A_im"][0]))
    put("ldt", L1(np.repeat(p["s5_log_dt"][0][:, None], 64, axis=1)))
    ctabs = {}
    for nm, key in (("Cr", "s5_C_re"), ("Ci", "s5_C_im")):
        Cg = p[key][0]
        t = np.zeros((2, 64, 32, 2, 16), np.float32)
        for gi in range(2):
            t[gi, :, :, gi, :] = Cg[gi::2].transpose(2, 0, 1)
        ctabs[nm] = t.reshape(128, 1024)
    sc = np.zeros((128, NSC), np.float32)

    def LB_bcast(a):
        t = a.reshape(8, 4, 2, 64)
        t = t.transpose(1, 0, 2, 3)
        t = np.broadcast_to(t[:, None, None], (4, 2, 16, 8, 2, 64))
        return t.reshape(128, 1024)

    def LB_B(B):
        t = np.zeros((4, 2, 16, 8, 2, 64), np.float32)
        Bq = B.reshape(8, 4, 2, 64, 16)
        for gi in range(2):
            t[:, gi, :, :, gi, :] = Bq[:, :, gi].transpose(1, 3, 0, 2)
        return t.reshape(128, 1024)
    for nm, arr in (("AreB", LB_bcast(p["s5_A_re"][0])), ("AimB", LB_bcast(p["s5_A_im"][0])),
                    ("ldtB", LB_bcast(np.repeat(p["s5_log_dt"][0][:, None], 64, axis=1))),
                    ("BreB", LB_B(p["s5_B_re"][0])), ("BimB", LB_B(p["s5_B_im"][0]))):
        o, w = SC_OFF[nm]
        sc[:, o:o + w] = arr
    cr = np.zeros((128, NCR), np.float32)
    o, w = CR_OFF["ones"]; cr[:, o:o + w] = 1.0
    o, w = CR_OFF["seqm"]
    cr[:, o:o + w] = (np.arange(128)[:, None] // LS == np.arange(NSEQ)[None, :])
    o, w = CR_OFF["identb"]; cr[:, o:o + w] = np.eye(128, dtype=np.float32)
    for nm in ("Cr", "Ci"):
        o, w = CR_OFF[nm]; cr[:, o:o + w] = ctabs[nm]
    return cf, cr, sc


def _core_consts(cf_common, st, c, flag=1.0):
    cf = cf_common.copy()
    o_, w_ = CF_OFF["flag"]
    cf[:, o_:o_ + w_] = flag
    sl = slice(NSEQ * c, NSEQ * c + NSEQ)

    def put(name, arr, p0=0):
        o, w = CF_OFF[name]
        arr = np.asarray(arr, np.float32)
        cf[p0:p0 + arr.shape[0], o:o + arr.shape[1]] = arr
    put("m0S", st["state_mlstm_m"][0][sl].T)
    for nm, key in (("h0r", "state_s5_re"), ("h0i", "state_s5_im")):
        a = st[key][0][sl]
        t = a.reshape(NSEQ, 32, 2, 64).transpose(2, 3, 1, 0)
        put(nm, t.reshape(128, 512))
    n0 = st["state_mlstm_n"][0][sl]
    put("n0S", n0.transpose(2, 1, 0).reshape(128, 64))
    cv = st["state_mlstm_conv"][0][sl]
    t = cv.reshape(NSEQ, 3, 8, 128).transpose(3, 2, 0, 1)
    put("histS", t.reshape(128, 384))
    return cf


_PROG = {}
SPLIT = PLEN // 2
TILEW = 384


def kernel(**inputs):
    inp = {k_: np.asarray(v) for k_, v in inputs.items()}
    ncores = 8
    if "prog" not in _PROG:
        _PROG["prog"] = build_program(PLEN, debug=False, split=SPLIT, tw=TILEW)
    nc, plan = _PROG["prog"]
    cf_common, cr, sc = _consts_common(inp)
    wmap = {"f1g": inp["ffn1_w_gate"][0], "f1u": inp["ffn1_w_up"][0], "f1d": inp["ffn1_w_down"][0],
            "win": inp["w_in"][0], "wglu": inp["s5_w_glu"][0], "wbs": inp["w_branch_s5"][0],
            "wbm": inp["w_branch_mlstm"][0], "wout": inp["w_out"][0],
            "f2g": inp["ffn2_w_gate"][0], "f2u": inp["ffn2_w_up"][0], "f2d": inp["ffn2_w_down"][0]}
    wmap = {k_: np.ascontiguousarray(v, dtype=np.float32) for k_, v in wmap.items()}
    in_maps = []
    for c in range(ncores):
        b, half = c // 2, c % 2
        sl = slice(NSEQ * c, NSEQ * c + NSEQ)
        hpT = np.concatenate([inp["meta_tokens"].T, inp["x_prompt"][b].T], axis=1).astype(np.float32)
        xT = np.zeros((D, PLEN + SC), np.float32)
        if half == 0:
            xT[:, SPLIT:PLEN] = hpT[:, 0:SPLIT]
        else:
            xT[:, 0:PLEN] = hpT
        xT[:, PLEN:] = inp["x_sample"][sl].reshape(SC, D).T
        m = dict(wmap)
        m["xT"] = xT
        m["cf"] = _core_consts(cf_common, inp, c, flag=float(half))
        m["cr"] = cr
        m["sc"] = sc
        m["sC"] = np.ascontiguousarray(inp["state_mlstm_C"][0][sl], dtype=np.float32)
        in_maps.append(m)
    res = run_bass_kernel_spmd(nc, in_maps, core_ids=list(range(ncores)))
    return assemble_split(res.results)


def assemble_split(R):
    ncores = len(R)
    nm = PLEN - SPLIT
    y_prompt = np.stack([np.concatenate([R[2 * b]["yT"][:, NMETA:nm].T, R[2 * b + 1]["yT"][:, 0:nm].T], axis=0)
                         for b in range(4)])
    y_sample = np.concatenate([R[c]["yT"][:, nm:].T.reshape(NSEQ, LS, D) for c in range(ncores)])
    P = [R[2 * b + 1] for b in range(4)]
    rest = _assemble_states(P, R)
    outs = (y_prompt, y_sample) + rest
    return tuple(np.ascontiguousarray(o, dtype=np.float32) for o in outs)


def _assemble_states(P, R):
    ncores = len(R)

    def s5p(r, o):
        return r["o_ps5"][:, o:o + 32].reshape(2, 64, 32).transpose(2, 0, 1).reshape(64, 64)
    p_re = np.stack([s5p(r, 0) for r in P])[None]
    p_im = np.stack([s5p(r, 32) for r in P])[None]
    p_C = np.stack([r["o_pC"].reshape(128, NH, 2, 128).transpose(1, 2, 0, 3).reshape(NH, DV, DQK) for r in P])[None]
    p_n = np.stack([r["o_pn"].T for r in P])[None]
    p_m = np.stack([r["o_pm"][:, 0] for r in P])[None]
    p_cv = np.stack([r["o_pconv"].reshape(128, 8, 3).transpose(2, 1, 0).reshape(3, QKW) for r in P])[None]

    def s5s(c, o):
        t = R[c]["o_ss5"][:, o:o + 512].reshape(2, 64, 32, NSEQ)
        return t.transpose(3, 2, 0, 1).reshape(NSEQ, 64, 64)
    s_re = np.concatenate([s5s(c, 0) for c in range(ncores)])[None]
    s_im = np.concatenate([s5s(c, 512) for c in range(ncores)])[None]
    s_C = np.concatenate([R[c]["o_sC"] for c in range(ncores)])[None]
    s_n = np.concatenate([R[c]["o_sn"].reshape(128, NH, NSEQ).transpose(2, 1, 0) for c in range(ncores)])[None]
    s_m = np.concatenate([R[c]["o_sm"].T for c in range(ncores)])[None]
    s_cv = np.concatenate([R[c]["o_sconv"].reshape(128, 8, NSEQ, 3).transpose(2, 3, 1, 0).reshape(NSEQ, 3, QKW)
                           for c in range(ncores)])[None]
    return (p_re, p_im, p_C, p_n, p_m, p_cv, s_re, s_im, s_C, s_n, s_m, s_cv)


def assemble(R, nprompt=4, pl=PLEN):
    ncores = len(R)
    y_prompt = np.stack([R[c]["yT"][:, NMETA:pl].T for c in range(nprompt)])
    y_sample = np.concatenate([R[c]["yT"][:, pl:].T.reshape(NSEQ, LS, D) for c in range(ncores)])
    rest = _assemble_states([R[c] for c in range(nprompt)], R)
    outs = (y_prompt, y_sample) + rest
    return tuple(np.ascontiguousarray(o, dtype=np.float32) for o in outs)
```

```python
import math
import numpy as np
import concourse.bass as bass
import concourse.mybir as mybir
from concourse.bass_utils import run_bass_kernel_spmd
from contextlib import ExitStack

F32 = mybir.dt.float32
F32R = mybir.dt.float32r
BF16 = mybir.dt.bfloat16
MMT = BF16
AF = mybir.ActivationFunctionType
ALU = mybir.AluOpType

D = 2048
DFF = 5632
NKD = D // 128
S5W = 1024
QKW = 1024
VW = 1024
NH = 4
DQK = 128
DV = 256
INW = 8200
O_S5, O_QK, O_V, O_O, O_I, O_F, O_G1, O_G2 = 0, 1024, 2048, 3072, 4096, 4100, 4104, 6152
EPS = 1e-6
NMETA = 16
SEQ = 2048
PLEN = NMETA + SEQ
NSEQ = 16
LS = 8
SC = NSEQ * LS
NTM = 392
LT = 64
NEGBIG = -1.0e30
TWO_PI = 2.0 * math.pi
MAGIC = 12582912.0
STRICT_SAME_ENGINE = True


def make_plan(pl):
    tiles = []
    p0 = 0
    rem = pl
    while rem > 0:
        pc = min(NTM, rem)
        tiles.append([pc, p0, 0])
        p0 += pc
        rem -= pc
    if tiles and tiles[-1][0] + SC <= NTM:
        tiles[-1][2] = SC
    else:
        tiles.append([0, p0, SC])
    return [tuple(t) for t in tiles]


def _layout(items):
    off = {}
    o = 0
    for n, w in items:
        off[n] = (o, w)
        o += w
    return off, o


CF_ITEMS = [
    ("ident", 128), ("maskP", 128), ("maskS", 128),
    ("g1", 16), ("gm", 16), ("g2", 16), ("gf", 16), ("s5D", 8), ("convw", 40), ("mlg", 8),
    ("sel", 512), ("bi", 1), ("bf", 1), ("flag", 1),
    ("rmP", 128), ("rmS", 128), ("raP", 128), ("raS", 128), ("m0S", 16),
    ("tau1", LT), ("m01", LT), ("Are", 32), ("Aim", 32), ("ldt", 32),
    ("h0r", 512), ("h0i", 512), ("n0S", 64), ("histS", 384),
]
CF_OFF, NCF = _layout(CF_ITEMS)
CR_ITEMS = [("ones", 128), ("seqm", 16), ("identb", 128), ("Cr", 1024), ("Ci", 1024)]
CR_OFF, NCR = _layout(CR_ITEMS)
SC_ITEMS = [("AreB", 1024), ("AimB", 1024), ("ldtB", 1024), ("BreB", 1024), ("BimB", 1024)]
SC_OFF, NSC = _layout(SC_ITEMS)


class Res:
    __slots__ = ("w", "r")

    def __init__(self):
        self.w = None
        self.r = []


def RL(n):
    return [Res() for _ in range(n)]


class KB:
    def __init__(self, nc, es):
        self.nc = nc
        self.es = es
        self.E = {"pe": nc.tensor, "act": nc.scalar, "dve": nc.vector, "pool": nc.gpsimd, "sp": nc.sync}
        self.semobj = {}
        self.cnt = {}
        for k in self.E:
            self.semobj[k] = es.enter_context(nc.semaphore("sem_" + k))
            self.cnt[k] = 0
        self.seen = {k: {} for k in self.E}
        self.ndma = 0
        self.misc = {"sp": [], "pool": []}
        self.misc_i = {"sp": 0, "pool": 0}
        for q_, n_ in (("sp", 8), ("pool", 4)):
            for i in range(n_):
                key = "m%s%d" % (q_, i)
                self.semobj[key] = es.enter_context(nc.semaphore("sem_" + key))
                self.cnt[key] = 0
                self.misc[q_].append(key)

    def new_dma_sem(self, key):
        self.semobj[key] = self.es.enter_context(self.nc.semaphore("sem_" + key))
        self.cnt[key] = 0

    def _wait(self, eng, ev):
        if ev is None:
            return
        key, val = ev
        if self.seen[eng].get(key, 0) >= val:
            return
        self.E[eng].wait_ge(self.semobj[key], val)
        self.seen[eng][key] = val

    def _deps(self, eng, rd, wr):
        for r in rd:
            self._wait(eng, r.w)
        strict = STRICT_SAME_ENGINE and eng != "pe"
        for w in wr:
            if w.w is not None and (strict or w.w[0] != eng):
                self._wait(eng, w.w)
            for ev in w.r:
                if strict or ev[0] != eng:
                    self._wait(eng, ev)

    def op(self, eng, fn, rd=(), wr=()):
        self._deps(eng, rd, wr)
        ins = fn(self.E[eng])
        self.cnt[eng] += 1
        ins.then_inc(self.semobj[eng], 1)
        ev = (eng, self.cnt[eng])
        for r in rd:
            r.r.append(ev)
        for w in wr:
            w.w = ev
            w.r = []
        return ev

    def dma(self, eng, out, in_, rd=(), wr=(), semkey=None, **kw):
        if semkey is None:
            semkey = self.misc[eng][self.misc_i[eng] % len(self.misc[eng])]
            self.misc_i[eng] += 1
        if self.cnt[semkey] > 0:
            self._wait(eng, (semkey, self.cnt[semkey]))
        self._deps(eng, rd, wr)
        ins = self.E[eng].dma_start(out=out, in_=in_, **kw)
        self.cnt[semkey] += 16
        ins.then_inc(self.semobj[semkey], 16)
        ev = (semkey, self.cnt[semkey])
        for r in rd:
            r.r.append(ev)
        for w in wr:
            w.w = ev
            w.r = []
        self.ndma += 1
        return ev

    def barrier(self, full=False):
        evs = [(k, self.cnt[k]) for k in self.semobj if self.cnt[k] > 0]
        for eng in (("pe", "act", "dve", "pool", "sp") if full else ("pe", "act", "dve")):
            for ev in evs:
                self._wait(eng, ev)

    def mm(self, out, lhsT, rhs, start, stop, rd, wr):
        return self.op("pe", lambda e: e.matmul(out, lhsT, rhs, start=start, stop=stop), rd, wr)

    def tr(self, out, in_, ident, rd, wr):
        return self.op("pe", lambda e: e.transpose(out, in_, ident), rd, wr)

    def act(self, out, in_, func, rd, wr, **kw):
        return self.op("act", lambda e: e.activation(out, in_, func, **kw), rd, wr)

    def tt(self, out, in0, in1, op, rd, wr, eng="dve"):
        return self.op(eng, lambda e: e.tensor_tensor(out, in0, in1, op), rd, wr)

    def ts(self, out, in0, s1, s2, op0, op1, rd, wr, eng="dve"):
        if op1 is None:
            return self.op(eng, lambda e: e.tensor_scalar(out, in0, s1, None, op0), rd, wr)
        return self.op(eng, lambda e: e.tensor_scalar(out, in0, s1, s2, op0, op1), rd, wr)

    def stt(self, out, in0, sc, in1, op0, op1, rd, wr):
        return self.op("dve", lambda e: e.scalar_tensor_tensor(out, in0, sc, in1, op0, op1), rd, wr)

    def cp(self, out, in_, rd, wr, eng="dve"):
        return self.op(eng, lambda e: e.tensor_copy(out, in_), rd, wr)

    def scan(self, out, d0, d1, init, op0, op1, rd, wr):
        return self.op("dve", lambda e: e.tensor_tensor_scan(out, d0, d1, init, op0, op1), rd, wr)

    def recip(self, out, in_, rd, wr):
        return self.op("dve", lambda e: e.reciprocal(out, in_), rd, wr)

    def memset(self, ap, val, wr, eng="dve"):
        return self.op(eng, lambda e: e.memset(ap, val), (), wr)


def fr(ap):
    return ap


def make_plan_split(pl, split, tw):
    tiles = []
    p0 = 0
    while p0 < split:
        pc = min(tw, split - p0)
        tiles.append((pc, p0, 0, "pre"))
        p0 += pc
    while p0 < pl:
        pc = min(tw, pl - p0)
        tiles.append([pc, p0, 0, "main"])
        p0 += pc
    if tiles[-1][0] + SC <= NTM:
        tiles[-1][2] = SC
    else:
        tiles.append([0, pl, SC, "main"])
    return [tuple(t) for t in tiles]


def build_program(pl=PLEN, debug=False, stop=None, split=None, tw=NTM):
    lvl = {'ffn1': 0, 's5a': 1, 's5b': 2, 's5c': 3, 's5d': 4, 's5': 5, 'mlstm': 6}.get(stop, 9)
    if split is None:
        plan = [t + ("main",) for t in make_plan(pl)]
        npre = 0
    else:
        plan = make_plan_split(pl, split, tw)
        npre = split
        debug = False
    ncols = pl + SC
    nycols = pl - npre + SC
    nc = bass.Bass("TRN2", target_bir_lowering=False)
    dram = {}

    def din(name, shape):
        dram[name] = nc.dram_tensor(name, list(shape), F32, kind="ExternalInput").ap()
        return dram[name]

    def dout(name, shape):
        dram[name] = nc.dram_tensor(name, list(shape), F32, kind="ExternalOutput").ap()
        return dram[name]

    xT = din("xT", [D, ncols])
    W = {}
    for n, s in [("f1g", [D, DFF]), ("f1u", [D, DFF]), ("f1d", [DFF, D]), ("win", [D, INW]),
                 ("wglu", [S5W, S5W]), ("wbs", [S5W, D]), ("wbm", [VW, D]), ("wout", [D, D]),
                 ("f2g", [D, DFF]), ("f2u", [D, DFF]), ("f2d", [DFF, D])]:
        W[n] = din(n, s)
    cf_d = din("cf", [128, NCF])
    cr_d = din("cr", [128, NCR])
    sc_d = din("sc", [128, NSC])
    sC_d = din("sC", [NSEQ, NH, DV, DQK])
    yT = dout("yT", [D, nycols])
    o_ps5 = dout("o_ps5", [128, 64])
    o_pC = dout("o_pC", [128, NH * 2 * 128])
    o_pn = dout("o_pn", [128, NH])
    o_pm = dout("o_pm", [4, 1])
    o_pconv = dout("o_pconv", [128, 24])
    o_ss5 = dout("o_ss5", [128, 1024])
    o_sC = dout("o_sC", [NSEQ, NH, DV, DQK])
    o_sn = dout("o_sn", [128, 64])
    o_sm = dout("o_sm", [4, 16])
    o_sconv = dout("o_sconv", [128, 384])
    dbg = {}
    if debug:
        for n in ("d_h1", "d_ys5", "d_yml", "d_h2"):
            dbg[n] = dout(n, [D if n in ("d_h1", "d_h2") else 1024, ncols])

    es = ExitStack()
    with es:
        k = KB(nc, es)

        def sb(name, shape, dt=F32):
            return es.enter_context(nc.sbuf_tensor("s_" + name, list(shape), dt))

        cf = sb("cf", [128, NCF]); cf_r = Res()
        crt = sb("crt", [128, NCR], MMT); cr_r = Res()
        h = sb("h", [128, NKD, NTM]); h_r = RL(NKD)
        xn = sb("xn", [128, NKD, NTM], MMT); xn_r = RL(NKD)
        rstd = sb("rstd", [128, NTM]); rstd_r = Res()
        tA = sb("tA", [128, NTM]); tA_r = Res()
        tB = sb("tB", [128, NTM]); tB_r = Res()
        tC = sb("tC", [128, NTM]); tC_r = Res()
        tD = sb("tD", [128, NTM]); tD_r = Res()
        regX = sb("regX", [128, 8 * NTM], MMT)
        regF = sb("regF", [128, 12 * NTM + 8])
        mid = regX[:].rearrange("p (a b n) -> p a b n", a=2, b=4); mid_r = [RL(4), RL(4)]
        ys5 = sb("ys5", [128, 8, NTM], MMT); ys5_r = RL(8)
        yml = sb("yml", [128, 8, NTM], MMT); yml_r = RL(8)
        sq = yml[:, 0:2, :]; sq_r = RL(2)
        mrg = sb("mrg", [128, 8, NTM], MMT); mrg_r = RL(8)
        NRING = 6
        ring = [sb("ring%d" % i, [128, 2048], MMT) for i in range(NRING)]
        ring_r = RL(NRING)
        for i in range(NRING):
            k.new_dma_sem("rg%d" % i)
        ring_i = [0]
        psum = [es.enter_context(nc.psum_tensor("ps%d" % i, [128, 512], F32)) for i in range(8)]
        psum_r = RL(8)
        ps_i = [0]
        ct = sb("ct", [128, 32, LT]); st = sb("st", [128, 32, LT]); trig_r = Res()
        am16 = sb("am16", [128, 32, LT]); am8 = sb("am8", [128, 32, LS])
        lam = sb("lam", [128, 32]); ilam = sb("ilam", [128, 32])
        Rpr = sb("Rpr", [128, 32]); Rpi = sb("Rpi", [128, 32])
        Rp8r = sb("Rp8r", [128, 32]); Rp8i = sb("Rp8i", [128, 32])
        Btab = sb("Btab", [128, 8, 2, 128], MMT); btab_r = Res()
        injr = sb("injr", [128, 32]); inji = sb("inji", [128, 32]); inj_r = RL(8)
        u8 = yml[:, 4:6, :]; u8_r = RL(2)
        twr = regF[:, 0:4 * NTM]; twi = regF[:, 4 * NTM:8 * NTM]; tw_r = Res()
        xs = sb("xs", [128, 4, NTM], MMT); xs_r = RL(4)
        u3 = yml[:, 6:8, :]
        Bext = sb("Bext", [128, 3, 2, 128], MMT)
        ysp = mrg; ysp_r = mrg_r
        sS_r = Res()
        bt = sb("bt", [128, 8, 4]); bt_r = Res()
        Cp = sb("Cp", [128, NH, 2, 128]); Cp_r = RL(NH)
        np_ = sb("np", [128, NH]); np_r = Res()
        m0p = sb("m0p", [4, 1]); m0p_r = Res()
        hist = sb("hist", [128, 8, 3]); hist_r = RL(8)
        cbuf = regF[:, 8 * NTM:8 * NTM + 2 * (NTM + 3)].rearrange("p (a n) -> p a n", a=2); cbuf_r = RL(2)
        cbs = sb("cbs", [128, 2, NSEQ, LS + 3]); cbs_r = RL(2)
        cacc = regF[:, 8 * NTM + 2 * (NTM + 3):8 * NTM + 2 * (NTM + 3) + 2 * NTM].rearrange("p (a n) -> p a n", a=2); cacc_r = RL(2)
        qT = sb("qT", [128, NTM], MMT); qT_r = Res()
        kT = sb("kT", [128, NTM], MMT); kT_r = Res()
        vtok = sb("vtok", [128, 3, DV], MMT); vtok_r = RL(3)
        sigo = sb("sigo", [128, 2, NTM]); sigo_r = RL(2)
        rows = regF[0:4, 0:8 * NTM].rearrange("p (r n) -> p r n", r=8); rows_r = Res()
        RB = sb("RB", [4, 3, 128]); RB_r = Res()
        aT = sb("aT", [128, 4]); wT = sb("wT", [128, 4]); awT_r = Res()
        Et = sb("Et", [128, 128]); Et_r = Res()
        Wt = sb("Wt", [128, 128]); Wt_r = Res()
        SW = sb("SW", [128, 128], MMT); SW_r = Res()
        qs = sb("qs", [128, 128], MMT); qs_r = Res()
        nq = sb("nq", [128, 128], MMT); nq_r = Res()
        CT = sb("CT", [128, 2, DV], MMT); CT_r = RL(2)
        Cs = sb("Cs", [128, NSEQ, 2, 128]); Cs_r = RL(NSEQ)
        kw = sb("kw", [128, 128], MMT); kw_r = Res()
        kwm = sb("kwm", [128, 2, 128], MMT); kwm_r = RL(2)
        dab = sb("dab", [128, 128]); dab_r = Res()
        rec = sb("rec", [128, 128]); rec_r = Res()
        hT = sb("hT", [128, 2, 128]); hT_r = RL(2)
        hsq = sb("hsq", [128, 2, 128], MMT); hsq_r = RL(2)
        rsh = sb("rsh", [128, 128]); rsh_r = Res()
        nS = sb("nS", [128, NH, NSEQ]); nS_r = Res()
        mS = sb("mS", [4, NSEQ]); mS_r = Res()

        def C(name, lo=0, hi=None, p0=0, p1=128):
            o, w = CF_OFF[name]
            if hi is None:
                hi = w
            return cf[p0:p1, o + lo:o + hi]

        def CR(name, lo=0, hi=None, p0=0, p1=128):
            o, w = CR_OFF[name]
            if hi is None:
                hi = w
            return crt[p0:p1, o + lo:o + hi]

        ident = C("ident")
        ones_r = CR("ones")
        identb = CR("identb")

        def ps():
            i = ps_i[0] % 4
            ps_i[0] += 1
            return psum[i], psum_r[i]

        def psL(i):
            return psum[4 + i], psum_r[4 + i]

        def wtile(Wap, row0, kt, col0, ncol):
            i = ring_i[0] % NRING
            ring_i[0] += 1
            view = ring[i][:, 0:kt * ncol].rearrange("p (k c) -> p k c", k=kt)
            src = Wap[row0:row0 + 128 * kt, col0:col0 + ncol].rearrange("(k p) c -> p k c", p=128)
            k.dma("pool", view, src, rd=(), wr=[ring_r[i]], semkey="rg%d" % i)
            return view, ring_r[i]

        def linear(Wap, row0, nkc, col0, ncols_, rhs_fn, evac_fn, N):
            KT = min(nkc, 8)
            nkh = nkc // KT
            BC = 2048 // KT
            c = 0
            j = 0
            while c < ncols_:
                bc = min(BC, ncols_ - c)
                tiles_ = [wtile(Wap, row0 + 128 * KT * hh, KT, col0 + c, bc) for hh in range(nkh)]
                subs = []
                cc = 0
                while cc < bc:
                    m = min(128, bc - cc)
                    subs.append((cc, m, ps()))
                    cc += m
                for hh in range(nkh):
                    tv, tr_ = tiles_[hh]
                    for (cc, m, (pt, pr)) in subs:
                        for kk in range(KT):
                            kc = hh * KT + kk
                            rap, rres = rhs_fn(kc)
                            k.mm(pt[0:m, 0:N], tv[:, kk, cc:cc + m], rap, kc == 0, kc == nkc - 1,
                                 rd=[tr_, rres], wr=[pr])
                for (cc, m, (pt, pr)) in subs:
                    evac_fn(j, pt, pr, m)
                    j += 1
                c += bc

        k.dma("sp", cf[:], cf_d, wr=[cf_r])
        for c0_ in range(0, NCR, 1024):
            c1_ = min(NCR, c0_ + 1024)
            k.dma("pool", crt[:, c0_:c1_], cr_d[:, c0_:c1_], wr=[cr_r])
        if True:
            sct = h[:].rearrange("p a b -> p (a b)")
            assert NSC <= NKD * NTM
            sct_r = Res(); sw_r = Res()
            k.dma("sp", sct[:, 0:NSC], sc_d, wr=[sct_r])
            _pc = [tA[:, 0:128], tA[:, 128:256], tB[:, 0:128], tB[:, 128:256], tC[:, 0:128], tC[:, 128:256],
                   tD[:, 0:128], tD[:, 128:256], rstd[:, 0:128], rstd[:, 128:256]]

            def sincos(theta, s_out, c_out, t0, t1, rdl, wrl):
                for (shift, dst) in ((0.0, s_out), (0.5 * math.pi, c_out)):
                    k.ts(t0, theta, 1.0 / TWO_PI, shift / TWO_PI, ALU.mult, ALU.add, rd=rdl, wr=wrl)
                    k.ts(t0, t0, MAGIC, None, ALU.add, None, rd=wrl, wr=wrl)
                    k.ts(t0, t0, -MAGIC, None, ALU.add, None, rd=wrl, wr=wrl)
                    k.stt(t1, t0, -TWO_PI, theta, ALU.mult, ALU.add, rd=rdl + wrl, wr=wrl)
                    k.ts(t1, t1, shift, None, ALU.add, None, rd=wrl, wr=wrl)
                    k.ts(t1, t1, math.pi, -math.pi, ALU.min, ALU.max, rd=wrl, wr=wrl)
                    k.act(dst, t1, AF.Sin, rd=wrl, wr=wrl)

            rs = [sct_r, sw_r]
            ws = [sw_r]
            for c8_ in range(8):
                def S(name, c8_=c8_):
                    o, w = SC_OFF[name]
                    return sct[:, o + 128 * c8_:o + 128 * c8_ + 128]
                X = lambda i: _pc[i]
                k.act(X(0), S("ldtB"), AF.Exp, rd=rs, wr=ws)
                k.tt(X(1), S("AimB"), X(0), ALU.mult, rd=rs, wr=ws)
                k.tt(X(2), S("AreB"), X(0), ALU.mult, rd=rs, wr=ws)
                k.act(X(2), X(2), AF.Exp, rd=rs, wr=ws)
                sincos(X(1), X(3), X(4), X(5), X(6), rs, ws)
                k.tt(X(3), X(3), X(2), ALU.mult, rd=rs, wr=ws)
                k.tt(X(4), X(4), X(2), ALU.mult, rd=rs, wr=ws)
                k.ts(X(4), X(4), -1.0, None, ALU.add, None, rd=rs, wr=ws)
                k.tt(X(5), S("AreB"), S("AreB"), ALU.mult, rd=rs, wr=ws)
                k.tt(X(6), S("AimB"), S("AimB"), ALU.mult, rd=rs, wr=ws)
                k.tt(X(5), X(5), X(6), ALU.add, rd=rs, wr=ws)
                k.recip(X(5), X(5), rd=rs, wr=ws)
                k.tt(X(6), X(4), S("AreB"), ALU.mult, rd=rs, wr=ws)
                k.tt(X(7), X(3), S("AimB"), ALU.mult, rd=rs, wr=ws)
                k.tt(X(6), X(6), X(7), ALU.add, rd=rs, wr=ws)
                k.tt(X(6), X(6), X(5), ALU.mult, rd=rs, wr=ws)
                k.tt(X(7), X(3), S("AreB"), ALU.mult, rd=rs, wr=ws)
                k.tt(X(8), X(4), S("AimB"), ALU.mult, rd=rs, wr=ws)
                k.tt(X(7), X(7), X(8), ALU.subtract, rd=rs, wr=ws)
                k.tt(X(7), X(7), X(5), ALU.mult, rd=rs, wr=ws)
                k.tt(X(8), X(6), S("BreB"), ALU.mult, rd=rs, wr=ws)
                k.tt(X(9), X(7), S("BimB"), ALU.mult, rd=rs, wr=ws)
                k.tt(Btab[:, c8_, 0, :], X(8), X(9), ALU.subtract, rd=rs, wr=[btab_r])
                k.tt(X(8), X(6), S("BimB"), ALU.mult, rd=rs, wr=ws)
                k.tt(X(9), X(7), S("BreB"), ALU.mult, rd=rs, wr=ws)
                k.tt(Btab[:, c8_, 1, :], X(8), X(9), ALU.add, rd=rs + [btab_r], wr=[btab_r])
                s3 = c8_ % 3
                k.cp(Bext[32 * s3:32 * s3 + 32, c8_ // 3, :, :], fr(Btab[96:128, c8_, :, :]), rd=[btab_r], wr=[btab_r])
            rs = [cf_r, sw_r, trig_r]
            ws = [sw_r, trig_r]
            dtL = tD[:, 0:32]; thL = tD[:, 32:64]; adL = tD[:, 64:96]
            k.act(dtL, C("ldt"), AF.Exp, rd=rs, wr=ws)
            k.tt(thL, C("Aim"), dtL, ALU.mult, rd=rs, wr=ws)
            k.tt(adL, C("Are"), dtL, ALU.mult, rd=rs, wr=ws)
            k.act(lam[:], adL, AF.Exp, rd=rs, wr=ws)
            k.act(ilam[:], adL, AF.Exp, rd=rs, wr=ws, scale=-1.0)
            for hq in range(8):
                qs_ = slice(4 * hq, 4 * hq + 4)
                ang = tA[:, 0:4 * LT]
                k.tt(ang.rearrange("p (q t) -> p q t", q=4), thL[:, qs_].unsqueeze(2).broadcast_to([128, 4, LT]),
                     C("tau1").unsqueeze(1).broadcast_to([128, 4, LT]), ALU.mult, rd=rs, wr=ws)
                sincos(ang, st[:, qs_, :].rearrange("p q t -> p (q t)"), ct[:, qs_, :].rearrange("p q t -> p (q t)"),
                       tB[:, 0:4 * LT], tC[:, 0:4 * LT], rs, ws)
            k.tt(am16[:], lam[:].unsqueeze(2).broadcast_to([128, 32, LT]),
                 C("m01").unsqueeze(1).broadcast_to([128, 32, LT]), ALU.mult, rd=rs, wr=ws)
            k.cp(am8[:], am16[:, :, 0:LS], rd=rs, wr=ws)
            k.tt(Rpr[:], lam[:], ct[:, :, LT - 1], ALU.mult, rd=rs, wr=ws)
            k.tt(Rpi[:], lam[:], st[:, :, LT - 1], ALU.mult, rd=rs, wr=ws)
            k.tt(Rp8r[:], lam[:], ct[:, :, LS - 1], ALU.mult, rd=rs, wr=ws)
            k.tt(Rp8i[:], lam[:], st[:, :, LS - 1], ALU.mult, rd=rs, wr=ws)
            k.memset(injr[:], 0.0, wr=inj_r)
            k.memset(inji[:], 0.0, wr=inj_r)
            k.memset(Cp[:].rearrange("p a b c -> p (a b c)"), 0.0, wr=Cp_r)
            k.memset(np_[:], 0.0, wr=[np_r])
            k.memset(m0p[:], 0.0, wr=[m0p_r])
            k.memset(hist[:].rearrange("p a b -> p (a b)"), 0.0, wr=hist_r)
            k.barrier(full=True)
        k.barrier(full=True)

        def rms_stats(N, src_fn):
            pt, pr = ps()
            for c in range(NKD):
                sap, sres = src_fn(c)
                k.act(sq[:, c % 2, 0:N], sap, AF.Square, rd=[sres], wr=[sq_r[c % 2]])
                k.mm(pt[:, 0:N], ones_r, sq[:, c % 2, 0:N], c == 0, c == NKD - 1, rd=[cr_r, sq_r[c % 2]], wr=[pr])
            k.ts(tA[:, 0:N], pt[:, 0:N], 1.0 / D, EPS, ALU.mult, ALU.add, rd=[pr], wr=[tA_r])
            k.act(tA[:, 0:N], tA[:, 0:N], AF.Sqrt, rd=[tA_r], wr=[tA_r])
            k.recip(rstd[:, 0:N], tA[:, 0:N], rd=[tA_r], wr=[rstd_r])

        def apply_norm(N, gname):
            for c in range(NKD):
                k.stt(xn[:, c, 0:N], h[:, c, 0:N], C(gname, c, c + 1), rstd[:, 0:N], ALU.mult, ALU.mult,
                      rd=[h_r[c], cf_r, rstd_r], wr=[xn_r[c]])

        def ffn(N, wg, wu, wd):
            xrhs = lambda kc: (xn[:, kc, 0:N], xn_r[kc])
            for p in range(DFF // 512):
                mb = p % 2
                gs = sb_tmp_g

                def ev_gate(j, pt, pr, m, gs=gs):
                    k.act(gs[j][0][:, 0:N], pt[:, 0:N], AF.Silu, rd=[pr], wr=[gs[j][1]])

                def ev_up(j, pt, pr, m, gs=gs, mb=mb):
                    k.tt(mid[:, mb, j, 0:N], pt[:, 0:N], gs[j][0][:, 0:N], ALU.mult, rd=[pr, gs[j][1]],
                         wr=[mid_r[mb][j]])
                for half in range(2):
                    linear(wg, 0, NKD, 512 * p + 256 * half, 256, xrhs,
                           lambda j, pt, pr, m, half=half: ev_gate(2 * half + j, pt, pr, m), N)
                    linear(wu, 0, NKD, 512 * p + 256 * half, 256, xrhs,
                           lambda j, pt, pr, m, half=half: ev_up(2 * half + j, pt, pr, m), N)

                def ev_down(j, pt, pr, m):
                    k.stt(h[:, j, 0:N], pt[:, 0:N], 0.5, h[:, j, 0:N], ALU.mult, ALU.add, rd=[pr, h_r[j]], wr=[h_r[j]])
                linear(wd, 512 * p, 4, 0, D, lambda kc, mb=mb: (mid[:, mb, kc, 0:N], mid_r[mb][kc]), ev_down, N)

        sb_tmp_g = [(tA, tA_r), (tB, tB_r), (tC, tC_r), (tD, tD_r)]

        col0 = 0
        scol0 = pl
        for ti, (pc, p0, scn, mode) in enumerate(plan):
            N = pc + scn
            state_only = (mode == "pre")
            if pc > 0:
                k.dma("sp", h[:, :, 0:pc], xT[:, p0:p0 + pc].rearrange("(c p) t -> p c t", p=128), wr=h_r)
            if scn > 0:
                k.dma("sp", h[:, :, pc:pc + scn], xT[:, scol0:scol0 + scn].rearrange("(c p) t -> p c t", p=128), wr=h_r)
            rms_stats(N, lambda c: (h[:, c, 0:N], h_r[c]))
            apply_norm(N, "g1")
            ffn(N, W["f1g"], W["f1u"], W["f1d"])
            if debug:
                if pc > 0:
                    k.dma("sp", dbg["d_h1"][:, p0:p0 + pc].rearrange("(c p) t -> p c t", p=128), h[:, :, 0:pc], rd=h_r)
                if scn > 0:
                    k.dma("sp", dbg["d_h1"][:, scol0:scol0 + scn].rearrange("(c p) t -> p c t", p=128), h[:, :, pc:N], rd=h_r)
            if stop == "ffn1":
                continue
            rms_stats(N, lambda c: (h[:, c, 0:N], h_r[c]))
            apply_norm(N, "gm")
            xrhs = lambda kc: (xn[:, kc, 0:N], xn_r[kc])

            k.barrier()
            segs = []
            if pc > 0:
                pmain = (pc // LT) * LT
                if pmain > 0:
                    segs.append(("P", 0, pmain, pmain // LT, LT))
                if pc > pmain:
                    assert (pc - pmain) % LS == 0
                    segs.append(("P", pmain, pc - pmain, (pc - pmain) // LS, LS))
            if scn > 0:
                segs.append(("S", pc, scn, NSEQ, LS))
            for c8 in range(8):
                ub = c8 % 2
                holder = {}

                s3 = c8 % 3

                def ev_u(j, pt, pr, m, ub=ub, s3=s3):
                    k.cp(u8[:, ub, 0:N], pt[:, 0:N], rd=[pr], wr=[u8_r[ub]])
                    k.cp(u3[32 * s3:32 * s3 + 32, ub, 0:N], pt[96:128, 0:N], rd=[pr], wr=[u8_r[ub]])
                    if not state_only:
                        k.cp(sigo[:, ub, 0:N], pt[:, 0:N], rd=[pr], wr=[sigo_r[ub]])
                linear(W["win"], 0, NKD, O_S5 + 128 * c8, 128, xrhs, ev_u, N)
                if lvl <= 1:
                    continue
                for j in range(4):
                    q = 4 * c8 + j
                    pre, prr = ps()
                    pim, pir = ps()
                    if j < 3:
                        lre = Btab[32 * j:32 * j + 32, c8, 0, :]; lim = Btab[32 * j:32 * j + 32, c8, 1, :]
                        urhs = u8[32 * j:32 * j + 32, ub, 0:N]
                    else:
                        lre = Bext[32 * s3:32 * s3 + 32, c8 // 3, 0, :]; lim = Bext[32 * s3:32 * s3 + 32, c8 // 3, 1, :]
                        urhs = u3[32 * s3:32 * s3 + 32, ub, 0:N]
                    k.mm(pre[:, 0:N], lre, urhs, True, True, rd=[btab_r, u8_r[ub]], wr=[prr])
                    k.mm(pim[:, 0:N], lim, urhs, True, True, rd=[btab_r, u8_r[ub]], wr=[pir])
                    for (kind, s0, sn, nsub, L) in segs:
                        def V3(ap2):
                            return ap2.rearrange("p (s l) -> p s l", l=L)
                        cb = ct[:, q, 0:L].unsqueeze(1).broadcast_to([128, nsub, L])
                        sbb = st[:, q, 0:L].unsqueeze(1).broadcast_to([128, nsub, L])
                        t1 = V3(tA[:, s0:s0 + sn]); t2 = V3(tB[:, s0:s0 + sn])
                        tr4 = twr[:, 4 * s0:4 * (s0 + sn)].rearrange("p (s j l) -> p s j l", j=4, l=L)
                        ti4 = twi[:, 4 * s0:4 * (s0 + sn)].rearrange("p (s j l) -> p s j l", j=4, l=L)
                        k.tt(t1, V3(pre[:, s0:s0 + sn]), cb, ALU.mult, rd=[prr, trig_r], wr=[tA_r])
                        k.tt(t2, V3(pim[:, s0:s0 + sn]), sbb, ALU.mult, rd=[pir, trig_r], wr=[tB_r])
                        k.tt(tr4[:, :, j, :], t1, t2, ALU.add, rd=[tA_r, tB_r], wr=[tw_r])
                        k.tt(t1, V3(pim[:, s0:s0 + sn]), cb, ALU.mult, rd=[pir, trig_r], wr=[tA_r])
                        k.tt(t2, V3(pre[:, s0:s0 + sn]), sbb, ALU.mult, rd=[prr, trig_r], wr=[tB_r])
                        k.tt(ti4[:, :, j, :], t1, t2, ALU.subtract, rd=[tA_r, tB_r], wr=[tw_r])
                for (kind, s0, sn, nsub, L) in segs:
                    tr4 = twr[:, 4 * s0:4 * (s0 + sn)].rearrange("p (s j l) -> p s j l", j=4, l=L)
                    ti4 = twi[:, 4 * s0:4 * (s0 + sn)].rearrange("p (s j l) -> p s j l", j=4, l=L)
                    q4 = slice(4 * c8, 4 * c8 + 4)
                    if kind == "P":
                        amf = (am16 if L == LT else am8)[:, q4, :].rearrange("p j l -> p (j l)")
                        Rr_, Ri_ = (Rpr, Rpi) if L == LT else (Rp8r, Rp8i)
                        for sbi in range(nsub):
                            k.tt(tr4[:, sbi, :, 0], tr4[:, sbi, :, 0], injr[:, q4], ALU.add, rd=[tw_r, inj_r[c8]], wr=[tw_r])
                            k.tt(ti4[:, sbi, :, 0], ti4[:, sbi, :, 0], inji[:, q4], ALU.add, rd=[tw_r, inj_r[c8]], wr=[tw_r])
                            fr_ = tr4[:, sbi, :, :].rearrange("p j l -> p (j l)")
                            fi_ = ti4[:, sbi, :, :].rearrange("p j l -> p (j l)")
                            k.scan(fr_, amf, fr_, 0.0, ALU.mult, ALU.add, rd=[tw_r, trig_r], wr=[tw_r])
                            k.scan(fi_, amf, fi_, 0.0, ALU.mult, ALU.add, rd=[tw_r, trig_r], wr=[tw_r])
                            glr = tr4[:, sbi, :, L - 1]; gli = ti4[:, sbi, :, L - 1]
                            b = lambda i: bt[:, i, :]
                            k.tt(b(0), Rr_[:, q4], glr, ALU.mult, rd=[tw_r, trig_r], wr=[bt_r])
                            k.tt(b(1), Ri_[:, q4], gli, ALU.mult, rd=[tw_r, trig_r], wr=[bt_r])
                            k.tt(b(2), Rr_[:, q4], gli, ALU.mult, rd=[tw_r, trig_r], wr=[bt_r])
                            k.tt(b(3), Ri_[:, q4], glr, ALU.mult, rd=[tw_r, trig_r], wr=[bt_r])
                            k.tt(injr[:, q4], b(0), b(1), ALU.subtract, rd=[bt_r], wr=[inj_r[c8]])
                            k.tt(inji[:, q4], b(2), b(3), ALU.add, rd=[bt_r], wr=[inj_r[c8]])
                    else:
                        h0r = C("h0r").rearrange("p (q s) -> p q s", q=32)[:, q4, :].rearrange("p j s -> p s j")
                        h0i = C("h0i").rearrange("p (q s) -> p q s", q=32)[:, q4, :].rearrange("p j s -> p s j")
                        lb = lam[:, q4].unsqueeze(1).broadcast_to([128, NSEQ, 4])
                        b16 = lambda i: bt[:, 2 * i:2 * i + 2, :].rearrange("p a b -> p (a b)")
                        k.tt(tA[:, 0:64].rearrange("p (s j) -> p s j", j=4), h0r, lb, ALU.mult, rd=[cf_r, trig_r], wr=[tA_r])
                        k.tt(tr4[:, :, :, 0], tr4[:, :, :, 0], tA[:, 0:64].rearrange("p (s j) -> p s j", j=4), ALU.add,
                             rd=[tw_r, tA_r], wr=[tw_r])
                        k.tt(tA[:, 0:64].rearrange("p (s j) -> p s j", j=4), h0i, lb, ALU.mult, rd=[cf_r, trig_r], wr=[tA_r])
                        k.tt(ti4[:, :, :, 0], ti4[:, :, :, 0], tA[:, 0:64].rearrange("p (s j) -> p s j", j=4), ALU.add,
                             rd=[tw_r, tA_r], wr=[tw_r])
                        amf = am8[:, q4, :].rearrange("p j l -> p (j l)")
                        for sq_ in range(NSEQ):
                            fr_ = tr4[:, sq_, :, :].rearrange("p j l -> p (j l)")
                            fi_ = ti4[:, sq_, :, :].rearrange("p j l -> p (j l)")
                            k.scan(fr_, amf, fr_, 0.0, ALU.mult, ALU.add, rd=[tw_r, trig_r], wr=[tw_r])
                            k.scan(fi_, amf, fi_, 0.0, ALU.mult, ALU.add, rd=[tw_r, trig_r], wr=[tw_r])
                        glr = tr4[:, :, :, L - 1]; gli = ti4[:, :, :, L - 1]
                        cb = ct[:, q4, LS - 1].unsqueeze(1).broadcast_to([128, NSEQ, 4])
                        sbb = st[:, q4, LS - 1].unsqueeze(1).broadcast_to([128, NSEQ, 4])
                        T = lambda i: tA[:, 64 * i:64 * i + 64].rearrange("p (s j) -> p s j", j=4)
                        k.tt(T(0), glr, cb, ALU.mult, rd=[tw_r, trig_r], wr=[tA_r])
                        k.tt(T(1), gli, sbb, ALU.mult, rd=[tw_r, trig_r], wr=[tA_r])
                        k.tt(T(2), gli, cb, ALU.mult, rd=[tw_r, trig_r], wr=[tA_r])
                        k.tt(T(3), glr, sbb, ALU.mult, rd=[tw_r, trig_r], wr=[tA_r])
                        k.tt(tB[:, 0:64].rearrange("p (j s) -> p s j", j=4), T(0), T(1), ALU.subtract, rd=[tA_r], wr=[tB_r])
                        k.tt(tB[:, 64:128].rearrange("p (j s) -> p s j", j=4), T(2), T(3), ALU.add, rd=[tA_r], wr=[tB_r])
                        k.dma("sp", o_ss5[:, 64 * c8:64 * c8 + 64], tB[:, 0:64], rd=[tB_r])
                        k.dma("sp", o_ss5[:, 512 + 64 * c8:512 + 64 * c8 + 64], tB[:, 64:128], rd=[tB_r])
                if lvl <= 2 or state_only:
                    continue
                yps, ypr = psL(0)
                for j in range(4):
                    q = 4 * c8 + j
                    for (kind, s0, sn, nsub, L) in segs:
                        def V3(ap2):
                            return ap2.rearrange("p (s l) -> p s l", l=L)
                        tr4 = twr[:, 4 * s0:4 * (s0 + sn)].rearrange("p (s j l) -> p s j l", j=4, l=L)
                        ti4 = twi[:, 4 * s0:4 * (s0 + sn)].rearrange("p (s j l) -> p s j l", j=4, l=L)
                        cb = ct[:, q, 0:L].unsqueeze(1).broadcast_to([128, nsub, L])
                        sbb = st[:, q, 0:L].unsqueeze(1).broadcast_to([128, nsub, L])
                        gr = tr4[:, :, j, :]; gi = ti4[:, :, j, :]
                        k.tt(V3(xs[:, 0, s0:s0 + sn]), cb, gr, ALU.mult, rd=[tw_r, trig_r], wr=[xs_r[0]])
                        k.stt(V3(xs[:, 1, s0:s0 + sn]), sbb, -1.0, gi, ALU.mult, ALU.mult, rd=[tw_r, trig_r], wr=[xs_r[1]])
                        k.stt(V3(xs[:, 2, s0:s0 + sn]), sbb, -1.0, gr, ALU.mult, ALU.mult, rd=[tw_r, trig_r], wr=[xs_r[2]])
                        k.stt(V3(xs[:, 3, s0:s0 + sn]), cb, -1.0, gi, ALU.mult, ALU.mult, rd=[tw_r, trig_r], wr=[xs_r[3]])
                    Crq = CR("Cr").rearrange("p (q m) -> p q m", q=32)[:, q, :]
                    Ciq = CR("Ci").rearrange("p (q m) -> p q m", q=32)[:, q, :]
                    if j < 3:
                        for i4, lh in enumerate((Crq, Crq, Ciq, Ciq)):
                            k.mm(yps[32 * j:32 * j + 32, 0:N], lh, xs[:, i4, 0:N], i4 == 0, i4 == 3, rd=[cr_r, xs_r[i4]], wr=[ypr])
                    else:
                        y2, y2r = psL(1)
                        for i4, lh in enumerate((Crq, Crq, Ciq, Ciq)):
                            k.mm(y2[0:32, 0:N], lh, xs[:, i4, 0:N], i4 == 0, i4 == 3, rd=[cr_r, xs_r[i4]], wr=[y2r])
                        k.act(yps[96:128, 0:N], y2[0:32, 0:N], AF.Copy, rd=[y2r], wr=[ypr])
                if lvl <= 3:
                    continue
                k.stt(tC[:, 0:N], sigo[:, ub, 0:N], C("s5D", c8, c8 + 1), yps[:, 0:N], ALU.mult, ALU.add,
                      rd=[sigo_r[ub], cf_r, ypr], wr=[tC_r])
                k.act(tD[:, 0:N], tC[:, 0:N], AF.Square, rd=[tC_r], wr=[tD_r])
                k.ts(tD[:, 0:N], tD[:, 0:N], 0.044715, 1.0, ALU.mult, ALU.add, rd=[tD_r], wr=[tD_r])
                k.tt(tD[:, 0:N], tD[:, 0:N], tC[:, 0:N], ALU.mult, rd=[tD_r, tC_r], wr=[tD_r])
                k.act(tD[:, 0:N], tD[:, 0:N], AF.Sigmoid, rd=[tD_r], wr=[tD_r], scale=2.0 * math.sqrt(2.0 / math.pi))
                k.tt(ysp[:, c8, 0:N], tC[:, 0:N], tD[:, 0:N], ALU.mult, rd=[tC_r, tD_r], wr=[ysp_r[c8]])
            if lvl <= 4:
                continue
            def ev_glu(j, pt, pr, m):
                k.act(tA[:, 0:N], pt[:, 0:N], AF.Sigmoid, rd=[pr], wr=[tA_r])
                k.tt(ys5[:, j, 0:N], fr(ysp[:, j, 0:N]), tA[:, 0:N], ALU.mult, rd=[ysp_r[j], tA_r], wr=[ys5_r[j]])
            if not state_only:
                linear(W["wglu"], 0, 8, 0, S5W, lambda kc: (ysp[:, kc, 0:N], ysp_r[kc]), ev_glu, N)
            if debug:
                if pc > 0:
                    k.dma("pool", dbg["d_ys5"][:, p0:p0 + pc].rearrange("(c p) t -> p c t", p=128), fr(ys5[:, :, 0:pc]), rd=ys5_r)
                if scn > 0:
                    k.dma("pool", dbg["d_ys5"][:, scol0:scol0 + scn].rearrange("(c p) t -> p c t", p=128), fr(ys5[:, :, pc:N]), rd=ys5_r)

            if stop == "s5":
                continue
            k.barrier()
            R = lambda i: rows[:, i, 0:N]
            pi_, pir_ = ps()
            pf_, pfr_ = ps()
            wif = [wtile(W["win"], 1024 * hh, 8, O_I, 8) for hh in range(2)]
            for kc in range(NKD):
                tv, tr_ = wif[kc // 8]
                k.mm(pi_[0:4, 0:N], tv[:, kc % 8, 0:4], xn[:, kc, 0:N], kc == 0, kc == NKD - 1, rd=[tr_, xn_r[kc]], wr=[pir_])
            for kc in range(NKD):
                tv, tr_ = wif[kc // 8]
                k.mm(pf_[0:4, 0:N], tv[:, kc % 8, 4:8], xn[:, kc, 0:N], kc == 0, kc == NKD - 1, rd=[tr_, xn_r[kc]], wr=[pfr_])
            k.ts(R(0), pf_[0:4, 0:N], C("bf", p0=0, p1=4), None, ALU.add, None, rd=[pfr_, cf_r], wr=[rows_r])
            k.act(R(0), R(0), AF.Exp, rd=[rows_r], wr=[rows_r], scale=-1.0)
            k.act(R(0), R(0), AF.Ln, rd=[rows_r], wr=[rows_r], bias=1.0)
            k.ts(R(1), pi_[0:4, 0:N], C("bi", p0=0, p1=4), None, ALU.add, None, rd=[pir_, cf_r], wr=[rows_r])
            chunks = []
            c = 0
            while c < pc:
                L = min(128, pc - c)
                chunks.append(("P", c, L))
                c += L
            if scn > 0:
                chunks.append(("S", pc, scn))
            for (kind, c0, L) in chunks:
                cs = slice(c0, c0 + L)
                rm = C("rmP" if kind == "P" else "rmS", 0, L, 0, 4)
                ra = C("raP" if kind == "P" else "raS", 0, L, 0, 4)
                k.scan(rows[:, 2, cs], rm, rows[:, 0, cs], 0.0, ALU.mult, ALU.subtract, rd=[rows_r, cf_r], wr=[rows_r])
                k.tt(rows[:, 3, cs], rows[:, 1, cs], rows[:, 2, cs], ALU.subtract, rd=[rows_r], wr=[rows_r])
                k.scan(rows[:, 4, cs], ra, rows[:, 3, cs], NEGBIG, ALU.add, ALU.max, rd=[rows_r, cf_r], wr=[rows_r])
                if kind == "P":
                    k.ts(rows[:, 4, cs], rows[:, 4, cs], m0p[:, 0:1], None, ALU.max, None, rd=[rows_r, m0p_r], wr=[rows_r])
                    k.ts(rows[:, 5, cs], rows[:, 4, cs], -1.0, m0p[:, 0:1], ALU.mult, ALU.add, rd=[rows_r, m0p_r], wr=[rows_r])
                else:
                    m0row = C("m0S", p0=0, p1=4).unsqueeze(2).broadcast_to([4, NSEQ, LS])
                    v3 = lambda i: rows[:, i, cs].rearrange("p (s l) -> p s l", l=LS)
                    k.tt(v3(4), v3(4), m0row, ALU.max, rd=[rows_r, cf_r], wr=[rows_r])
                    k.tt(v3(5), m0row, v3(4), ALU.subtract, rd=[rows_r, cf_r], wr=[rows_r])
                k.tt(rows[:, 6, cs], rows[:, 2, cs], rows[:, 4, cs], ALU.add, rd=[rows_r], wr=[rows_r])
                if kind == "P":
                    k.ts(rows[:, 7, cs], rows[:, 3, cs], rows[:, 4, c0 + L - 1:c0 + L], None, ALU.subtract, None, rd=[rows_r], wr=[rows_r])
                else:
                    v3 = lambda i: rows[:, i, cs].rearrange("p (s l) -> p s l", l=LS)
                    ml = v3(4)[:, :, LS - 1:LS].broadcast_to([4, NSEQ, LS])
                    k.tt(v3(7), v3(3), ml, ALU.subtract, rd=[rows_r], wr=[rows_r])
                k.ts(rows[:, 7, cs], rows[:, 7, cs], 0.0, None, ALU.min, None, rd=[rows_r], wr=[rows_r])
                k.act(rows[:, 7, cs], rows[:, 7, cs], AF.Exp, rd=[rows_r], wr=[rows_r])
                if kind == "P":
                    k.cp(m0p[:, 0:1], rows[:, 6, c0 + L - 1:c0 + L], rd=[rows_r], wr=[m0p_r])
                else:
                    k.cp(mS[:, :], rows[:, 6, cs].rearrange("p (s l) -> p s l", l=LS)[:, :, LS - 1], rd=[rows_r], wr=[mS_r])
            def prompt_state_update(hd, cs, L, vb, pb, pbr):
                pk, pkr = ps()
                pkb = pk[:].bitcast(MMT)
                k.tr(pkb[0:L, 0:128], kT[:, cs], identb, rd=[kT_r, cr_r], wr=[pkr])
                k.ts(kw[0:L, :], pkb[0:L, 0:128], wT[0:L, hd:hd + 1], None, ALU.mult, None, rd=[pkr, awT_r], wr=[kw_r])
                dcol = pb[:, 128 + L - 1:128 + L]
                for vc in range(2):
                    pcu, pcur = ps()
                    k.mm(pcu[:, 0:128], vtok[0:L, vb, 128 * vc:128 * vc + 128], kw[0:L, :], True, True,
                         rd=[vtok_r[vb], kw_r], wr=[pcur])
                    k.stt(Cp[:, hd, vc, :], Cp[:, hd, vc, :], dcol, pcu[:, 0:128], ALU.mult, ALU.add,
                          rd=[Cp_r[hd], pbr, pcur], wr=[Cp_r[hd]])
                pnu, pnur = ps()
                k.mm(pnu[:, 0:2], kw[0:L, :], ones_r[0:L, 0:2], True, True, rd=[kw_r, cr_r], wr=[pnur])
                k.stt(np_[:, hd:hd + 1], np_[:, hd:hd + 1], dcol, pnu[:, 0:1], ALU.mult, ALU.add,
                      rd=[np_r, pbr, pnur], wr=[np_r])

            for hd in range(NH):
                for qk in range(2):
                    idx = qk * 4 + hd

                    def ev_qk(j, pt, pr, m, qk=qk, idx=idx):
                        if pc > 0:
                            k.cp(cbuf[:, qk, 0:3], hist[:, idx, :], rd=[hist_r[idx]], wr=[cbuf_r[qk]])
                            k.act(cbuf[:, qk, 3:3 + pc], pt[:, 0:pc], AF.Copy, rd=[pr], wr=[cbuf_r[qk]])
                            k.cp(hist[:, idx, :], cbuf[:, qk, pc:pc + 3], rd=[cbuf_r[qk]], wr=[hist_r[idx]])
                        if scn > 0:
                            hs = C("histS").rearrange("p (i s j) -> p i s j", i=8, s=NSEQ)[:, idx, :, :]
                            k.cp(cbs[:, qk, :, 0:3], hs, rd=[cf_r], wr=[cbs_r[qk]])
                            k.act(cbs[:, qk, :, 3:3 + LS], pt[:, pc:pc + scn].rearrange("p (s l) -> p s l", l=LS), AF.Copy,
                                  rd=[pr], wr=[cbs_r[qk]])
                    linear(W["win"], 0, NKD, O_QK + 512 * qk + 128 * hd, 128, xrhs, ev_qk, N)
                    wc = lambda jj, idx=idx: C("convw", 5 * idx + jj, 5 * idx + jj + 1)
                    if state_only and qk == 0:
                        continue
                    if pc > 0:
                        k.ts(cacc[:, qk, 0:pc], cbuf[:, qk, 0:pc], wc(0), wc(4), ALU.mult, ALU.add, rd=[cbuf_r[qk], cf_r], wr=[cacc_r[qk]])
                        for jj in range(1, 4):
                            k.stt(cacc[:, qk, 0:pc], cbuf[:, qk, jj:jj + pc], wc(jj), cacc[:, qk, 0:pc], ALU.mult, ALU.add,
                                  rd=[cbuf_r[qk], cf_r, cacc_r[qk]], wr=[cacc_r[qk]])
                    if scn > 0:
                        ca3 = cacc[:, qk, pc:pc + scn].rearrange("p (s l) -> p s l", l=LS)
                        k.ts(ca3, cbs[:, qk, :, 0:LS], wc(0), wc(4), ALU.mult, ALU.add, rd=[cbs_r[qk], cf_r], wr=[cacc_r[qk]])
                        for jj in range(1, 4):
                            k.stt(ca3, cbs[:, qk, :, jj:jj + LS], wc(jj), ca3, ALU.mult, ALU.add,
                                  rd=[cbs_r[qk], cf_r, cacc_r[qk]], wr=[cacc_r[qk]])
                        if hd == NH - 1 or True:
                            k.dma("sp", o_sconv[:, 48 * idx:48 * idx + 48].rearrange("p (s j) -> p s j", j=3),
                                  cbs[:, qk, :, LS:LS + 3], rd=[cbs_r[qk]])
                    k.act(tA[:, 0:N], cacc[:, qk, 0:N], AF.Sigmoid, rd=[cacc_r[qk]], wr=[tA_r])
                    dst, dst_r = (qT, qT_r) if qk == 0 else (kT, kT_r)
                    k.stt(dst[:, 0:N], cacc[:, qk, 0:N], (DQK ** -0.5) if qk == 0 else 1.0, tA[:, 0:N], ALU.mult, ALU.mult,
                          rd=[cacc_r[qk], tA_r], wr=[dst_r])
                def ev_o(j, pt, pr, m):
                    k.act(sigo[:, j, 0:N], pt[:, 0:N], AF.Sigmoid, rd=[pr], wr=[sigo_r[j]])
                if not state_only:
                    linear(W["win"], 0, NKD, O_O + 256 * hd, 256, xrhs, ev_o, N)
                wv = [wtile(W["win"], 1024 * hh, 8, O_V + 256 * hd, 256) for hh in range(2)]
                for ci, (kind, c0, L) in enumerate(chunks):
                    cs = slice(c0, c0 + L)
                    pa, par = ps()
                    k.tr(pa[0:L, 0:4], rows[:, 3, cs], ident[0:4, 0:4], rd=[rows_r, cf_r], wr=[par])
                    k.tr(pa[0:L, 4:8], rows[:, 7, cs], ident[0:4, 0:4], rd=[rows_r, cf_r], wr=[par])
                    k.cp(aT[0:L, :], pa[0:L, 0:4], rd=[par], wr=[awT_r])
                    k.cp(wT[0:L, :], pa[0:L, 4:8], rd=[par], wr=[awT_r])
                    k.ts(RB[:, 0, 0:L], rows[:, 4, cs], -1.0, None, ALU.mult, None, rd=[rows_r], wr=[RB_r])
                    k.act(RB[:, 1, 0:L], rows[:, 5, cs], AF.Exp, rd=[rows_r], wr=[RB_r])
                    k.act(RB[:, 2, 0:L], rows[:, 6, cs], AF.Exp, rd=[rows_r], wr=[RB_r], scale=-1.0)
                    vb = ci % 3
                    pv, pvr = ps()
                    for kc in range(NKD):
                        tv, tr_ = wv[kc // 8]
                        k.mm(pv[0:L, 0:DV], xn[:, kc, cs], tv[:, kc % 8, :], kc == 0, kc == NKD - 1, rd=[tr_, xn_r[kc]], wr=[pvr])
                    k.cp(vtok[0:L, vb, :], pv[0:L, 0:DV], rd=[pvr], wr=[vtok_r[vb]])
                    pb, pbr = psL(0)
                    sel = C("sel", 128 * hd, 128 * hd + 128, 0, 4)
                    for i3 in range(3):
                        k.mm(pb[:, 128 * i3:128 * i3 + L], sel, RB[:, i3, 0:L], True, True, rd=[cf_r, RB_r], wr=[pbr])
                    if state_only:
                        prompt_state_update(hd, cs, L, vb, pb, pbr)
                        continue
                    pS, pSr = ps()
                    k.mm(pS[0:L, 0:L], kT[:, cs], qT[:, cs], True, True, rd=[kT_r, qT_r], wr=[pSr])
                    k.ts(Et[0:L, 0:L], pb[0:L, 0:L], aT[0:L, hd:hd + 1], 0.0, ALU.add, ALU.min, rd=[pbr, awT_r], wr=[Et_r])
                    k.act(Wt[0:L, 0:L], Et[0:L, 0:L], AF.Exp, rd=[Et_r], wr=[Wt_r])
                    mask = C("maskP" if kind == "P" else "maskS", 0, L, 0, L)
                    k.tt(Wt[0:L, 0:L], Wt[0:L, 0:L], mask, ALU.mult, rd=[Wt_r, cf_r], wr=[Wt_r])
                    k.tt(SW[0:L, 0:L], pS[0:L, 0:L], Wt[0:L, 0:L], ALU.mult, rd=[pSr, Wt_r], wr=[SW_r])
                    k.tt(qs[:, 0:L], fr(qT[:, cs]), pb[:, 128:128 + L], ALU.mult, rd=[qT_r, pbr], wr=[qs_r])
                    if kind == "P":
                        k.ts(nq[:, 0:L], fr(qs[:, 0:L]), np_[:, hd:hd + 1], None, ALU.mult, None, rd=[qs_r, np_r], wr=[nq_r])
                    else:
                        n0 = C("n0S").rearrange("p (h s) -> p h s", h=NH)[:, hd, :].unsqueeze(2).broadcast_to([128, NSEQ, LS])
                        k.tt(nq[:, 0:L].rearrange("p (s l) -> p s l", l=LS), fr(qs[:, 0:L]).rearrange("p (s l) -> p s l", l=LS),
                             n0, ALU.mult, rd=[qs_r, cf_r], wr=[nq_r])
                    pd, pdr = psL(3)
                    k.mm(pd[:, 0:L], ones_r[0:L, :], SW[0:L, 0:L], True, False, rd=[cr_r, SW_r], wr=[pdr])
                    k.mm(pd[:, 0:L], ones_r, nq[:, 0:L], False, True, rd=[cr_r, nq_r], wr=[pdr])
                    pn = [psL(1), psL(2)]
                    if kind == "P":
                        for vc in range(2):
                            ptc, ptcr = ps()
                            k.tr(ptc[:, 0:128], Cp[:, hd, vc, :], ident, rd=[Cp_r[hd], cf_r], wr=[ptcr])
                            k.cp(CT[:, 0, 128 * vc:128 * vc + 128], ptc[:, 0:128], rd=[ptcr], wr=[CT_r[0]])
                        for vc in range(2):
                            pnt, pnr = pn[vc]
                            k.mm(pnt[:, 0:L], vtok[0:L, vb, 128 * vc:128 * vc + 128], SW[0:L, 0:L], True, False,
                                 rd=[vtok_r[vb], SW_r], wr=[pnr])
                            k.mm(pnt[:, 0:L], CT[:, 0, 128 * vc:128 * vc + 128], qs[:, 0:L], False, True,
                                 rd=[CT_r[0], qs_r], wr=[pnr])
                    else:
                        for vc in range(2):
                            pnt, pnr = pn[vc]
                            k.mm(pnt[:, 0:L], vtok[0:L, vb, 128 * vc:128 * vc + 128], SW[0:L, 0:L], True, False,
                                 rd=[vtok_r[vb], SW_r], wr=[pnr])
                        pk, pkr = ps()
                        pkb = pk[:].bitcast(MMT)
                        k.tr(pkb[0:L, 0:128], kT[:, cs], identb, rd=[kT_r, cr_r], wr=[pkr])
                        k.ts(kw[0:L, :], pkb[0:L, 0:128], wT[0:L, hd:hd + 1], None, ALU.mult, None, rd=[pkr, awT_r], wr=[kw_r])
                        for g_ in range(2):
                            gs_ = slice(8 * g_, 8 * g_ + 8)
                            for vc in range(2):
                                k.dma("sp", Cs[:, gs_, vc, :], sC_d[gs_, hd, 128 * vc:128 * vc + 128, :].rearrange("s p kk -> p s kk"),
                                      wr=Cs_r[gs_])
                        for s_ in range(NSEQ):
                            cb_ = s_
                            tb = s_ % 2
                            for vc in range(2):
                                ptc, ptcr = ps()
                                k.tr(ptc[:, 0:128], Cs[:, cb_, vc, :], ident, rd=[Cs_r[cb_], cf_r], wr=[ptcr])
                                k.cp(CT[:, tb, 128 * vc:128 * vc + 128], ptc[:, 0:128], rd=[ptcr], wr=[CT_r[tb]])
                            for vc in range(2):
                                pnt, pnr = pn[vc]
                                k.mm(pnt[:, LS * s_:LS * s_ + LS], CT[:, tb, 128 * vc:128 * vc + 128], qs[:, LS * s_:LS * s_ + LS],
                                     False, s_ == NSEQ - 1, rd=[CT_r[tb], qs_r], wr=[pnr])
                            k.ts(kwm[:, tb, :], fr(kw[:, :]), fr(CR("seqm", s_, s_ + 1)), None, ALU.mult, None,
                                 rd=[kw_r, cr_r], wr=[kwm_r[tb]])
                            for vc in range(2):
                                pcu, pcur = ps()
                                k.mm(pcu[:, 0:128], vtok[:, vb, 128 * vc:128 * vc + 128], kwm[:, tb, :], True, True,
                                     rd=[vtok_r[vb], kwm_r[tb]], wr=[pcur])
                                k.stt(Cs[:, cb_, vc, :], Cs[:, cb_, vc, :], pb[:, 128 + LS * s_ + LS - 1:128 + LS * s_ + LS],
                                      pcu[:, 0:128], ALU.mult, ALU.add, rd=[Cs_r[cb_], pbr, pcur], wr=[Cs_r[cb_]])
                        for g_ in range(2):
                            gs_ = slice(8 * g_, 8 * g_ + 8)
                            for vc in range(2):
                                k.dma("sp", o_sC[gs_, hd, 128 * vc:128 * vc + 128, :].rearrange("s p kk -> p s kk"), Cs[:, gs_, vc, :],
                                      rd=Cs_r[gs_])
                    k.act(dab[:, 0:L], pd[:, 0:L], AF.Abs, rd=[pdr], wr=[dab_r])
                    k.tt(dab[:, 0:L], dab[:, 0:L], pb[:, 256:256 + L], ALU.max, rd=[dab_r, pbr], wr=[dab_r])
                    k.recip(rec[:, 0:L], dab[:, 0:L], rd=[dab_r], wr=[rec_r])
                    for vc in range(2):
                        pnt, pnr = pn[vc]
                        k.tt(hT[:, vc, 0:L], pnt[:, 0:L], rec[:, 0:L], ALU.mult, rd=[pnr, rec_r], wr=[hT_r[vc]])
                        k.act(hsq[:, vc, 0:L], hT[:, vc, 0:L], AF.Square, rd=[hT_r[vc]], wr=[hsq_r[vc]])
                    ph, phr = ps()
                    for vc in range(2):
                        k.mm(ph[:, 0:L], ones_r, hsq[:, vc, 0:L], vc == 0, vc == 1, rd=[cr_r, hsq_r[vc]], wr=[phr])
                    k.ts(rsh[:, 0:L], ph[:, 0:L], 1.0 / DV, EPS, ALU.mult, ALU.add, rd=[phr], wr=[rsh_r])
                    k.act(rsh[:, 0:L], rsh[:, 0:L], AF.Sqrt, rd=[rsh_r], wr=[rsh_r])
                    k.recip(rsh[:, 0:L], rsh[:, 0:L], rd=[rsh_r], wr=[rsh_r])
                    for vc in range(2):
                        cidx = 2 * hd + vc
                        k.stt(hT[:, vc, 0:L], hT[:, vc, 0:L], C("mlg", cidx, cidx + 1), rsh[:, 0:L], ALU.mult, ALU.mult,
                              rd=[hT_r[vc], cf_r, rsh_r], wr=[hT_r[vc]])
                        k.tt(yml[:, cidx, cs], hT[:, vc, 0:L], sigo[:, vc, cs], ALU.mult, rd=[hT_r[vc], sigo_r[vc]], wr=[yml_r[cidx]])
                    if kind == "P":
                        prompt_state_update(hd, cs, L, vb, pb, pbr)
                    else:
                        pnu, pnur = ps()
                        k.mm(pnu[:, 0:NSEQ], kw[:, :], CR("seqm"), True, True, rd=[kw_r, cr_r], wr=[pnur])
                        dec = pb[:, 128:256].rearrange("p (s l) -> p s l", l=LS)[:, :, LS - 1]
                        n0h = C("n0S").rearrange("p (h s) -> p h s", h=NH)[:, hd, :]
                        k.tt(nS[:, hd, :], n0h, dec, ALU.mult, rd=[cf_r, pbr], wr=[nS_r])
                        k.tt(nS[:, hd, :], nS[:, hd, :], pnu[:, 0:NSEQ], ALU.add, rd=[nS_r, pnur], wr=[nS_r])
            if debug:
                if pc > 0:
                    k.dma("pool", dbg["d_yml"][:, p0:p0 + pc].rearrange("(c p) t -> p c t", p=128), fr(yml[:, :, 0:pc]), rd=yml_r)
                if scn > 0:
                    k.dma("pool", dbg["d_yml"][:, scol0:scol0 + scn].rearrange("(c p) t -> p c t", p=128), fr(yml[:, :, pc:N]), rd=yml_r)

            if stop == "mlstm":
                continue
            if state_only:
                if ti + 1 < len(plan) and plan[ti + 1][3] != "pre":
                    fl = C("flag")
                    k.ts(injr[:], injr[:], fl, None, ALU.mult, None, rd=inj_r + [cf_r], wr=inj_r)
                    k.ts(inji[:], inji[:], fl, None, ALU.mult, None, rd=inj_r + [cf_r], wr=inj_r)
                    cpf = Cp[:].rearrange("p a b c -> p (a b c)")
                    k.ts(cpf, cpf, fl, None, ALU.mult, None, rd=Cp_r + [cf_r], wr=Cp_r)
                    k.ts(np_[:], np_[:], fl, None, ALU.mult, None, rd=[np_r, cf_r], wr=[np_r])
                    k.ts(m0p[:], m0p[:], C("flag", p0=0, p1=4), None, ALU.mult, None, rd=[m0p_r, cf_r], wr=[m0p_r])
                    hf = hist[:].rearrange("p a b -> p (a b)")
                    k.ts(hf, hf, fl, None, ALU.mult, None, rd=hist_r + [cf_r], wr=hist_r)
                continue
            k.barrier()
            for half in range(2):
                for blk in range(4):
                    cbase = 8 * half + 2 * blk

                    def ev_g(j, pt, pr, m, dst):
                        k.act(dst[j][0][:, 0:N], pt[:, 0:N], AF.Sigmoid, rd=[pr], wr=[dst[j][1]])
                    g1 = [(tA, tA_r), (tB, tB_r)]
                    g2 = [(tC, tC_r), (tD, tD_r)]
                    linear(W["win"], 0, NKD, O_G1 + 128 * cbase, 256, xrhs, lambda j, pt, pr, m: ev_g(j, pt, pr, m, g1), N)

                    def ev_a(j, pt, pr, m):
                        k.tt(g1[j][0][:, 0:N], pt[:, 0:N], g1[j][0][:, 0:N], ALU.mult, rd=[pr, g1[j][1]], wr=[g1[j][1]])
                    linear(W["wbs"], 0, 8, 128 * cbase, 256, lambda kc: (ys5[:, kc, 0:N], ys5_r[kc]), ev_a, N)
                    linear(W["win"], 0, NKD, O_G2 + 128 * cbase, 256, xrhs, lambda j, pt, pr, m: ev_g(j, pt, pr, m, g2), N)

                    def ev_b(j, pt, pr, m):
                        k.tt(g2[j][0][:, 0:N], pt[:, 0:N], g2[j][0][:, 0:N], ALU.mult, rd=[pr, g2[j][1]], wr=[g2[j][1]])
                        k.tt(mrg[:, 2 * blk + j, 0:N], g1[j][0][:, 0:N], g2[j][0][:, 0:N], ALU.add,
                             rd=[g1[j][1], g2[j][1]], wr=[mrg_r[2 * blk + j]])
                    linear(W["wbm"], 0, 8, 128 * cbase, 256, lambda kc: (yml[:, kc, 0:N], yml_r[kc]), ev_b, N)

                def ev_out(j, pt, pr, m):
                    k.tt(h[:, j, 0:N], pt[:, 0:N], h[:, j, 0:N], ALU.add, rd=[pr, h_r[j]], wr=[h_r[j]])
                linear(W["wout"], 1024 * half, 8, 0, D, lambda kc: (mrg[:, kc, 0:N], mrg_r[kc]), ev_out, N)
            if debug:
                if pc > 0:
                    k.dma("sp", dbg["d_h2"][:, p0:p0 + pc].rearrange("(c p) t -> p c t", p=128), h[:, :, 0:pc], rd=h_r)
                if scn > 0:
                    k.dma("sp", dbg["d_h2"][:, scol0:scol0 + scn].rearrange("(c p) t -> p c t", p=128), h[:, :, pc:N], rd=h_r)

            k.barrier()
            rms_stats(N, lambda c: (h[:, c, 0:N], h_r[c]))
            apply_norm(N, "g2")
            ffn(N, W["f2g"], W["f2u"], W["f2d"])
            rms_stats(N, lambda c: (h[:, c, 0:N], h_r[c]))
            for c in range(NKD):
                yo, yo_r = (tC, tC_r) if c % 2 == 0 else (tD, tD_r)
                k.stt(yo[:, 0:N], h[:, c, 0:N], C("gf", c, c + 1), rstd[:, 0:N], ALU.mult, ALU.mult,
                      rd=[h_r[c], cf_r, rstd_r], wr=[yo_r])
                if pc > 0:
                    k.dma("sp", yT[128 * c:128 * c + 128, p0 - npre:p0 - npre + pc], yo[:, 0:pc], rd=[yo_r])
                if scn > 0:
                    k.dma("sp", yT[128 * c:128 * c + 128, pl - npre:pl - npre + scn], yo[:, pc:N], rd=[yo_r])

        k.tt(tA[:, 0:32], injr[:], ilam[:], ALU.mult, rd=inj_r + [trig_r], wr=[tA_r])
        k.tt(tA[:, 32:64], inji[:], ilam[:], ALU.mult, rd=inj_r + [trig_r], wr=[tA_r])
        k.dma("sp", o_ps5, tA[:, 0:64], rd=[tA_r])
        k.dma("sp", o_pC, Cp[:].rearrange("p a b c -> p (a b c)"), rd=Cp_r)
        k.dma("sp", o_pn, np_[:], rd=[np_r])
        k.dma("sp", o_pm, m0p[:], rd=[m0p_r])
        k.dma("sp", o_pconv, hist[:].rearrange("p a b -> p (a b)"), rd=hist_r)
        k.dma("sp", o_sn, nS[:].rearrange("p a b -> p (a b)"), rd=[nS_r])
        k.dma("sp", o_sm, mS[:], rd=[mS_r])
        for key in k.misc["sp"] + k.misc["pool"]:
            if k.cnt[key] > 0:
                k._wait("sp", (key, k.cnt[key]))
        k.barrier(full=True)
    return nc, plan


def _consts_common(p):
    cf = np.zeros((128, NCF), np.float32)

    def put(name, arr, p0=0):
        o, w = CF_OFF[name]
        arr = np.asarray(arr, np.float32)
        cf[p0:p0 + arr.shape[0], o:o + arr.shape[1]] = arr
    put("ident", np.eye(128))
    s_ = np.arange(128)[:, None]; l_ = np.arange(128)[None, :]
    put("maskP", (s_ <= l_))
    put("maskS", (s_ <= l_) & ((s_ // LS) == (l_ // LS)))
    for nm, key in (("g1", "ffn1_norm"), ("gm", "mix_norm"), ("g2", "ffn2_norm")):
        put(nm, p[key][0].reshape(NKD, 128).T)
    put("gf", p["final_norm"].reshape(NKD, 128).T)
    put("s5D", p["s5_D"][0].reshape(8, 128).T)
    cw = np.zeros((128, 40), np.float32)
    for idx in range(8):
        ch = slice(128 * idx, 128 * idx + 128)
        for jj in range(4):
            cw[:, 5 * idx + jj] = p["mlstm_conv_w"][0][jj, ch]
        cw[:, 5 * idx + 4] = p["mlstm_conv_b"][0][ch]
    put("convw", cw)
    put("mlg", p["mlstm_norm"][0].reshape(8, 128).T)
    sel = np.zeros((4, 512), np.float32)
    for hd in range(4):
        sel[hd, 128 * hd:128 * hd + 128] = 1.0
    put("sel", sel)
    put("bi", p["mlstm_b_i"][0].reshape(4, 1))
    put("bf", p["mlstm_b_f"][0].reshape(4, 1))
    rmP = np.ones((4, 128), np.float32); rmP[:, 0] = 0
    rmS = np.ones((4, 128), np.float32); rmS[:, ::LS] = 0
    raP = np.zeros((4, 128), np.float32); raP[:, 0] = NEGBIG
    raS = np.zeros((4, 128), np.float32); raS[:, ::LS] = NEGBIG
    put("rmP", rmP); put("rmS", rmS); put("raP", raP); put("raS", raS)
    put("tau1", np.tile(np.arange(1, LT + 1, dtype=np.float32)[None], (128, 1)))
    m01 = np.ones((128, LT), np.float32); m01[:, 0] = 0
    put("m01", m01)
    def L1(a):
        return a.reshape(32, 2, 64).transpose(1, 2, 0).reshape(128, 32)
    put("Are", L1(p["s5_A_re"][0])); put("Aim", L1(p["s5_A_im"][0]))
    put("ldt", L1(np.repeat(p["s5_log_dt"][0][:, None], 64, axis=1)))
    ctabs = {}
    for nm, key in (("Cr", "s5_C_re"), ("Ci", "s5_C_im")):
        Cg = p[key][0]
        t = np.zeros((2, 64, 32, 2, 16), np.float32)
        for gi in range(2):
            t[gi, :, :, gi, :] = Cg[gi::2].transpose(2, 0, 1)
        ctabs[nm] = t.reshape(128, 1024)
    sc = np.zeros((128, NSC), np.float32)

    def LB_bcast(a):
        t = a.reshape(8, 4, 2, 64)
        t = t.transpose(1, 0, 2, 3)
        t = np.broadcast_to(t[:, None, None], (4, 2, 16, 8, 2, 64))
        return t.reshape(128, 1024)

    def LB_B(B):
        t = np.zeros((4, 2, 16, 8, 2, 64), np.float32)
        Bq = B.reshape(8, 4, 2, 64, 16)
        for gi in range(2):
            t[:, gi, :, :, gi, :] = Bq[:, :, gi].transpose(1, 3, 0, 2)
        return t.reshape(128, 1024)
    for nm, arr in (("AreB", LB_bcast(p["s5_A_re"][0])), ("AimB", LB_bcast(p["s5_A_im"][0])),
                    ("ldtB", LB_bcast(np.repeat(p["s5_log_dt"][0][:, None], 64, axis=1))),
                    ("BreB", LB_B(p["s5_B_re"][0])), ("BimB", LB_B(p["s5_B_im"][0]))):
        o, w = SC_OFF[nm]
        sc[:, o:o + w] = arr
    cr = np.zeros((128, NCR), np.float32)
    o, w = CR_OFF["ones"]; cr[:, o:o + w] = 1.0
    o, w = CR_OFF["seqm"]
    cr[:, o:o + w] = (np.arange(128)[:, None] // LS == np.arange(NSEQ)[None, :])
    o, w = CR_OFF["identb"]; cr[:, o:o + w] = np.eye(128, dtype=np.float32)
    for nm in ("Cr", "Ci"):
        o, w = CR_OFF[nm]; cr[:, o:o + w] = ctabs[nm]
    return cf, cr, sc


def _core_consts(cf_common, st, c, flag=1.0):
    cf = cf_common.copy()
    o_, w_ = CF_OFF["flag"]
    cf[:, o_:o_ + w_] = flag
    sl = slice(NSEQ * c, NSEQ * c + NSEQ)

    def put(name, arr, p0=0):
        o, w = CF_OFF[name]
        arr = np.asarray(arr, np.float32)
        cf[p0:p0 + arr.shape[0], o:o + arr.shape[1]] = arr
    put("m0S", st["state_mlstm_m"][0][sl].T)
    for nm, key in (("h0r", "state_s5_re"), ("h0i", "state_s5_im")):
        a = st[key][0][sl]
        t = a.reshape(NSEQ, 32, 2, 64).transpose(2, 3, 1, 0)
        put(nm, t.reshape(128, 512))
    n0 = st["state_mlstm_n"][0][sl]
    put("n0S", n0.transpose(2, 1, 0).reshape(128, 64))
    cv = st["state_mlstm_conv"][0][sl]
    t = cv.reshape(NSEQ, 3, 8, 128).transpose(3, 2, 0, 1)
    put("histS", t.reshape(128, 384))
    return cf


_PROG = {}
SPLIT = PLEN // 2
TILEW = 384


def kernel(**inputs):
    inp = {k_: np.asarray(v) for k_, v in inputs.items()}
    ncores = 8
    if "prog" not in _PROG:
        _PROG["prog"] = build_program(PLEN, debug=False, split=SPLIT, tw=TILEW)
    nc, plan = _PROG["prog"]
    cf_common, cr, sc = _consts_common(inp)
    wmap = {"f1g": inp["ffn1_w_gate"][0], "f1u": inp["ffn1_w_up"][0], "f1d": inp["ffn1_w_down"][0],
            "win": inp["w_in"][0], "wglu": inp["s5_w_glu"][0], "wbs": inp["w_branch_s5"][0],
            "wbm": inp["w_branch_mlstm"][0], "wout": inp["w_out"][0],
            "f2g": inp["ffn2_w_gate"][0], "f2u": inp["ffn2_w_up"][0], "f2d": inp["ffn2_w_down"][0]}
    wmap = {k_: np.ascontiguousarray(v, dtype=np.float32) for k_, v in wmap.items()}
    in_maps = []
    for c in range(ncores):
        b, half = c // 2, c % 2
        sl = slice(NSEQ * c, NSEQ * c + NSEQ)
        hpT = np.concatenate([inp["meta_tokens"].T, inp["x_prompt"][b].T], axis=1).astype(np.float32)
        xT = np.zeros((D, PLEN + SC), np.float32)
        if half == 0:
            xT[:, SPLIT:PLEN] = hpT[:, 0:SPLIT]
        else:
            xT[:, 0:PLEN] = hpT
        xT[:, PLEN:] = inp["x_sample"][sl].reshape(SC, D).T
        m = dict(wmap)
        m["xT"] = xT
        m["cf"] = _core_consts(cf_common, inp, c, flag=float(half))
        m["cr"] = cr
        m["sc"] = sc
        m["sC"] = np.ascontiguousarray(inp["state_mlstm_C"][0][sl], dtype=np.float32)
        in_maps.append(m)
    res = run_bass_kernel_spmd(nc, in_maps, core_ids=list(range(ncores)))
    return assemble_split(res.results)


def assemble_split(R):
    ncores = len(R)
    nm = PLEN - SPLIT
    y_prompt = np.stack([np.concatenate([R[2 * b]["yT"][:, NMETA:nm].T, R[2 * b + 1]["yT"][:, 0:nm].T], axis=0)
                         for b in range(4)])
    y_sample = np.concatenate([R[c]["yT"][:, nm:].T.reshape(NSEQ, LS, D) for c in range(ncores)])
    P = [R[2 * b + 1] for b in range(4)]
    rest = _assemble_states(P, R)
    outs = (y_prompt, y_sample) + rest
    return tuple(np.ascontiguousarray(o, dtype=np.float32) for o in outs)


def _assemble_states(P, R):
    ncores = len(R)

    def s5p(r, o):
        return r["o_ps5"][:, o:o + 32].reshape(2, 64, 32).transpose(2, 0, 1).reshape(64, 64)
    p_re = np.stack([s5p(r, 0) for r in P])[None]
    p_im = np.stack([s5p(r, 32) for r in P])[None]
    p_C = np.stack([r["o_pC"].reshape(128, NH, 2, 128).transpose(1, 2, 0, 3).reshape(NH, DV, DQK) for r in P])[None]
    p_n = np.stack([r["o_pn"].T for r in P])[None]
    p_m = np.stack([r["o_pm"][:, 0] for r in P])[None]
    p_cv = np.stack([r["o_pconv"].reshape(128, 8, 3).transpose(2, 1, 0).reshape(3, QKW) for r in P])[None]

    def s5s(c, o):
        t = R[c]["o_ss5"][:, o:o + 512].reshape(2, 64, 32, NSEQ)
        return t.transpose(3, 2, 0, 1).reshape(NSEQ, 64, 64)
    s_re = np.concatenate([s5s(c, 0) for c in range(ncores)])[None]
    s_im = np.concatenate([s5s(c, 512) for c in range(ncores)])[None]
    s_C = np.concatenate([R[c]["o_sC"] for c in range(ncores)])[None]
    s_n = np.concatenate([R[c]["o_sn"].reshape(128, NH, NSEQ).transpose(2, 1, 0) for c in range(ncores)])[None]
    s_m = np.concatenate([R[c]["o_sm"].T for c in range(ncores)])[None]
    s_cv = np.concatenate([R[c]["o_sconv"].reshape(128, 8, NSEQ, 3).transpose(2, 3, 1, 0).reshape(NSEQ, 3, QKW)
                           for c in range(ncores)])[None]
    return (p_re, p_im, p_C, p_n, p_m, p_cv, s_re, s_im, s_C, s_n, s_m, s_cv)


def assemble(R, nprompt=4, pl=PLEN):
    ncores = len(R)
    y_prompt = np.stack([R[c]["yT"][:, NMETA:pl].T for c in range(nprompt)])
    y_sample = np.concatenate([R[c]["yT"][:, pl:].T.reshape(NSEQ, LS, D) for c in range(ncores)])
    rest = _assemble_states([R[c] for c in range(nprompt)], R)
    outs = (y_prompt, y_sample) + rest
    return tuple(np.ascontiguousarray(o, dtype=np.float32) for o in outs)
```
